# Optimizing a Trainium2 kernel written in Bass

```python
import math
import jax, jax.numpy as jnp
from jax import lax
import numpy as np

D_MODEL = 2048
BATCH = 4
SEQ = 2048
DEPTH = 4
DEC_BATCH = 32
DEC_SEQ = 1
PAST_LEN = 16384
PAGE_SIZE = 128

N_META = 16
S5_GROUP = 16
S5_WIDTH = D_MODEL // 2
S5_GROUPS = S5_WIDTH // S5_GROUP
S5_STATE = 64
GLA_HEADS = 4
GLA_DK = D_MODEL // 16
GLA_DV = D_MODEL // 8
GLA_RANK = 16
GLA_TAU = 16.0
GLA_CHUNK = 16
SWA_QH = 16
SWA_KVH = 4
SWA_GRP = SWA_QH // SWA_KVH
SWA_HD = 64
WINDOW = 128
N_BUCKETS = 32
MAX_DISTANCE = WINDOW
D_FF = 4 * D_MODEL
N_BRANCH = 3
EPS = 1e-6
IN_SIZES = (S5_WIDTH,
            GLA_HEADS * GLA_DK,
            GLA_HEADS * GLA_DK,
            GLA_HEADS * GLA_DV,
            GLA_RANK,
            GLA_HEADS * GLA_DV,
            SWA_QH * SWA_HD,
            SWA_KVH * SWA_HD,
            SWA_KVH * SWA_HD,
            N_BRANCH * D_MODEL)
IN_WIDTH = sum(IN_SIZES)

kernel_name = 'hybrid_s5_gla_swa_decoder_step'

F32 = jnp.float32


def _rmsnorm(x, g):
    xf = x.astype(F32)
    y = xf * lax.rsqrt(jnp.mean(xf * xf, axis=-1, keepdims=True) + EPS)
    return (y * g.astype(F32)).astype(x.dtype)


def _split_cols(z):
    offs = [0]
    for s in IN_SIZES:
        offs.append(offs[-1] + s)
    return [z[..., a:b] for a, b in zip(offs[:-1], offs[1:])]


def _cplx_combine(e1, e2):
    a1r, a1i, b1r, b1i = e1
    a2r, a2i, b2r, b2i = e2
    return (a1r * a2r - a1i * a2i,
            a1r * a2i + a1i * a2r,
            a2r * b1r - a2i * b1i + b2r,
            a2r * b1i + a2i * b1r + b2i)


def _s5(u, x0_re, x0_im, lam_re, lam_im, log_step, b_re, b_im, c_re, c_im, d_skip, w_glu, b_glu):
    n, t, _ = u.shape
    dt = jnp.exp(log_step.astype(F32))[:, None]
    lr, li = lam_re.astype(F32), lam_im.astype(F32)
    mag = jnp.exp(lr * dt)
    ar, ai = mag * jnp.cos(li * dt), mag * jnp.sin(li * dt)
    den = lr * lr + li * li
    fr = ((ar - 1.0) * lr + ai * li) / den
    fi = (ai * lr - (ar - 1.0) * li) / den
    br, bi = b_re.astype(F32), b_im.astype(F32)
    bbr = fr[..., None] * br - fi[..., None] * bi
    bbi = fr[..., None] * bi + fi[..., None] * br
    uf = u.astype(F32)
    ug = uf.reshape(n, t, S5_GROUPS, S5_GROUP)
    e_re = jnp.einsum('gph,ntgh->tngp', bbr, ug)
    e_im = jnp.einsum('gph,ntgh->tngp', bbi, ug)
    x0r, x0i = x0_re.astype(F32), x0_im.astype(F32)
    e_re = e_re.at[0].add(ar * x0r - ai * x0i)
    e_im = e_im.at[0].add(ar * x0i + ai * x0r)
    a_re = jnp.broadcast_to(ar, (t, 1, S5_GROUPS, S5_STATE))
    a_im = jnp.broadcast_to(ai, (t, 1, S5_GROUPS, S5_STATE))
    _, _, xr, xi = lax.associative_scan(_cplx_combine, (a_re, a_im, e_re, e_im), axis=0)
    y = (jnp.einsum('ghp,tngp->ntgh', c_re.astype(F32), xr)
         - jnp.einsum('ghp,tngp->ntgh', c_im.astype(F32), xi))
    y = y.reshape(n, t, S5_WIDTH) + d_skip.astype(F32) * uf
    z = jax.nn.gelu(y)
    out = z * jax.nn.sigmoid(z @ w_glu.astype(F32) + b_glu.astype(F32))
    return out, xr[-1], xi[-1]


def _gla_chunked(q, k, v, lg, s0):
    n, t, h, _ = q.shape
    nc = t // GLA_CHUNK

    def rs(a):
        return a.reshape(n, nc, GLA_CHUNK, h, a.shape[-1]).transpose(1, 0, 3, 2, 4)

    qc, kc, vc, gc = rs(q), rs(k), rs(v), rs(lg)
    b = jnp.cumsum(gc, axis=3)
    causal = jnp.tril(jnp.ones((GLA_CHUNK, GLA_CHUNK), bool))[..., None]
    diff = b[..., :, None, :] - b[..., None, :, :]
    decay = jnp.exp(jnp.where(causal, diff, -jnp.inf))
    att = jnp.einsum('znhtd,znhsd,znhtsd->znhts', qc, kc, decay)
    intra = jnp.einsum('znhts,znhsv->znhtv', att, vc)
    q_inter = qc * jnp.exp(b)
    k_upd = kc * jnp.exp(b[..., -1:, :] - b)
    chunk_decay = jnp.exp(b[..., -1, :])

    def step(s, inp):
        qi, ku, vi, cd = inp
        o = jnp.einsum('nhtd,nhdv->nhtv', qi, s)
        s = cd[..., None] * s + jnp.einsum('nhsd,nhsv->nhdv', ku, vi)
        return s, o

    s_fin, inter = lax.scan(step, s0, (q_inter, k_upd, vc, chunk_decay))
    o = (intra + inter).transpose(1, 0, 3, 2, 4).reshape(n, t, h, v.shape[-1])
    return o, s_fin


def _gla_recurrent(q, k, v, lg, s0):
    def step(s, inp):
        qt, kt, vt, gt = inp
        s = jnp.exp(gt)[..., None] * s + kt[..., :, None] * vt[..., None, :]
        return s, jnp.einsum('nhd,nhdv->nhv', qt, s)

    tf = lambda a: a.transpose(1, 0, 2, 3)
    s_fin, o = lax.scan(step, s0, (tf(q), tf(k), tf(v), tf(lg)))
    return tf(o), s_fin


def _gla(q, k, v, a_in, r, w_gate2, b_gate2, g_out, s0, prompt):
    n, t, _ = q.shape
    q = q.astype(F32).reshape(n, t, GLA_HEADS, GLA_DK) * (GLA_DK ** -0.5)
    k = k.astype(F32).reshape(n, t, GLA_HEADS, GLA_DK)
    v = v.astype(F32).reshape(n, t, GLA_HEADS, GLA_DV)
    lg = jax.nn.log_sigmoid(a_in.astype(F32) @ w_gate2.astype(F32) + b_gate2.astype(F32)) / GLA_TAU
    lg = lg.reshape(n, t, GLA_HEADS, GLA_DK)
    if prompt:
        o, s = _gla_chunked(q, k, v, lg, s0.astype(F32))
    else:
        o, s = _gla_recurrent(q, k, v, lg, s0.astype(F32))
    o = _rmsnorm(o, g_out).reshape(n, t, GLA_HEADS * GLA_DV)
    return o * jax.nn.silu(r.astype(F32)), s


def _t5_bucket(dist):
    max_exact = N_BUCKETS // 2
    d = jnp.maximum(dist, 0)
    large = max_exact + (jnp.log(jnp.maximum(d, 1).astype(F32) / max_exact)
                         / math.log(MAX_DISTANCE / max_exact) * (N_BUCKETS - max_exact)).astype(jnp.int32)
    large = jnp.minimum(large, N_BUCKETS - 1)
    return jnp.where(d < max_exact, d, large)


def _rel_bias(rel_bias, dist):
    bias = rel_bias.astype(F32)[_t5_bucket(dist)]
    return bias.transpose(2, 0, 1).reshape(SWA_KVH, SWA_GRP, dist.shape[0], dist.shape[1])


def _sink_softmax(s, sink):
    sk = sink.astype(F32).reshape(SWA_KVH, SWA_GRP)[:, :, None, None]
    m = jnp.maximum(jnp.max(s, axis=-1, keepdims=True), sk)
    p = jnp.exp(s - m)
    return p / (jnp.sum(p, axis=-1, keepdims=True) + jnp.exp(sk - m))


def _swa_prompt(q, k, v, rel_bias, sinks):
    n, t, _ = q.shape
    pad = (-t) % WINDOW
    tp = t + pad
    nb = tp // WINDOW
    qf, kf, vf = q.astype(F32), k.astype(F32), v.astype(F32)
    qp = jnp.pad(qf, ((0, 0), (pad, 0), (0, 0))).reshape(n, nb, WINDOW, SWA_KVH, SWA_GRP, SWA_HD)
    kp = jnp.pad(kf, ((0, 0), (pad, 0), (0, 0))).reshape(n, nb, WINDOW, SWA_KVH, SWA_HD)
    vp = jnp.pad(vf, ((0, 0), (pad, 0), (0, 0))).reshape(n, nb, WINDOW, SWA_KVH, SWA_HD)
    shift = lambda a: jnp.pad(a, ((0, 0), (1, 0), (0, 0), (0, 0), (0, 0)))[:, :-1]
    kb = jnp.concatenate([shift(kp), kp], axis=2)
    vb = jnp.concatenate([shift(vp), vp], axis=2)
    s = jnp.einsum('nbqkgd,nbskd->nbkgqs', qp, kb) * (SWA_HD ** -0.5)
    dist = (jnp.arange(WINDOW)[:, None] + WINDOW) - jnp.arange(2 * WINDOW)[None, :]
    band = (dist >= 0) & (dist < WINDOW)
    kabs = jnp.arange(nb)[:, None] * WINDOW - WINDOW + jnp.arange(2 * WINDOW)[None, :]
    mask = band[None] & (kabs >= pad)[:, None, :]
    s = jnp.where(mask[None, :, None, None], s + _rel_bias(rel_bias, dist), -jnp.inf)
    p = _sink_softmax(s, sinks)
    o = jnp.einsum('nbkgqs,nbskd->nbqkgd', p, vb).reshape(n, tp, SWA_QH * SWA_HD)[:, pad:]
    wb = min(WINDOW, t)
    k_last = k.reshape(n, t, SWA_KVH, SWA_HD)[:, t - wb:]
    v_last = v.reshape(n, t, SWA_KVH, SWA_HD)[:, t - wb:]
    return o, k_last, v_last


def _swa_decode(q, k, v, k_buf, v_buf, rel_bias, sinks):
    n, s_new, _ = q.shape
    wb = k_buf.shape[1]
    kk = jnp.concatenate([k_buf, k.reshape(n, s_new, SWA_KVH, SWA_HD).astype(k_buf.dtype)], axis=1)
    vv = jnp.concatenate([v_buf, v.reshape(n, s_new, SWA_KVH, SWA_HD).astype(v_buf.dtype)], axis=1)
    qq = q.astype(F32).reshape(n, s_new, SWA_KVH, SWA_GRP, SWA_HD)
    s = jnp.einsum('nqkgd,nskd->nkgqs', qq, kk.astype(F32)) * (SWA_HD ** -0.5)
    dist = (wb + jnp.arange(s_new)[:, None]) - jnp.arange(wb + s_new)[None, :]
    band = (dist >= 0) & (dist < WINDOW)
    s = jnp.where(band, s + _rel_bias(rel_bias, dist), -jnp.inf)
    p = _sink_softmax(s, sinks)
    o = jnp.einsum('nkgqs,nskd->nqkgd', p, vv.astype(F32)).reshape(n, s_new, SWA_QH * SWA_HD)
    return o, kk[:, s_new:], vv[:, s_new:]


def _mixer(h, lp, rel_bias, st, prompt):
    n, t, _ = h.shape
    dt = h.dtype
    u_a, q_b, k_b, v_b, a_b, r_b, q_c, k_c, v_c, gates = _split_cols(h @ lp['w_in'])
    if prompt:
        s5r0 = jnp.zeros((n, S5_GROUPS, S5_STATE), F32)
        s5i0 = jnp.zeros((n, S5_GROUPS, S5_STATE), F32)
        gla0 = jnp.zeros((n, GLA_HEADS, GLA_DK, GLA_DV), F32)
    else:
        s5r0, s5i0, gla0, k_buf, v_buf = st
    y_a, s5r, s5i = _s5(u_a, s5r0, s5i0, lp['s5_lam_re'], lp['s5_lam_im'], lp['s5_log_step'],
                        lp['s5_b_re'], lp['s5_b_im'], lp['s5_c_re'], lp['s5_c_im'], lp['s5_d'],
                        lp['s5_w_glu'], lp['s5_b_glu'])
    y_b, gla1 = _gla(q_b, k_b, v_b, a_b, r_b, lp['gla_w_gate2'], lp['gla_b_gate2'], lp['gla_g_out'],
                     gla0, prompt)
    if prompt:
        y_c, kn, vn = _swa_prompt(q_c, k_c, v_c, rel_bias, lp['swa_sinks'])
    else:
        y_c, kn, vn = _swa_decode(q_c, k_c, v_c, k_buf, v_buf, rel_bias, lp['swa_sinks'])
    g = jax.nn.sigmoid(gates.astype(F32)).reshape(n, t, N_BRANCH, D_MODEL)
    merged = (g[..., 0, :] * (y_a.astype(dt) @ lp['w_up_s5'])
              + g[..., 1, :] * (y_b.astype(dt) @ lp['w_up_gla'])
              + g[..., 2, :] * (y_c.astype(dt) @ lp['w_up_swa']))
    return merged.astype(dt) @ lp['w_out'], (s5r, s5i, gla1, kn, vn)


def _trunk(x, w, rel_bias, states, prompt):
    new = ([], [], [], [], [])
    for l in range(DEPTH):
        lp = {name: arr[l] for name, arr in w.items()}
        st = None if prompt else tuple(s[l] for s in states)
        h = _rmsnorm(x, lp['norm_pre_mix'])
        mix, ns = _mixer(h, lp, rel_bias, st, prompt)
        x = x + _rmsnorm(mix, lp['norm_post_mix'])
        h = _rmsnorm(x, lp['norm_pre_ffn'])
        ff = jnp.square(jax.nn.relu(h @ lp['w_ff1'])) @ lp['w_ff2']
        x = x + _rmsnorm(ff, lp['norm_post_ffn'])
        for lst, a in zip(new, ns):
            lst.append(a)
    return x, [jnp.stack(a) for a in new]


def setup_inputs(seed: int = 0) -> dict:
    key = jax.random.key(seed)
    ks = iter(jax.random.split(key, 48))
    nrm = lambda shape, scale: jax.random.normal(next(ks), shape, F32) * scale
    wb = min(WINDOW, PAST_LEN)
    d = {}
    d['x_prompt'] = nrm((BATCH, SEQ, D_MODEL), 1.0)
    d['x_sample'] = nrm((DEC_BATCH, DEC_SEQ, D_MODEL), 1.0)
    d['state_s5_re'] = nrm((DEPTH, DEC_BATCH, S5_GROUPS, S5_STATE), 0.1)
    d['state_s5_im'] = nrm((DEPTH, DEC_BATCH, S5_GROUPS, S5_STATE), 0.1)
    d['state_gla'] = nrm((DEPTH, DEC_BATCH, GLA_HEADS, GLA_DK, GLA_DV), 1.0)
    d['cache_swa_k'] = nrm((DEPTH, DEC_BATCH, wb, SWA_KVH, SWA_HD), 1.0)
    d['cache_swa_v'] = nrm((DEPTH, DEC_BATCH, wb, SWA_KVH, SWA_HD), 1.0)
    d['meta_tokens'] = nrm((N_META, D_MODEL), 1.0)
    d['rel_bias'] = nrm((N_BUCKETS, SWA_QH), 0.5)
    for name in ('norm_pre_mix', 'norm_post_mix', 'norm_pre_ffn', 'norm_post_ffn'):
        d[name] = 1.0 + nrm((DEPTH, D_MODEL), 0.05)
    d['w_in'] = nrm((DEPTH, D_MODEL, IN_WIDTH), D_MODEL ** -0.5)
    d['s5_lam_re'] = -0.5 + nrm((DEPTH, S5_GROUPS, S5_STATE), 0.01)
    d['s5_lam_im'] = (jnp.pi * jnp.arange(S5_STATE, dtype=F32))[None, None, :] + nrm((DEPTH, S5_GROUPS, S5_STATE), 0.01)
    d['s5_log_step'] = jax.random.uniform(next(ks), (DEPTH, S5_GROUPS), F32, math.log(1e-3), math.log(1e-1))
    d['s5_b_re'] = nrm((DEPTH, S5_GROUPS, S5_STATE, S5_GROUP), (2.0 * S5_GROUP) ** -0.5)
    d['s5_b_im'] = nrm((DEPTH, S5_GROUPS, S5_STATE, S5_GROUP), (2.0 * S5_GROUP) ** -0.5)
    d['s5_c_re'] = nrm((DEPTH, S5_GROUPS, S5_GROUP, S5_STATE), (2.0 * S5_STATE) ** -0.5)
    d['s5_c_im'] = nrm((DEPTH, S5_GROUPS, S5_GROUP, S5_STATE), (2.0 * S5_STATE) ** -0.5)
    d['s5_d'] = nrm((DEPTH, S5_WIDTH), 1.0)
    d['s5_w_glu'] = nrm((DEPTH, S5_WIDTH, S5_WIDTH), S5_WIDTH ** -0.5)
    d['s5_b_glu'] = nrm((DEPTH, S5_WIDTH), 0.01)
    d['gla_w_gate2'] = nrm((DEPTH, GLA_RANK, GLA_HEADS * GLA_DK), GLA_RANK ** -0.5)
    d['gla_b_gate2'] = 1.0 + nrm((DEPTH, GLA_HEADS * GLA_DK), 0.5)
    d['gla_g_out'] = 1.0 + nrm((DEPTH, GLA_DV), 0.05)
    d['swa_sinks'] = nrm((DEPTH, SWA_QH), 0.5)
    d['w_up_s5'] = nrm((DEPTH, S5_WIDTH, D_MODEL), S5_WIDTH ** -0.5)
    d['w_up_gla'] = nrm((DEPTH, GLA_HEADS * GLA_DV, D_MODEL), (GLA_HEADS * GLA_DV) ** -0.5)
    d['w_up_swa'] = nrm((DEPTH, SWA_QH * SWA_HD, D_MODEL), (SWA_QH * SWA_HD) ** -0.5)
    d['w_out'] = nrm((DEPTH, D_MODEL, D_MODEL), D_MODEL ** -0.5)
    d['w_ff1'] = nrm((DEPTH, D_MODEL, D_FF), D_MODEL ** -0.5)
    d['w_ff2'] = nrm((DEPTH, D_FF, D_MODEL), D_FF ** -0.5)
    return d


def reference(x_prompt, x_sample, state_s5_re, state_s5_im, state_gla, cache_swa_k, cache_swa_v,
              meta_tokens, rel_bias, norm_pre_mix, norm_post_mix, norm_pre_ffn, norm_post_ffn,
              w_in, s5_lam_re, s5_lam_im, s5_log_step, s5_b_re, s5_b_im, s5_c_re, s5_c_im, s5_d,
              s5_w_glu, s5_b_glu, gla_w_gate2, gla_b_gate2, gla_g_out, swa_sinks,
              w_up_s5, w_up_gla, w_up_swa, w_out, w_ff1, w_ff2):
    w = {'norm_pre_mix': norm_pre_mix, 'norm_post_mix': norm_post_mix,
         'norm_pre_ffn': norm_pre_ffn, 'norm_post_ffn': norm_post_ffn,
         'w_in': w_in, 's5_lam_re': s5_lam_re, 's5_lam_im': s5_lam_im, 's5_log_step': s5_log_step,
         's5_b_re': s5_b_re, 's5_b_im': s5_b_im, 's5_c_re': s5_c_re, 's5_c_im': s5_c_im,
         's5_d': s5_d, 's5_w_glu': s5_w_glu, 's5_b_glu': s5_b_glu,
         'gla_w_gate2': gla_w_gate2, 'gla_b_gate2': gla_b_gate2, 'gla_g_out': gla_g_out,
         'swa_sinks': swa_sinks, 'w_up_s5': w_up_s5, 'w_up_gla': w_up_gla, 'w_up_swa': w_up_swa,
         'w_out': w_out, 'w_ff1': w_ff1, 'w_ff2': w_ff2}
    b = x_prompt.shape[0]
    meta = jnp.broadcast_to(meta_tokens.astype(x_prompt.dtype)[None], (b, N_META, D_MODEL))
    xp = jnp.concatenate([meta, x_prompt], axis=1)
    yp_full, pst = _trunk(xp, w, rel_bias, None, True)
    y_prompt = yp_full[:, N_META:]
    y_sample, sst = _trunk(x_sample, w, rel_bias,
                           (state_s5_re, state_s5_im, state_gla, cache_swa_k, cache_swa_v), False)
    p_s5_re, p_s5_im, p_gla, p_swa_k, p_swa_v = pst
    s_s5_re, s_s5_im, s_gla, s_swa_k, s_swa_v = sst
    return (y_prompt, y_sample, p_s5_re, p_s5_im, p_gla, p_swa_k, p_swa_v,
            s_s5_re, s_s5_im, s_gla, s_swa_k, s_swa_v)
```

```python
import math
import numpy as np
from contextlib import ExitStack
import concourse.bass as bass
import concourse.mybir as mybir
from concourse.bass_utils import run_bass_kernel_spmd

F32 = mybir.dt.float32
BF16 = mybir.dt.bfloat16
AF = mybir.ActivationFunctionType
ALU = mybir.AluOpType

D = 2048; NT = 2176; QW = 544; NQ = 4; SEQ = 2048; NMETA = 16; COL0 = 112
NS = 4
NBLK_IN = 93
EPS = 1e-6
PI = math.pi
ENG = ['pe', 'act', 'dve', 'pool', 'sp']
TILES_ALL = [(0, 512), (512, 512), (1024, 512), (1536, 512), (2048, 128)]
TILES_Q = [(0, 512), (512, 32)]
NEG = -30000.0


class Prog:
    def __init__(self, nc):
        self.nc = nc
        self.q = {e: [] for e in ENG}
        self.cnt = {e: 0 for e in ENG}
        self.seen = {e: {} for e in ENG}
        self.keys = {}
        self.dtot = {}
        self.es = ExitStack()
        self.nops = 0
        self.banks = None
        self.bi = 0

    def sb(self, name, shape, dt):
        return self.es.enter_context(self.nc.sbuf_tensor(name, list(shape), dt))

    def mkbanks(self):
        self.banks = [self.es.enter_context(self.nc.psum_tensor(f"psb{i}", [128, 512], F32)) for i in range(8)]

    def bank(self):
        i = self.bi % 8
        self.bi += 1
        return self.banks[i], f"ps{i}"

    def _deps(self, reads, writes):
        toks = []
        for k in reads:
            st = self.keys.get(k)
            if st is not None and st[0] is not None:
                toks.append(st[0])
        for k in writes:
            st = self.keys.get(k)
            if st is not None:
                if st[0] is not None:
                    toks.append(st[0])
                toks.extend(st[1].items())
        return toks

    def _commit(self, tok, reads, writes):
        for k in reads:
            st = self.keys.get(k)
            if st is None:
                st = self.keys[k] = [None, {}]
            if st[1].get(tok[0], 0) < tok[1]:
                st[1][tok[0]] = tok[1]
        for k in writes:
            self.keys[k] = [tok, {}]

    def _mkwaits(self, eng, toks):
        best = {}
        for s, v in toks:
            if s == eng and eng == 'pe':
                continue
            if best.get(s, 0) < v:
                best[s] = v
        seen = self.seen[eng]
        out = []
        for s, v in best.items():
            if seen.get(s, 0) >= v:
                continue
            seen[s] = v
            out.append((s, v))
        return out

    def op(self, eng, fn, reads=(), writes=(), inc=True):
        waits = self._mkwaits(eng, self._deps(reads, writes))
        if inc:
            self.cnt[eng] += 1
            tok = (eng, self.cnt[eng])
        else:
            tok = (eng, self.cnt[eng] + 1)
        self.q[eng].append((waits, fn, eng if inc else None, 1))
        self._commit(tok, reads, writes)
        self.nops += 1

    def dma(self, eng, fn, chan, reads=(), writes=()):
        waits = self._mkwaits(eng, self._deps(reads, writes))
        self.dtot[chan] = self.dtot.get(chan, 0) + 16
        tok = (chan, self.dtot[chan])
        self.q[eng].append((waits, fn, chan, 16))
        self._commit(tok, reads, writes)
        self.nops += 1

    def barrier(self):
        toks = [(e, self.cnt[e]) for e in ENG if self.cnt[e] > 0] + list(self.dtot.items())
        for e in ENG:
            waits = self._mkwaits(e, toks)
            if waits:
                self.q[e].append((waits, None, None, 0))

    def build(self):
        nc = self.nc
        names = sorted(set(ENG) | set(self.dtot.keys()))
        sems = {n: self.es.enter_context(nc.semaphore("s_" + n)) for n in names}
        fin = list(self.dtot.items()) + [(e, self.cnt[e]) for e in ENG if e != 'sp' and self.cnt[e] > 0]
        q = self.q
        block = self.es.enter_context(nc.Block())
        emap = {'pe': block.tensor, 'act': block.scalar, 'dve': block.vector, 'pool': block.gpsimd, 'sp': block.sync}

        def mk(engname):
            def body(e):
                for waits, fn, s, n in q[engname]:
                    for (ws, wv) in waits:
                        e.wait_ge(sems[ws], wv)
                    if fn is None:
                        continue
                    ins = fn(e)
                    if s is not None:
                        ins.then_inc(sems[s], n)
                if engname == 'sp':
                    for (ws, wv) in fin:
                        e.wait_ge(sems[ws], wv)
            return body
        for engname in ENG:
            emap[engname](mk(engname))
        self.es.close()


class Arena:
    def __init__(self, P, words):
        self.t = P.sb("arena", [128, words], F32)
        self.words = words
        self.off = 0
        self.gen = 0

    def reset(self):
        self.off = 0
        self.gen += 1

    def alloc(self, shape, dt):
        n = 1
        for s in shape[1:]:
            n *= s
        w = n if dt == F32 else (n + 1) // 2
        w = (w + 15) // 16 * 16
        assert self.off + w <= self.words, (self.off, w, self.words)
        v = self.t[:, self.off:self.off + w]
        self.off += w
        if dt != F32:
            v = v.bitcast(dt)
        v = v[:, :n]
        if len(shape) == 3:
            v = v.rearrange("p (a b) -> p a b", a=shape[1])
        elif len(shape) == 4:
            v = v.rearrange("p (a b c) -> p a b c", a=shape[1], b=shape[2])
        elif len(shape) == 5:
            v = v.rearrange("p (a b c d) -> p a b c d", a=shape[1], b=shape[2], c=shape[3])
        return v


def build_program(L, debug=False):
    nc = bass.Bass("TRN2", target_bir_lowering=False)
    P = Prog(nc)
    A = Arena(P, 46000)
    P.mkbanks()

    def din(name, shape):
        return nc.dram_tensor(name, list(shape), F32, kind="ExternalInput").ap()

    def dout(name, shape):
        return nc.dram_tensor(name, list(shape), F32, kind="ExternalOutput").ap()

    def dscr(name, shape, dt):
        return nc.dram_tensor(name, list(shape), dt, kind=("ExternalOutput" if debug else "Internal")).ap()

    xT = din("xT", [128, 16, NT])
    w_in = din("w_in", [L, NBLK_IN, 128, 16, 128])
    w_glu = din("w_glu", [L, 8, 128, 8, 128])
    w_up = din("w_up", [L, 3, 16, 128, 8, 128])
    w_out = din("w_out", [L, 16, 128, 16, 128])
    w_ff1 = din("w_ff1", [L, 64, 128, 16, 128])
    w_ff2 = din("w_ff2", [L, 16, 4, 128, 16, 128])
    gnorm = din("gnorm", [L, 128, 4, 16])
    s5rep = din("s5rep", [L, 128, 5, 512])
    s5cm = din("s5cm", [L, 128, 3, 32])
    s5c = din("s5c", [L, 128, 2, 32, 16])
    s5d = din("s5d", [L, 128, 2, 8])
    s5x0 = din("s5x0", [L, 128, 2, 32, NS])
    gla_w2 = din("gla_w2", [L, 16, 512])
    gla_p = din("gla_p", [L, 128, 6])
    gla_s0 = din("gla_s0", [L, NS, 4, 128, 256])
    swa_sink = din("swa_sink", [L, 64, 16])
    swa_kT = din("swa_kT", [L, 128, NS, 2, 128])
    swa_kn = din("swa_kn", [L, NS, 128, 256])
    swa_vn = din("swa_vn", [L, NS, 128, 256])
    relb = din("relb", [32, 16])
    c_oh = din("c_oh", [33, 2 * 256 + 128 + 4])
    c_misc = din("c_misc", [128, 128 * 3 + 8 + 4])
    c_tpos = din("c_tpos", [128, NT])
    c_rmask = din("c_rmask", [128, NT])
    c_pad = din("c_pad", [128, 2])
    yT = dout("yT", [128, 16, NT])
    o_ps5 = dout("o_ps5", [L, 128, 2, 32])
    o_pgla = dout("o_pgla", [L, 4, 128, 256])
    o_pk = dout("o_pk", [L, 128, 256])
    o_pv = dout("o_pv", [L, 128, 256])
    o_ss5 = dout("o_ss5", [L, 128, 2, 32, NS])
    o_sgla = dout("o_sgla", [L, NS, 4, 128, 256])
    o_sk = dout("o_sk", [L, NS, 128, 256])
    o_sv = dout("o_sv", [L, NS, 128, 256])
    x_scr = dscr("x_scr", [128, 16, NT], F32)
    h_scr = dscr("h_scr", [128, 16, NT], BF16)
    z_scr = dscr("z_scr", [NBLK_IN, 128, NT], BF16)
    y_scr = dscr("y_scr", [24, 128, NT], BF16)
    hid_scr = dscr("hid_scr", [64, 128, NT], BF16)
    e_scr = dscr("e_scr", [2, 16, 256], F32)

    NWB = 4
    wbufs = [P.sb(f"wb{i}", [128, 16, 128], BF16) for i in range(NWB)]
    cm = P.sb("cmisc", [128, 128 * 3 + 12], F32)
    cb = P.sb("cbf", [128, 128 * 3 + 128], BF16)
    gn = P.sb("gn", [128, L, 4, 16], F32)
    ident_f = cm[:, 0:128]
    maskBD = cm[:, 384:392]
    I4 = cm[:, 392:396]
    ident_b = cb[:, 0:128]
    J_b = cb[:, 128:256]
    tri_b = cb[:, 256:384]
    ones_b = cb[:, 384:512]

    P.dma('sp', lambda e: e.dma_start(out=cm[:], in_=c_misc), 'cm', writes=['cm'])
    P.op('dve', lambda e: e.tensor_copy(out=cb[:, 0:384], in_=cm[:, 0:384]), reads=['cm'], writes=['cb'])
    P.op('dve', lambda e: e.memset(cb[:, 384:512], 1.0), writes=['cb1'])
    P.dma('sp', lambda e: e.dma_start(out=gn[:], in_=gnorm.rearrange("l p a c -> p l a c")), 'gn', writes=['gn'])

    wcnt = [0]

    def wload(src, kc):
        i = wcnt[0] % NWB
        wcnt[0] += 1
        P.dma('pool', lambda e: e.dma_start(out=wbufs[i][:, :kc, :], in_=src), f'wb{i}', writes=[f'wb{i}'])
        return wbufs[i], f'wb{i}'

    def gemm(blocks, rhs_fn, rhs_keys, tiles, epi):
        flat = [(ap, kc) for blk in blocks for (ap, kc) in blk]
        LA = NWB - 1
        loaded = {}
        for i in range(min(LA, len(flat))):
            loaded[i] = wload(*flat[i])
        idx = 0
        for j, blk in enumerate(blocks):
            bks = [P.bank() for _ in tiles]
            nk = sum(kc for _, kc in blk)
            kd = 0
            for (ap, kc) in blk:
                if idx + LA < len(flat):
                    loaded[idx + LA] = wload(*flat[idx + LA])
                wb, wk = loaded.pop(idx)
                idx += 1
                for ti, (t0, tn) in enumerate(tiles):
                    bk, bkey = bks[ti]
                    for k in range(kc):
                        kg = kd + k
                        P.op('pe', lambda e, bk=bk, wb=wb, k=k, kg=kg, t0=t0, tn=tn, nk=nk:
                             e.matmul(bk[:, :tn], lhsT=wb[:, k, :], rhs=rhs_fn(kg, t0, tn), start=(kg == 0), stop=(kg == nk - 1)),
                             reads=[wk] + rhs_keys, writes=[bkey], inc=(k == kc - 1))
                kd += kc
            for ti, (t0, tn) in enumerate(tiles):
                epi(j, ti, t0, tn, bks[ti][0], bks[ti][1])

    def act(out, in_, func, r, w, **kw):
        P.op('act', lambda e: e.activation(out=out, in_=in_, func=func, **kw), reads=r, writes=w)

    def tt(eng, out, a, b, op, r, w):
        P.op(eng, lambda e: e.tensor_tensor(out=out, in0=a, in1=b, op=op), reads=r, writes=w)

    def ts(eng, out, a, s1, s2, op0, op1, r, w):
        if op1 is None:
            P.op(eng, lambda e: e.tensor_scalar(out=out, in0=a, scalar1=s1, scalar2=None, op0=op0), reads=r, writes=w)
        else:
            P.op(eng, lambda e: e.tensor_scalar(out=out, in0=a, scalar1=s1, scalar2=s2, op0=op0, op1=op1), reads=r, writes=w)

    def stt(out, a, s, b, op0, op1, r, w):
        P.op('dve', lambda e: e.scalar_tensor_tensor(out=out, in0=a, scalar=s, in1=b, op0=op0, op1=op1), reads=r, writes=w)

    def cp(eng, out, in_, r, w):
        if eng == 'act':
            act(out, in_, AF.Copy, r, w)
        else:
            P.op(eng, lambda e: e.tensor_copy(out=out, in_=in_), reads=r, writes=w)

    def mm(out, lhsT, rhs, start, stop, r, w, inc=True):
        P.op('pe', lambda e: e.matmul(out, lhsT=lhsT, rhs=rhs, start=start, stop=stop), reads=r, writes=w, inc=inc)

    def tr(out, in_, ident, r, w):
        P.op('pe', lambda e: e.transpose(out, in_, ident), reads=r, writes=w)

    def dm(eng, out, in_, chan, r, w):
        P.dma(eng, lambda e: e.dma_start(out=out, in_=in_), chan, reads=r, writes=w)

    def memset(eng, ap, val, w):
        P.op(eng, lambda e: e.memset(ap, val), writes=w)

    def rstd_of(src, skey, n, sq, rstd, tag, ckey=None):
        tl = [(0, min(512, n))] + ([(512, n - 512)] if n > 512 else [])
        for (t0, tn) in tl:
            bk, bkey = P.bank()
            for c in range(16):
                sk = tag + 'sq' + str(c % 2)
                rk = [skey] + ([ckey(c)] if ckey else [])
                act(sq[:, c % 2, t0:t0 + tn], src[:, c, t0:t0 + tn], AF.Square, rk, [sk])
                mm(bk[:, :tn], ones_b, sq[:, c % 2, t0:t0 + tn], c == 0, c == 15, [sk, 'cb1'], [bkey])
            ts('dve', rstd[:, t0:t0 + tn], bk[:, :tn], 1.0 / D, EPS, ALU.mult, ALU.add, [bkey], [tag + 'rs'])
        act(rstd[:, :n], rstd[:, :n], AF.Ln, [tag + 'rs'], [tag + 'rs'])
        act(rstd[:, :n], rstd[:, :n], AF.Exp, [tag + 'rs'], [tag + 'rs'], scale=-0.5)

    def norm_to(out, okey, src, skey, n, gcols, rstd, tag, ckey=None):
        for c in range(16):
            rk = [skey, tag + 'rs', 'gn'] + ([ckey(c)] if ckey else [])
            stt(out[:, c, :n], src[:, c, :n], gcols[:, c:c + 1], rstd[:, :n], ALU.mult, ALU.mult, rk, [okey])

    def stage_prenorm0():
        A.reset()
        xq = A.alloc([128, 16, QW], F32)
        hq = A.alloc([128, 16, QW], BF16)
        sq = A.alloc([128, 2, QW], BF16)
        rs = A.alloc([128, QW], F32)
        for q in range(NQ):
            c0 = q * QW
            dm('sp', xq, xT[:, :, c0:c0 + QW], 'xq', [], ['xq'])
            dm('sp', x_scr[:, :, c0:c0 + QW], xq, 'xst', ['xq'], [('x_scr', q)])
            rstd_of(xq, 'xq', QW, sq, rs, 'n0')
            norm_to(hq, 'hq', xq, 'xq', QW, gn[:, 0, 0, :], rs, 'n0')
            dm('sp', h_scr[:, :, c0:c0 + QW], hq, 'hst', ['hq'], [('h_scr', q)])

    def stage_win(l):
        A.reset()
        hT = A.alloc([128, 16, NT], BF16)
        stg = [A.alloc([128, NT], BF16) for _ in range(2)]
        dm('sp', hT, h_scr, 'hT', [('h_scr', q) for q in range(NQ)], ['hT'])
        blocks = [[(w_in[l, j], 16)] for j in range(NBLK_IN)]

        def epi(j, ti, t0, tn, bk, bkey):
            s = stg[j % 2]
            sk = ('stg', j % 2, ti)
            if j >= 45:
                act(s[:, t0:t0 + tn], bk[:, :tn], AF.Sigmoid, [bkey], [sk])
            elif 25 <= j < 33:
                act(s[:, t0:t0 + tn], bk[:, :tn], AF.Silu, [bkey], [sk])
            else:
                cp('dve' if ti % 2 == 0 else 'act', s[:, t0:t0 + tn], bk[:, :tn], [bkey], [sk])
            if ti == len(TILES_ALL) - 1:
                dm('sp', z_scr[j], s, f'zst{j % 2}', [('stg', j % 2, t) for t in range(len(TILES_ALL))], [('z', j)])
        gemm(blocks, lambda kg, t0, tn: hT[:, kg, t0:t0 + tn], ['hT'], TILES_ALL, epi)

    def stage_s5(l):
        A.reset()
        ub = A.alloc([128, 8, NT], BF16)
        BD = A.alloc([128, 8, 4, 2, 128], BF16)
        CD = A.alloc([128, 8, 4, 2, 128], BF16)
        uni = A.alloc([128, 6656], F32)
        rep = uni[:, 0:2560].rearrange("p (a b) -> p a b", a=5)
        wk = uni[:, 2560:6656].rearrange("p (a b) -> p a b", a=8)
        cmp_ = A.alloc([128, 3, 32], F32)
        cw = A.alloc([128, 12, 32], F32)
        ct = A.alloc([128, 2, 32, 16], F32)
        dd = A.alloc([128, 2, 8], F32)
        x0 = A.alloc([128, 2, 32, NS], F32)
        x1 = A.alloc([128, 2, 32, NS], F32)
        pst = A.alloc([128, 2, 32], F32)
        tpos = A.alloc([128, 512], F32)
        tcs = [A.alloc([128, 512], F32) for _ in range(2)]
        tss = [A.alloc([128, 512], F32) for _ in range(2)]
        targ = A.alloc([128, 512], F32)
        tki = A.alloc([128, 512], F32).bitcast(mybir.dt.int32)
        vini = A.alloc([128, 4], F32)
        magt = A.alloc([128, 512], F32)
        t4 = [A.alloc([128, 512], F32) for _ in range(4)]
        et = [A.alloc([128, 512], F32) for _ in range(2)]
        esb = [[A.alloc([128, 512], F32) for _ in range(2)] for _ in range(2)]
        vv = [[A.alloc([128, 512], F32) for _ in range(2)] for _ in range(2)]
        xx = [[A.alloc([128, 512], BF16) for _ in range(2)] for _ in range(2)]
        pt = [A.alloc([128, 512], F32) for _ in range(2)]
        yacc = A.alloc([128, NT], F32)
        g1 = A.alloc([128, 512], F32)
        g2 = A.alloc([128, 512], F32)
        ystg = [A.alloc([128, NT], BF16) for _ in range(2)]
        dtmp = A.alloc([128, 8], F32)

        dm('sp', ub, z_scr[0:8].rearrange("b p n -> p b n"), 'ub', [('z', j) for j in range(8)], ['ub'])
        dm('sp', rep, s5rep[l], 'rep', [], ['rep'])
        dm('sp', cmp_, s5cm[l], 'cmp', [], ['cmp'])
        dm('sp', ct, s5c[l], 'ct', [], ['ct'])
        dm('sp', dd, s5d[l], 'dd', [], ['dd'])
        dm('sp', x0, s5x0[l], 'x0', [], ['x0'])
        dm('sp', tpos, c_tpos[:, COL0:COL0 + 512], 'tpos', [], ['tpos'])
        memset('pool', BD.rearrange("p a b c d -> p (a b c d)"), 0.0, ['BD'])
        memset('pool', CD.rearrange("p a b c d -> p (a b c d)"), 0.0, ['CD'])
        for par in range(2):
            for ri in range(2):
                memset('dve', xx[par][ri], 0.0, [('xx', par)])

        def sincos(ang, sn, cs, tmp, tmpi, key):
            K = [key]
            ts('dve', tmpi, ang, 1.0 / (2 * PI), None, ALU.mult, None, K, K)
            stt(tmp, tmpi, -2 * PI, ang, ALU.mult, ALU.add, K, K)
            ts('dve', tmp, tmp, PI, -PI, ALU.min, ALU.max, K, K)
            act(sn, tmp, AF.Sin, K, K)
            ts('dve', tmpi, ang, 0.5 * PI, 1.0 / (2 * PI), ALU.add, ALU.mult, K, K)
            stt(tmp, tmpi, -2 * PI, ang, ALU.mult, ALU.add, K, K)
            ts('dve', tmp, tmp, 0.5 * PI, PI, ALU.add, ALU.min, K, K)
            ts('dve', tmp, tmp, -PI, None, ALU.max, None, K, K)
            act(cs, tmp, AF.Sin, K, K)

        lr, li, ls, br, bi = (rep[:, i, :] for i in range(5))
        w = [wk[:, i, :] for i in range(8)]
        R = ['rep', 'wk']
        act(w[0], ls, AF.Exp, ['rep'], ['wk'])
        tt('dve', w[1], lr, w[0], ALU.mult, R, ['wk'])
        act(w[1], w[1], AF.Exp, ['wk'], ['wk'])
        tt('dve', w[2], li, w[0], ALU.mult, R, ['wk'])
        sincos(w[2], w[3], w[4], w[5], w[6].bitcast(mybir.dt.int32), 'wk')
        tt('dve', w[3], w[3], w[1], ALU.mult, R, ['wk'])
        tt('dve', w[4], w[4], w[1], ALU.mult, R, ['wk'])
        ts('dve', w[4], w[4], -1.0, None, ALU.add, None, R, ['wk'])
        tt('dve', w[5], lr, lr, ALU.mult, R, ['wk'])
        tt('dve', w[6], li, li, ALU.mult, R, ['wk'])
        tt('dve', w[5], w[5], w[6], ALU.add, R, ['wk'])
        P.op('dve', lambda e: e.reciprocal(out=w[5], in_=w[5]), reads=R, writes=['wk'])
        tt('dve', w[6], w[4], lr, ALU.mult, R, ['wk'])
        tt('dve', w[7], w[3], li, ALU.mult, R, ['wk'])
        tt('dve', w[6], w[6], w[7], ALU.add, R, ['wk'])
        tt('dve', w[6], w[6], w[5], ALU.mult, R, ['wk'])
        tt('dve', w[7], w[3], lr, ALU.mult, R, ['wk'])
        tt('dve', w[0], w[4], li, ALU.mult, R, ['wk'])
        tt('dve', w[7], w[7], w[0], ALU.subtract, R, ['wk'])
        tt('dve', w[7], w[7], w[5], ALU.mult, R, ['wk'])
        tt('dve', w[0], w[6], br, ALU.mult, R, ['wk'])
        tt('dve', w[1], w[7], bi, ALU.mult, R, ['wk'])
        tt('dve', w[0], w[0], w[1], ALU.subtract, R, ['wk'])
        tt('dve', w[1], w[6], bi, ALU.mult, R, ['wk'])
        tt('dve', w[2], w[7], br, ALU.mult, R, ['wk'])
        tt('dve', w[1], w[1], w[2], ALU.add, R, ['wk'])
        for ri in range(2):
            src = w[ri].rearrange("p (a b) -> p a b", a=8)
            for ccl in range(4):
                for g2_ in range(2):
                    ts('dve', BD[:, :, ccl, ri, g2_ * 64:(g2_ + 1) * 64], src, maskBD[:, ccl * 2 + g2_:ccl * 2 + g2_ + 1], None,
                       ALU.mult, None, ['wk', 'cm', 'BD'], ['BD'])
        c = [cw[:, i, :] for i in range(12)]
        Rc = ['cmp', 'cw']
        act(c[0], cmp_[:, 2, :], AF.Exp, ['cmp'], ['cw'])
        tt('dve', c[1], cmp_[:, 0, :], c[0], ALU.mult, Rc, ['cw'])
        act(c[1], c[1], AF.Exp, ['cw'], ['cw'])
        tt('dve', c[2], cmp_[:, 1, :], c[0], ALU.mult, Rc, ['cw'])
        sincos(c[2], c[3], c[4], c[5], c[6].bitcast(mybir.dt.int32), 'cw')
        tt('dve', c[3], c[3], c[1], ALU.mult, Rc, ['cw'])
        tt('dve', c[4], c[4], c[1], ALU.mult, Rc, ['cw'])
        mag_c, th_c, ai_c, ar_c = c[1], c[2], c[3], c[4]
        c = [cw[:, i, :] for i in range(12)]
        ts('dve', c[7], th_c, 512.0, None, ALU.mult, None, Rc, ['cw'])
        sincos(c[7], c[8], c[9], c[5], c[6].bitcast(mybir.dt.int32), 'cw')
        s512_c, c512_c = c[8], c[9]
        for ri in range(2):
            c5 = ct[:, ri, :, :].rearrange("p (a b) h -> p a b h", a=8)
            for ccl in range(4):
                for g2_ in range(2):
                    ps_ = slice(g2_ * 64, (g2_ + 1) * 64)
                    ts('dve', CD[ps_, :, ccl, ri, 32 * ccl + 16 * g2_:32 * ccl + 16 * g2_ + 16], c5[ps_, :, ccl, :],
                       (1.0 if ri == 0 else -1.0), None, ALU.mult, None, ['ct', 'CD'], ['CD'])

        P.barrier()
        pending = []
        for fc in range(8):
            for ccl in range(4):
                cc = fc * 4 + ccl
                thc = th_c[:, cc:cc + 1]
                nv = NT - COL0
                tp_ = cc % 2
                tc_, tsn = tcs[tp_], tss[tp_]
                tck, tsk = ('tc', tp_), ('tsn', tp_)
                ts('dve', targ, tpos, thc, None, ALU.mult, None, ['tpos', 'cw', 'targ'], ['targ'])
                ts('dve', tki, targ, 1.0 / (2 * PI), None, ALU.mult, None, ['targ', 'tki'], ['tki'])
                stt(tsn, tki, -2 * PI, targ, ALU.mult, ALU.add, ['tki', 'targ', tsk], [tsk])
                ts('dve', tsn, tsn, PI, -PI, ALU.min, ALU.max, [tsk], [tsk])
                act(tsn, tsn, AF.Sin, [tsk], [tsk])
                ts('dve', tki, targ, 0.5 * PI, 1.0 / (2 * PI), ALU.add, ALU.mult, ['targ', 'tki'], ['tki'])
                stt(tc_, tki, -2 * PI, targ, ALU.mult, ALU.add, ['tki', 'targ', tck], [tck])
                ts('dve', tc_, tc_, 0.5 * PI, PI, ALU.add, ALU.min, [tck], [tck])
                ts('dve', tc_, tc_, -PI, None, ALU.max, None, [tck], [tck])
                act(tc_, tc_, AF.Sin, [tck], [tck])
                ts('dve', magt, tpos, 0.0, mag_c[:, cc:cc + 1], ALU.mult, ALU.add, ['tpos', 'cw', 'magt'], ['magt'])
                for ti, (t0, tn) in enumerate(TILES_ALL):
                    par = ti % 2
                    be, bek = P.bank()
                    bi_, bik = P.bank()
                    mm(be[:, :tn], BD[:, fc, ccl, 0, :], ub[:, fc, t0:t0 + tn], True, True, ['BD', 'ub'], [bek])
                    mm(bi_[:, :tn], BD[:, fc, ccl, 1, :], ub[:, fc, t0:t0 + tn], True, True, ['BD', 'ub'], [bik])
                    cs_ = tc_[:, 0:tn]
                    sn_ = tsn[:, 0:tn]
                    er, ei = be[:, :tn], bi_[:, :tn]
                    esr, esi = esb[par][0][:, :tn], esb[par][1][:, :tn]
                    ekr, eki = ('esb', par, 0), ('esb', par, 1)
                    cp('act', esr, er, [bek], [ekr])
                    cp('act', esi, ei, [bik], [eki])
                    tt('dve', t4[0][:, :tn], esr, cs_, ALU.mult, [ekr, tck], ['t40'])
                    tt('pool', t4[1][:, :tn], esi, sn_, ALU.mult, [eki, tsk], ['t41'])
                    tt('dve', t4[2][:, :tn], esi, cs_, ALU.mult, [eki, tck], ['t42'])
                    tt('dve', t4[3][:, :tn], esr, sn_, ALU.mult, [ekr, tsk], ['t43'])
                    tt('dve', et[1][:, :tn], t4[2][:, :tn], t4[3][:, :tn], ALU.subtract, ['t42', 't43'], ['et1'])
                    tt('dve', et[0][:, :tn], t4[0][:, :tn], t4[1][:, :tn], ALU.add, ['t40', 't41'], ['et0'])
                    if ti == 0:
                        memset('dve', et[0][:, 0:COL0], 0.0, ['et0'])
                        memset('dve', et[1][:, 0:COL0], 0.0, ['et1'])
                    else:
                        vrL, viL = vv[1 - par][0][:, 511:512], vv[1 - par][1][:, 511:512]
                        s5c_, c5c_ = s512_c[:, cc:cc + 1], c512_c[:, cc:cc + 1]
                        RK = [('vv', 1 - par, 0), ('vv', 1 - par, 1), 'cw', 'vini']
                        ts('dve', vini[:, 2:3], viL, s5c_, None, ALU.mult, None, RK, ['vini'])
                        stt(vini[:, 0:1], vrL, c5c_, vini[:, 2:3], ALU.mult, ALU.subtract, RK, ['vini'])
                        ts('dve', vini[:, 3:4], vrL, s5c_, None, ALU.mult, None, RK, ['vini'])
                        stt(vini[:, 1:2], viL, c5c_, vini[:, 3:4], ALU.mult, ALU.add, RK, ['vini'])
                    for ri in range(2):
                        vcur = vv[par][ri]
                        if ti == 0:
                            P.op('dve', lambda e, vcur=vcur, ri=ri, tn=tn: e.tensor_tensor_scan(out=vcur[:, :tn], data0=magt[:, :tn], data1=et[ri][:, :tn], initial=0.0, op0=ALU.mult, op1=ALU.add),
                                 reads=['magt', f'et{ri}'], writes=[('vv', par, ri)])
                        else:
                            vprev = vv[1 - par][ri]
                            P.op('dve', lambda e, vcur=vcur, vprev=vprev, ri=ri, tn=tn: e.tensor_tensor_scan(out=vcur[:, :tn], data0=magt[:, :tn], data1=et[ri][:, :tn], initial=vini[:, ri:ri + 1], op0=ALU.mult, op1=ALU.add),
                                 reads=['magt', f'et{ri}', 'vini'], writes=[('vv', par, ri)])
                    while pending:
                        pending.pop(0)()
                    vr, vi = vv[par][0][:, :tn], vv[par][1][:, :tn]
                    xr, xi = xx[par][0], xx[par][1]
                    lo = 0
                    if ti == 0:
                        lo = COL0
                    xk = ('xx', par)
                    tt('pool', pt[0][:, lo:tn], vr[:, lo:tn], cs_[:, lo:tn], ALU.mult, [('vv', par, 0), tck], ['pt0'])
                    tt('pool', pt[1][:, lo:tn], vi[:, lo:tn], sn_[:, lo:tn], ALU.mult, [('vv', par, 1), tsk], ['pt1'])
                    tt('pool', xr[:, lo:tn], pt[0][:, lo:tn], pt[1][:, lo:tn], ALU.subtract, ['pt0', 'pt1'], [xk])
                    tt('pool', pt[0][:, lo:tn], vr[:, lo:tn], sn_[:, lo:tn], ALU.mult, [('vv', par, 0), tsk, 'pt0'], ['pt0'])
                    tt('pool', pt[1][:, lo:tn], vi[:, lo:tn], cs_[:, lo:tn], ALU.mult, [('vv', par, 1), tck, 'pt1'], ['pt1'])
                    tt('pool', xi[:, lo:tn], pt[0][:, lo:tn], pt[1][:, lo:tn], ALU.add, ['pt0', 'pt1'], [xk])
                    if ti == 0:
                        arc, aic = ar_c[:, cc:cc + 1], ai_c[:, cc:cc + 1]
                        x0r, x0i = x0[:, 0, cc, :], x0[:, 1, cc, :]
                        ts('dve', dtmp[:, 0:4], x0i, aic, None, ALU.mult, None, ['x0', 'cw'], ['dtmp'])
                        stt(dtmp[:, 4:8], x0r, arc, dtmp[:, 0:4], ALU.mult, ALU.subtract, ['x0', 'cw', 'dtmp'], ['dtmp'])
                        tt('dve', x1[:, 0, cc, :], dtmp[:, 4:8], er[:, 0:NS], ALU.add, ['dtmp', bek], ['x1'])
                        ts('dve', dtmp[:, 0:4], x0r, aic, None, ALU.mult, None, ['x0', 'cw', 'dtmp'], ['dtmp'])
                        stt(dtmp[:, 4:8], x0i, arc, dtmp[:, 0:4], ALU.mult, ALU.add, ['x0', 'cw', 'dtmp'], ['dtmp'])
                        tt('dve', x1[:, 1, cc, :], dtmp[:, 4:8], ei[:, 0:NS], ALU.add, ['dtmp', bik], ['x1'])
                        cp('dve', xr[:, 0:NS], x1[:, 0, cc, :], ['x1'], [xk])
                        cp('dve', xi[:, 0:NS], x1[:, 1, cc, :], ['x1'], [xk])
                    if ti == len(TILES_ALL) - 1:
                        la = tn - 1
                        tt('dve', dtmp[:, 0:1], vr[:, la:la + 1], cs_[:, la:la + 1], ALU.mult, [('vv', par, 0), tck, 'dtmp'], ['dtmp'])
                        tt('dve', dtmp[:, 1:2], vi[:, la:la + 1], sn_[:, la:la + 1], ALU.mult, [('vv', par, 1), tsk, 'dtmp'], ['dtmp'])
                        tt('dve', pst[:, 0, cc:cc + 1], dtmp[:, 0:1], dtmp[:, 1:2], ALU.subtract, ['dtmp'], ['pst'])
                        tt('dve', dtmp[:, 2:3], vr[:, la:la + 1], sn_[:, la:la + 1], ALU.mult, [('vv', par, 0), tsk, 'dtmp'], ['dtmp'])
                        tt('dve', dtmp[:, 3:4], vi[:, la:la + 1], cs_[:, la:la + 1], ALU.mult, [('vv', par, 1), tck, 'dtmp'], ['dtmp'])
                        tt('dve', pst[:, 1, cc:cc + 1], dtmp[:, 2:3], dtmp[:, 3:4], ALU.add, ['dtmp'], ['pst'])
                    def cproj(fc=fc, ccl=ccl, ti=ti, t0=t0, tn=tn, xr=xr, xi=xi, xk=xk):
                        by, byk = P.bank()
                        mm(by[:, :tn], CD[:, fc, ccl, 0, :], xr[:, :tn], True, False, ['CD', xk], [byk], inc=False)
                        mm(by[:, :tn], CD[:, fc, ccl, 1, :], xi[:, :tn], False, True, ['CD', xk], [byk])
                        if ccl == 0:
                            cp('act', yacc[:, t0:t0 + tn], by[:, :tn], [byk], [('yacc', ti)])
                        else:
                            tt('dve', yacc[:, t0:t0 + tn], yacc[:, t0:t0 + tn], by[:, :tn], ALU.add, [byk, ('yacc', ti)], [('yacc', ti)])
                    pending.append(cproj)
            while pending:
                pending.pop(0)()
            for ti, (t0, tn) in enumerate(TILES_ALL):
                yk = ('yacc', ti)
                ysl = yacc[:, t0:t0 + tn]
                stt(ysl, ub[:, fc, t0:t0 + tn], dd[:, 0, fc:fc + 1], ysl, ALU.mult, ALU.add, ['ub', 'dd', yk], [yk])
                act(g1[:, :tn], ysl, AF.Square, [yk], ['g1'])
                ts('dve', g1[:, :tn], g1[:, :tn], 0.044715, 1.0, ALU.mult, ALU.add, ['g1'], ['g1'])
                tt('dve', g1[:, :tn], g1[:, :tn], ysl, ALU.mult, ['g1', yk], ['g1'])
                act(g2[:, :tn], g1[:, :tn], AF.Sigmoid, ['g1'], ['g2'], scale=2.0 * math.sqrt(2.0 / PI))
                tt('dve', ub[:, fc, t0:t0 + tn], ysl, g2[:, :tn], ALU.mult, [yk, 'g2'], ['ub'])
        dm('sp', o_ps5[l], pst, 'ops5', ['pst'], [])
        dm('sp', o_ss5[l], x1, 'oss5', ['x1'], [])
        blocks = [[(w_glu[l, j], 8)] for j in range(8)]

        def epi(j, ti, t0, tn, bk, bkey):
            s = ystg[j % 2]
            sk = ('ystg', j % 2, ti)
            act(g1[:, :tn], bk[:, :tn], AF.Sigmoid, [bkey, 'dd'], ['g1'], bias=dd[:, 1, j:j + 1])
            tt('dve', s[:, t0:t0 + tn], ub[:, j, t0:t0 + tn], g1[:, :tn], ALU.mult, ['ub', 'g1'], [sk])
            if ti == len(TILES_ALL) - 1:
                dm('sp', y_scr[j], s, f'yst{j % 2}', [('ystg', j % 2, t) for t in range(len(TILES_ALL))], [('y', j)])
        gemm(blocks, lambda kg, t0, tn: ub[:, kg, t0:t0 + tn], ['ub'], TILES_ALL, epi)

    def stage_gla(l):
        A.reset()
        aT = A.alloc([128, NT], BF16)
        w2 = A.alloc([128, 512], BF16)
        gp = A.alloc([128, 6], F32)
        nbg = A.alloc([128, 4], F32)
        rmask = A.alloc([128, NT], F32)
        qT = A.alloc([128, NT], BF16)
        kT = A.alloc([128, NT], BF16)
        vT = A.alloc([128, 2, NT], BF16)
        rT = A.alloc([128, 2, NT], BF16)
        G = A.alloc([128, NT], F32)
        B = A.alloc([128, NT], F32)
        E1 = A.alloc([128, NT], F32)
        E2 = A.alloc([128, NT], F32)
        qs = A.alloc([128, NT], BF16)
        ks = A.alloc([128, NT], BF16)
        ku = A.alloc([128, NT], BF16)
        cd = A.alloc([128, 17], F32)
        OT = A.alloc([128, 2, NT], F32)
        vtm = [A.alloc([128, 256], BF16) for _ in range(2)]
        kutm = [A.alloc([128, 128], BF16) for _ in range(2)]
        ATb = [A.alloc([128, 128], BF16) for _ in range(2)]
        Sf = A.alloc([128, 256], F32)
        Sb = A.alloc([128, 256], BF16)
        S0 = [A.alloc([128, 256], F32) for _ in range(2)]
        S1b = A.alloc([128, 256], BF16)
        krtm = A.alloc([128, 128], BF16)
        vrtm = A.alloc([128, 256], BF16)
        kmask = A.alloc([128, 128], BF16)
        gdec = A.alloc([128, NS], F32)
        sq = A.alloc([128, 512], BF16)
        rs = A.alloc([128, 512], F32)
        ystg = [A.alloc([128, NT], BF16) for _ in range(2)]
        scale = 128.0 ** -0.5

        dm('sp', aT[0:16, :], z_scr[24, 0:16, :], 'aT', [('z', 24)], ['aT'])
        dm('pool', w2[0:16, :], gla_w2[l], 'w2', [], ['w2'])
        dm('sp', gp, gla_p[l], 'gp', [], ['gp'])
        dm('sp', rmask, c_rmask, 'rmask', [], ['rmask'])
        ts('dve', nbg, gp[:, 0:4], -1.0, None, ALU.mult, None, ['gp'], ['nbg'])
        for h in range(4):
            dm('sp', qT, z_scr[8 + h], 'qT', [('z', 8 + h)], ['qT'])
            dm('sp', kT, z_scr[12 + h], 'kT', [('z', 12 + h)], ['kT'])
            dm('sp', vT, z_scr[16 + 2 * h:18 + 2 * h].rearrange("b p n -> p b n"), 'vT', [('z', 16 + 2 * h), ('z', 17 + 2 * h)], ['vT'])
            dm('sp', rT, z_scr[25 + 2 * h:27 + 2 * h].rearrange("b p n -> p b n"), 'rT', [('z', 25 + 2 * h), ('z', 26 + 2 * h)], ['rT'])
            for (t0, tn) in TILES_ALL:
                bk, bkey = P.bank()
                mm(bk[:, :tn], w2[0:16, h * 128:(h + 1) * 128], aT[0:16, t0:t0 + tn], True, True, ['w2', 'aT'], [bkey])
                act(G[:, t0:t0 + tn], bk[:, :tn], AF.Exp, [bkey, 'nbg'], ['G'], scale=-1.0, bias=nbg[:, h:h + 1])
            act(G, G, AF.Ln, ['G'], ['G'], bias=1.0)
            ts('dve', G, G, 1.0 / 16.0, None, ALU.mult, None, ['G'], ['G'])
            P.op('dve', lambda e: e.tensor_tensor_scan(out=B, data0=rmask, data1=G, initial=0.0, op0=ALU.mult, op1=ALU.add), reads=['rmask', 'G'], writes=['B'])
            act(E1, B, AF.Exp, ['B'], ['E1'], scale=-1.0)
            act(E2, B, AF.Exp, ['B'], ['E2'])
            stt(qs, qT, scale, E1, ALU.mult, ALU.mult, ['qT', 'E1'], ['qs'])
            tt('dve', ks, kT, E2, ALU.mult, ['kT', 'E2'], ['ks'])
            act(gdec, G[:, 0:NS], AF.Exp, ['G'], ['gdec'], scale=-1.0)
            B3 = B.rearrange("p (c n) -> p c n", n=128)
            E13 = E1.rearrange("p (c n) -> p c n", n=128)
            cp('dve', cd, E13[:, :, 127], ['E1'], ['cd'])
            tt('dve', E2.rearrange("p (c n) -> p c n", n=128), B3, B3[:, :, 127:128].broadcast_to([128, 17, 128]), ALU.subtract, ['B', 'ks'], ['E2'])
            act(E2, E2, AF.Exp, ['E2'], ['E2'])
            tt('dve', ku, kT, E2, ALU.mult, ['kT', 'E2'], ['ku'])
            memset('dve', ks[:, 0:COL0], 0.0, ['ks'])
            memset('dve', ku[:, 0:COL0], 0.0, ['ku'])
            memset('dve', Sf, 0.0, ['Sf'])
            memset('dve', Sb, 0.0, ['Sb'])
            for c in range(17):
                cs_ = slice(c * 128, (c + 1) * 128)
                par = c % 2
                for vc in range(2):
                    bk, bkey = P.bank()
                    bb = bk.bitcast(BF16)
                    tr(bb[:, 0:128], vT[:, vc, cs_], ident_b, ['vT', 'cb'], [bkey])
                    cp('act', vtm[par][:, vc * 128:(vc + 1) * 128], bb[:, 0:128], [bkey], [('vtm', par)])
                bk, bkey = P.bank()
                bb = bk.bitcast(BF16)
                tr(bb[:, 0:128], ku[:, cs_], ident_b, ['ku', 'cb'], [bkey])
                cp('act', kutm[par], bb[:, 0:128], [bkey], [('kutm', par)])
                ba, bak = P.bank()
                mm(ba[:, 0:128], ks[:, cs_], qs[:, cs_], True, True, ['ks', 'qs'], [bak])
                tt('dve', ATb[par], ba[:, 0:128], tri_b, ALU.mult, [bak, 'cb'], [('ATb', par)])
                for vc in range(2):
                    bo, bok = P.bank()
                    mm(bo[:, 0:128], vtm[par][:, vc * 128:(vc + 1) * 128], ATb[par], True, False, [('vtm', par), ('ATb', par)], [bok], inc=False)
                    mm(bo[:, 0:128], Sb[:, vc * 128:(vc + 1) * 128], qs[:, cs_], False, True, ['Sb', 'qs'], [bok])
                    cp('act', OT[:, vc, cs_], bo[:, 0:128], [bok], ['OT'])
                bs, bsk = P.bank()
                mm(bs[:, 0:256], kutm[par], vtm[par], True, True, [('kutm', par), ('vtm', par)], [bsk])
                stt(Sf, Sf, cd[:, c:c + 1], bs[:, 0:256], ALU.mult, ALU.add, ['Sf', 'cd', bsk], ['Sf'])
                cp('dve', Sb, Sf, ['Sf'], ['Sb'])
            dm('sp', o_pgla[l, h], Sf, 'opgla', ['Sf'], [])
            bk, bkey = P.bank()
            bb = bk.bitcast(BF16)
            tr(bb[:, 0:128], kT[:, 0:128], ident_b, ['kT', 'cb'], [bkey])
            cp('act', krtm, bb[:, 0:128], [bkey], ['krtm'])
            for vc in range(2):
                bk, bkey = P.bank()
                bb = bk.bitcast(BF16)
                tr(bb[:, 0:128], vT[:, vc, 0:128], ident_b, ['vT', 'cb'], [bkey])
                cp('act', vrtm[:, vc * 128:(vc + 1) * 128], bb[:, 0:128], [bkey], ['vrtm'])
            for n in range(NS):
                s0 = S0[n % 2]
                s0k = ('S0', n % 2)
                dm('sp', s0, gla_s0[l, n, h], f's0{n % 2}', [], [s0k])
                ts('dve', kmask[0:4, :], krtm[0:4, :], I4[0:4, n:n + 1], None, ALU.mult, None, ['krtm', 'cm'], ['kmask'])
                bs, bsk = P.bank()
                mm(bs[:, 0:256], kmask[0:4, :], vrtm[0:4, :], True, True, ['kmask', 'vrtm'], [bsk])
                stt(s0, s0, gdec[:, n:n + 1], bs[:, 0:256], ALU.mult, ALU.add, [s0k, 'gdec', bsk], [s0k])
                cp('dve', S1b, s0, [s0k], ['S1b'])
                dm('sp', o_sgla[l, n, h], s0, f's1{n % 2}', [s0k], [])
                for vc in range(2):
                    bo, bok = P.bank()
                    mm(bo[:, 0:1], S1b[:, vc * 128:(vc + 1) * 128], qT[:, n:n + 1], True, True, ['S1b', 'qT'], [bok])
                    act(OT[:, vc, n:n + 1], bo[:, 0:1], AF.Copy, [bok], ['OT'], scale=scale)
            for ti, (t0, tn) in enumerate(TILES_ALL):
                bk, bkey = P.bank()
                for vc in range(2):
                    act(sq[:, :tn], OT[:, vc, t0:t0 + tn], AF.Square, ['OT'], ['gsq'])
                    mm(bk[:, :tn], ones_b, sq[:, :tn], vc == 0, vc == 1, ['gsq', 'cb1'], [bkey])
                ts('dve', rs[:, :tn], bk[:, :tn], 1.0 / 256.0, EPS, ALU.mult, ALU.add, [bkey], ['grs'])
                act(rs[:, :tn], rs[:, :tn], AF.Ln, ['grs'], ['grs'])
                act(rs[:, :tn], rs[:, :tn], AF.Exp, ['grs'], ['grs'], scale=-0.5)
                for vc in range(2):
                    j = 2 * h + vc
                    s = ystg[j % 2]
                    sk = ('ystg', j % 2, ti)
                    stt(OT[:, vc, t0:t0 + tn], OT[:, vc, t0:t0 + tn], gp[:, 4 + vc:5 + vc], rs[:, :tn], ALU.mult, ALU.mult, ['OT', 'gp', 'grs'], ['OT'])
                    tt('dve', s[:, t0:t0 + tn], OT[:, vc, t0:t0 + tn], rT[:, vc, t0:t0 + tn], ALU.mult, ['OT', 'rT'], [sk])
            for vc in range(2):
                j = 2 * h + vc
                dm('sp', y_scr[8 + j], ystg[j % 2], f'yst{j % 2}', [('ystg', j % 2, t) for t in range(len(TILES_ALL))], [('y', 8 + j)])

    def stage_swa(l, first):
        A.reset()
        qT = A.alloc([128, 8, NT], BF16)
        kT = A.alloc([128, 2, NT], BF16)
        vT = A.alloc([128, 2, NT], BF16)
        vtm = A.alloc([128, 17, 256], BF16)
        yc = A.alloc([128, 8, NT], BF16)
        biasT = A.alloc([128, 2, 16, 128], BF16)
        bdec = A.alloc([128, 16], BF16)
        bnew = A.alloc([128, 16], F32)
        rb33 = A.alloc([128, 16], F32)
        oh = A.alloc([128, 2 * 256 + 128 + 4], F32)
        Gt = A.alloc([128, 2, 16, 128], F32)
        Gb = A.alloc([128, 2, 16, 128], BF16)
        estg = A.alloc([128, 2, 256], F32)
        padc = A.alloc([128, 2], F32)
        sk_ = A.alloc([128, 16], F32)
        esk = A.alloc([128, 16, 128], F32)
        PT = [A.alloc([128, 512], BF16) for _ in range(4)]
        den = A.alloc([128, 512], F32)
        kc_ = A.alloc([128, NS, 2, 128], BF16)
        vc_ = A.alloc([128, NS, 256], BF16)
        pd = A.alloc([128, 4], BF16)
        pn = A.alloc([128, 4], BF16)
        pn2 = A.alloc([128, 4], F32)
        d4 = A.alloc([128, 4], F32)
        ktm = A.alloc([128, 256], F32)
        vlast = A.alloc([128, 256], F32)
        k0tm = A.alloc([128, 256], F32)
        v0tm = A.alloc([128, 256], F32)

        dm('sp', qT, z_scr[33:41].rearrange("b p n -> p b n"), 'qT', [('z', j) for j in range(33, 41)], ['qT'])
        dm('sp', kT, z_scr[41:43].rearrange("b p n -> p b n"), 'kT', [('z', 41), ('z', 42)], ['kT'])
        dm('sp', vT, z_scr[43:45].rearrange("b p n -> p b n"), 'vT', [('z', 43), ('z', 44)], ['vT'])
        dm('sp', padc, c_pad, 'padc', [], ['padc'])
        memset('pool', yc[:, :, 0:COL0], 0.0, ['yc'])
        dm('sp', sk_[0:64, :], swa_sink[l], 'sk', [], ['sk'])
        dm('pool', kc_, swa_kT[l], 'kc', [], ['kc'])
        dm('pool', vc_, swa_vn[l].rearrange("n s f -> s n f"), 'vc', [], ['vc'])
        dm('sp', rb33[0:32, :], relb, 'rb', [], ['rb33'])
        memset('dve', rb33[32:33, :], NEG, ['rb33m'])
        dm('sp', oh[0:33, :], c_oh, 'oh', [], ['oh'])
        for rel in range(2):
            bk, bkey = P.bank()
            mm(bk[0:16, 0:256], rb33[0:33, :], oh[0:33, rel * 256:(rel + 1) * 256], True, True, ['rb33', 'rb33m', 'oh'], [bkey])
            act(estg[0:16, rel, :], bk[0:16, 0:256], AF.Copy, [bkey], ['estg'], scale=8.0)
        dm('sp', e_scr.rearrange("r h j -> h r j"), estg[0:16, :, :], 'est', ['estg'], ['e_scr'])
        for rel in range(2):
            src = bass.AP(tensor=e_scr.tensor, offset=rel * 16 * 256, ap=[[1, 128], [256, 16], [1, 128]])
            dm('sp', Gt[:, rel, :, :], src, f'Gt{rel}', ['e_scr'], [('Gt', rel)])
        cp('dve', Gb.rearrange("p a b c -> p (a b c)"), Gt.rearrange("p a b c -> p (a b c)"), [('Gt', 0), ('Gt', 1)], ['Gb'])
        Gb2 = Gb.rearrange("p a b c -> p (a b c)")
        bT2 = biasT.rearrange("p a b c -> p (a b c)")
        for i in range(8):
            bk, bkey = P.bank()
            mm(bk[:, :], J_b, Gb2[:, i * 512:(i + 1) * 512], True, True, ['Gb', 'cb'], [bkey])
            cp('act', bT2[:, i * 512:(i + 1) * 512], bk[:, :], [bkey], ['biasT'])
        bk, bkey = P.bank()
        mm(bk[:, 0:16], oh[0:33, 512:640], rb33[0:33, :], True, True, ['rb33', 'rb33m', 'oh'], [bkey])
        act(bdec, bk[:, 0:16], AF.Copy, [bkey], ['bdec'], scale=8.0)
        bk, bkey = P.bank()
        mm(bk[0:4, 0:16], oh[0:33, 640:644], rb33[0:33, :], True, True, ['rb33', 'rb33m', 'oh'], [bkey])
        act(bnew[0:4, :], bk[0:4, 0:16], AF.Copy, [bkey], ['bnew'], scale=8.0)
        act(sk_[0:64, :], sk_[0:64, :], AF.Exp, ['sk'], ['sk'])
        cp('dve', esk[0:64, :, :], sk_[0:64, :].unsqueeze(2).broadcast_to([64, 16, 128]), ['sk'], ['esk'])
        for b in range(17):
            for c in range(2):
                bk, bkey = P.bank()
                bb = bk.bitcast(BF16)
                tr(bb[:, 0:128], vT[:, c, b * 128:(b + 1) * 128], ident_b, ['vT', 'cb'], [bkey])
                cp('act' if c == 0 else 'dve', vtm[:, b, c * 128:(c + 1) * 128], bb[:, 0:128], [bkey], ['vtm'])
        pti = 0
        for b in range(17):
            qs_ = slice(b * 128, (b + 1) * 128)
            for kvh in range(4):
                kvp, po = kvh // 2, 64 * (kvh % 2)
                prt = slice(po, po + 64)
                bo, bok = P.bank()
                bd, bdk = P.bank()
                kbs = [b] if b == 0 else [b - 1, b]
                for ki, kb in enumerate(kbs):
                    rel = 1 if kb == b else 0
                    bs, bsk = P.bank()
                    mm(bs[:, :], kT[prt, kvp, kb * 128:(kb + 1) * 128], qT[prt, kvp * 4:kvp * 4 + 4, qs_], True, False, ['kT', 'qT'], [bsk], inc=False)
                    mm(bs[:, :], ident_b, biasT[:, rel, kvh * 4:kvh * 4 + 4, :], False, True, ['cb', 'biasT'], [bsk])
                    p_ = PT[pti % 4]
                    pk = ('PT', pti % 4)
                    pti += 1
                    act(p_, bs[:, :], AF.Exp, [bsk, 'padc'], [pk], scale=0.125, bias=padc[:, (1 if kb == 0 else 0):(2 if kb == 0 else 1)])
                    mm(bo[0:64, :], vtm[:, kb, kvh * 64:(kvh + 1) * 64], p_, ki == 0, ki == len(kbs) - 1, ['vtm', pk], [bok])
                    mm(bd[0:64, :], ones_b[:, 0:64], p_, ki == 0, ki == len(kbs) - 1, ['cb1', pk], [bdk])
                tt('dve', den[0:64, :], bd[0:64, :], esk[0:64, kvh * 4:kvh * 4 + 4, :], ALU.add, [bdk, 'esk'], ['den'])
                act(den[0:64, :], den[0:64, :], AF.Ln, ['den'], ['den'])
                act(den[0:64, :], den[0:64, :], AF.Exp, ['den'], ['den'], scale=-1.0)
                lo = COL0 if b == 0 else 0
                tt('dve', yc[prt, kvp * 4:kvp * 4 + 4, b * 128 + lo:(b + 1) * 128], bo[0:64, :].rearrange("p (g q) -> p g q", g=4)[:, :, lo:],
                   den[0:64, :].rearrange("p (g q) -> p g q", g=4)[:, :, lo:], ALU.mult, [bok, 'den'], ['yc'])
        for c in range(2):
            for (blk, dst, dk_) in ((16, ktm, 'ktm'), (0, k0tm, 'k0tm')):
                bk, bkey = P.bank()
                bb = bk.bitcast(BF16)
                tr(bb[:, 0:128], kT[:, c, blk * 128:(blk + 1) * 128], ident_b, ['kT', 'cb'], [bkey])
                cp('act', dst[:, c * 128:(c + 1) * 128], bb[:, 0:128], [bkey], [dk_])
        cp('dve', vlast, vtm[:, 16, :], ['vtm'], ['vlast'])
        cp('dve', v0tm, vtm[:, 0, :], ['vtm'], ['v0tm'])
        dm('sp', o_pk[l], ktm, 'opk', ['ktm'], [])
        dm('sp', o_pv[l], vlast, 'opv', ['vlast'], [])
        for n in range(NS):
            dm('sp', o_sk[l, n, 0:127, :], swa_kn[l, n, 1:128, :], 'osk', [], [])
            dm('sp', o_sv[l, n, 0:127, :], swa_vn[l, n, 1:128, :], 'osv', [], [])
        dm('sp', o_sk[l, :, 127, :], k0tm[0:NS, :], 'osk2', ['k0tm'], [])
        dm('sp', o_sv[l, :, 127, :], v0tm[0:NS, :], 'osv2', ['v0tm'], [])
        for n in range(NS):
            for kvh in range(4):
                kvp, po = kvh // 2, 64 * (kvh % 2)
                prt = slice(po, po + 64)
                qn = qT[prt, kvp * 4:kvp * 4 + 4, n]
                bs, bsk = P.bank()
                mm(bs[:, 0:4], kc_[prt, n, kvp, :], qn, True, False, ['kc', 'qT'], [bsk], inc=False)
                mm(bs[:, 0:4], ident_b, bdec[:, kvh * 4:kvh * 4 + 4], False, True, ['cb', 'bdec'], [bsk])
                act(pd, bs[:, 0:4], AF.Exp, [bsk], ['pd'], scale=0.125)
                bn, bnk = P.bank()
                mm(bn[0:4, 0:4], kT[prt, kvp, 0:NS], qn, True, True, ['kT', 'qT'], [bnk])
                tt('dve', pn2[0:4, :], bn[0:4, 0:4], bnew[0:4, kvh * 4:kvh * 4 + 4], ALU.add, [bnk, 'bnew'], ['pn2'])
                act(pn2[0:4, :], pn2[0:4, :], AF.Exp, ['pn2'], ['pn2'], scale=0.125)
                ts('dve', pn[0:4, :], pn2[0:4, :], I4[0:4, n:n + 1], None, ALU.mult, None, ['pn2', 'cm'], ['pn'])
                bo, bok = P.bank()
                mm(bo[0:64, 0:4], vc_[:, n, kvh * 64:(kvh + 1) * 64], pd, True, False, ['vc', 'pd'], [bok], inc=False)
                mm(bo[0:64, 0:4], vtm[0:4, 0, kvh * 64:(kvh + 1) * 64], pn[0:4, :], False, True, ['vtm', 'pn'], [bok])
                bd, bdk = P.bank()
                mm(bd[0:64, 0:4], ones_b[:, 0:64], pd, True, False, ['cb1', 'pd'], [bdk], inc=False)
                mm(bd[0:64, 0:4], ones_b[0:4, 0:64], pn[0:4, :], False, True, ['cb1', 'pn'], [bdk])
                tt('dve', d4[0:64, :], bd[0:64, 0:4], sk_[0:64, kvh * 4:kvh * 4 + 4], ALU.add, [bdk, 'sk'], ['d4'])
                P.op('dve', lambda e: e.reciprocal(out=d4[0:64, :], in_=d4[0:64, :]), reads=['d4'], writes=['d4'])
                tt('dve', yc[prt, kvp * 4:kvp * 4 + 4, n], bo[0:64, 0:4], d4[0:64, :], ALU.mult, [bok, 'd4'], ['yc'])
        for j in range(8):
            dm('sp', y_scr[16 + j], yc[:, j, :], 'ycst', ['yc'], [('y', 16 + j)])

    def stage_merge(l):
        A.reset()
        yq = A.alloc([128, 24, QW], BF16)
        gq = [A.alloc([128, 3, QW], BF16) for _ in range(2)]
        macc = A.alloc([128, QW], F32)
        mtmp = A.alloc([128, QW], F32)
        mq = A.alloc([128, 16, QW], BF16)
        mix = A.alloc([128, 16, QW], F32)
        xq = A.alloc([128, 16, QW], F32)
        hq = A.alloc([128, 16, QW], BF16)
        sq = A.alloc([128, 2, QW], BF16)
        rs = A.alloc([128, QW], F32)
        def load_yq(q):
            cq_ = slice(q * QW, q * QW + QW)
            dm('sp', yq, y_scr[:, :, cq_].rearrange("b p n -> p b n"), 'yq', [('y', j) for j in range(24)], ['yq'])
        load_yq(0)
        xck = lambda c: ('xqc', c)
        xall = ['xq'] + [('xqc', c) for c in range(16)]
        for q in range(NQ):
            c0 = q * QW
            cq = slice(c0, c0 + QW)
            dm('sp', xq, x_scr[:, :, cq], 'xq', [('x_scr', q)], ['xq'])
            blocks = []
            for j in range(16):
                for br in range(3):
                    blocks.append([(w_up[l, br, j], 8)])

            def epi(jj, ti, t0, tn, bk, bkey, q=q, cq=cq, c0=c0):
                j, br = jj // 3, jj % 3
                g = gq[j % 2]
                gk = ('gq', j % 2)
                if br == 0 and ti == 0:
                    src = bass.AP(tensor=z_scr.tensor, offset=(45 + j) * 128 * NT + c0, ap=[[NT, 128], [16 * 128 * NT, 3], [1, QW]])
                    dm('sp', g, src, f'gq{j % 2}', [('z', 45 + j), ('z', 61 + j), ('z', 77 + j)], [gk])
                sl = slice(t0, t0 + tn)
                if br == 0:
                    tt('dve', macc[:, sl], bk[:, :tn], g[:, 0, sl], ALU.mult, [bkey, gk], [('macc', ti)])
                else:
                    tt('dve', mtmp[:, sl], bk[:, :tn], g[:, br, sl], ALU.mult, [bkey, gk], [('mtmp', ti)])
                    if br == 1:
                        tt('dve', macc[:, sl], macc[:, sl], mtmp[:, sl], ALU.add, [('macc', ti), ('mtmp', ti)], [('macc', ti)])
                    else:
                        tt('dve', mq[:, j, sl], macc[:, sl], mtmp[:, sl], ALU.add, [('macc', ti), ('mtmp', ti)], ['mq'])
            gemm_blocks_rhs(blocks, [(lambda kg, t0, tn, br=jj % 3: yq[:, br * 8 + kg, t0:t0 + tn]) for jj in range(48)], ['yq'], TILES_Q, epi)
            if q + 1 < NQ:
                load_yq(q + 1)
            blocks = [[(w_out[l, j], 16)] for j in range(16)]

            def epi2(j, ti, t0, tn, bk, bkey):
                cp('act' if ti == 0 else 'dve', mix[:, j, t0:t0 + tn], bk[:, :tn], [bkey], ['mix'])
            gemm(blocks, lambda kg, t0, tn: mq[:, kg, t0:t0 + tn], ['mq'], TILES_Q, epi2)
            rstd_of(mix, 'mix', QW, sq, rs, 'n1')
            for c in range(16):
                stt(mix[:, c, :], mix[:, c, :], gn[:, l, 1, c:c + 1], rs, ALU.mult, ALU.mult, ['mix', 'n1rs', 'gn'], [('mixc', c)])
                tt('pool' if c % 2 == 0 else 'dve', xq[:, c, :], xq[:, c, :], mix[:, c, :], ALU.add, ['mix', ('mixc', c), 'xq'], [xck(c)])
            dm('sp', x_scr[:, :, cq], xq, 'xst', xall, [('x_scr', q)])
            rstd_of(xq, 'xq', QW, sq, rs, 'n2', ckey=xck)
            norm_to(hq, 'hq', xq, 'xq', QW, gn[:, l, 2, :], rs, 'n2', ckey=xck)
            dm('sp', h_scr[:, :, cq], hq, 'hst', ['hq'], [('h_scr', q)])

    def gemm_blocks_rhs(blocks, rhs_fns, rhs_keys, tiles, epi):
        flat = [(ap, kc) for blk in blocks for (ap, kc) in blk]
        LA = NWB - 1
        loaded = {}
        for i in range(min(LA, len(flat))):
            loaded[i] = wload(*flat[i])
        idx = 0
        for j, blk in enumerate(blocks):
            bks = [P.bank() for _ in tiles]
            (ap, kc) = blk[0]
            if idx + LA < len(flat):
                loaded[idx + LA] = wload(*flat[idx + LA])
            wb, wk = loaded.pop(idx)
            idx += 1
            rf = rhs_fns[j]
            for ti, (t0, tn) in enumerate(tiles):
                bk, bkey = bks[ti]
                for k in range(kc):
                    P.op('pe', lambda e, bk=bk, wb=wb, k=k, t0=t0, tn=tn, kc=kc, rf=rf:
                         e.matmul(bk[:, :tn], lhsT=wb[:, k, :], rhs=rf(k, t0, tn), start=(k == 0), stop=(k == kc - 1)),
                         reads=[wk] + rhs_keys, writes=[bkey], inc=(k == kc - 1))
            for ti, (t0, tn) in enumerate(tiles):
                epi(j, ti, t0, tn, bks[ti][0], bks[ti][1])

    def stage_ff1(l):
        A.reset()
        hT = A.alloc([128, 16, NT], BF16)
        stg = [A.alloc([128, NT], BF16) for _ in range(2)]
        rl = [A.alloc([128, 512], BF16) for _ in range(2)]
        dm('sp', hT, h_scr, 'hT', [('h_scr', q) for q in range(NQ)], ['hT'])
        blocks = [[(w_ff1[l, j], 16)] for j in range(64)]

        def epi(j, ti, t0, tn, bk, bkey):
            s = stg[j % 2]
            sk = ('stg', j % 2, ti)
            r = rl[ti % 2]
            act(r[:, :tn], bk[:, :tn], AF.Relu, [bkey], [('rl', ti % 2)])
            tt('pool', s[:, t0:t0 + tn], r[:, :tn], r[:, :tn], ALU.mult, [('rl', ti % 2)], [sk])
            if ti == len(TILES_ALL) - 1:
                dm('sp', hid_scr[j], s, f'hst{j % 2}', [('stg', j % 2, t) for t in range(len(TILES_ALL))], [('hid', j)])
        gemm(blocks, lambda kg, t0, tn: hT[:, kg, t0:t0 + tn], ['hT'], TILES_ALL, epi)

    def stage_ff2(l, last):
        A.reset()
        hid = A.alloc([128, 64, QW], BF16)
        ffo = A.alloc([128, 16, QW], F32)
        xq = A.alloc([128, 16, QW], F32)
        hq = A.alloc([128, 16, QW], BF16)
        sq = A.alloc([128, 2, QW], BF16)
        rs = A.alloc([128, QW], F32)
        def load_hid(q):
            cq_ = slice(q * QW, q * QW + QW)
            for part in range(4):
                dm('sp', hid[:, part * 16:(part + 1) * 16, :], hid_scr[part * 16:(part + 1) * 16, :, cq_].rearrange("b p n -> p b n"), f'hid{part}',
                   [('hid', j) for j in range(part * 16, part * 16 + 16)], [('hidq', part)])
        load_hid(0)
        xck = lambda c: ('xqc', c)
        xall = ['xq'] + [('xqc', c) for c in range(16)]
        for q in range(NQ):
            c0 = q * QW
            cq = slice(c0, c0 + QW)
            dm('sp', xq, x_scr[:, :, cq], 'xq', [('x_scr', q)], ['xq'])
            blocks = [[(w_ff2[l, j, kg], 16) for kg in range(4)] for j in range(16)]

            def epi(j, ti, t0, tn, bk, bkey):
                cp('act' if ti == 0 else 'dve', ffo[:, j, t0:t0 + tn], bk[:, :tn], [bkey], ['ffo'])
            gemm(blocks, lambda kg, t0, tn: hid[:, kg, t0:t0 + tn], [('hidq', p_) for p_ in range(4)], TILES_Q, epi)
            if q + 1 < NQ:
                load_hid(q + 1)
            rstd_of(ffo, 'ffo', QW, sq, rs, 'n3')
            for c in range(16):
                stt(ffo[:, c, :], ffo[:, c, :], gn[:, l, 3, c:c + 1], rs, ALU.mult, ALU.mult, ['ffo', 'n3rs', 'gn'], [('ffoc', c)])
                tt('pool' if c % 2 == 0 else 'dve', xq[:, c, :], xq[:, c, :], ffo[:, c, :], ALU.add, ['ffo', ('ffoc', c), 'xq'], [xck(c)])
            if last:
                dm('sp', yT[:, :, cq], xq, 'yout', xall, [])
            else:
                dm('sp', x_scr[:, :, cq], xq, 'xst', xall, [('x_scr', q)])
                rstd_of(xq, 'xq', QW, sq, rs, 'n4', ckey=xck)
                norm_to(hq, 'hq', xq, 'xq', QW, gn[:, l + 1, 0, :], rs, 'n4', ckey=xck)
                dm('sp', h_scr[:, :, cq], hq, 'hst', ['hq'], [('h_scr', q)])

    stage_prenorm0()
    for l in range(L):
        P.barrier(); stage_win(l)
        P.barrier(); stage_s5(l)
        P.barrier(); stage_gla(l)
        P.barrier(); stage_swa(l, l == 0)
        P.barrier(); stage_merge(l)
        P.barrier(); stage_ff1(l)
        P.barrier(); stage_ff2(l, l == L - 1)
    P.build()
    return nc, P


def _tile_w(W, kc):
    K, N = W.shape
    return np.ascontiguousarray(W.reshape(K // 128, 128, N // 128, 128).transpose(2, 1, 0, 3))


def _t5_bucket(d):
    if d < 16:
        return d
    v = 16 + int(np.float32(np.log(np.float32(d) / np.float32(16)) / np.float32(math.log(128 / 16)) * np.float32(16)))
    return min(v, 31)


def _consts():
    oh = np.zeros((33, 2 * 256 + 128 + 4), np.float32)
    for j in range(256):
        dist = j + 1
        if j <= 254 and dist < 128:
            oh[_t5_bucket(dist), j] = 1
        else:
            oh[32, j] = 1
        dist = j - 127
        if j <= 254 and dist >= 0:
            oh[_t5_bucket(dist), 256 + j] = 1
        else:
            oh[32, 256 + j] = 1
    for j in range(128):
        dist = 128 - j
        if dist < 128:
            oh[_t5_bucket(dist), 512 + j] = 1
        else:
            oh[32, 512 + j] = 1
    oh[0, 640:644] = 1
    misc = np.zeros((128, 128 * 3 + 12), np.float32)
    misc[:, 0:128] = np.eye(128)
    misc[:, 128:256] = np.eye(128)[::-1]
    misc[:, 256:384] = np.triu(np.ones((128, 128)))
    for gl in range(8):
        misc[gl * 16:(gl + 1) * 16, 384 + gl] = 1
    misc[0:4, 392:396] = np.eye(4)
    tpos = np.tile((np.arange(NT) - COL0).astype(np.float32)[None, :], (128, 1))
    rmask = np.ones((128, NT), np.float32)
    rmask[:, 0::128] = 0
    pad = np.zeros((128, 2), np.float32)
    pad[0:COL0, 1] = NEG
    return oh, misc, tpos, rmask, pad


def _prep_shared(inp, L):
    sh = {}
    w_in = inp['w_in'][:L]
    offs = np.cumsum([0, 1024, 512, 512, 1024, 16, 1024, 1024, 256, 256, 6144])
    cols = []
    cols.append(np.arange(offs[0], offs[1]))
    cols.append(np.arange(offs[1], offs[2]))
    cols.append(np.arange(offs[2], offs[3]))
    cols.append(np.arange(offs[3], offs[4]))
    a_cols = np.concatenate([np.arange(offs[4], offs[5]), -np.ones(112, np.int64)])
    cols.append(a_cols)
    cols.append(np.arange(offs[5], offs[6]))
    qc = []
    for kvp in range(2):
        for g in range(4):
            for half in range(2):
                qh = (2 * kvp + half) * 4 + g
                qc.append(offs[6] + qh * 64 + np.arange(64))
    qperm = np.concatenate(qc)
    cols.append(qperm)
    cols.append(np.arange(offs[7], offs[8]))
    cols.append(np.arange(offs[8], offs[9]))
    cols.append(np.arange(offs[9], offs[10]))
    allc = np.concatenate(cols)
    assert allc.shape[0] == NBLK_IN * 128
    wt = np.zeros((L, NBLK_IN, 128, 16, 128), np.float32)
    valid = allc >= 0
    for l in range(L):
        Wp = np.zeros((D, NBLK_IN * 128), np.float32)
        Wp[:, valid] = w_in[l][:, allc[valid]]
        wt[l] = _tile_w(Wp, 16)
    sh['w_in'] = wt
    sh['w_glu'] = np.stack([_tile_w(inp['s5_w_glu'][l], 8) for l in range(L)])
    qrow = qperm - offs[6]
    sh['w_up'] = np.stack([np.stack([_tile_w(inp['w_up_s5'][l], 8), _tile_w(inp['w_up_gla'][l], 8), _tile_w(inp['w_up_swa'][l][qrow, :], 8)]) for l in range(L)])
    sh['w_out'] = np.stack([_tile_w(inp['w_out'][l], 16) for l in range(L)])
    sh['w_ff1'] = np.stack([_tile_w(inp['w_ff1'][l], 16) for l in range(L)])
    ff2 = np.stack([_tile_w(inp['w_ff2'][l], 64) for l in range(L)])
    sh['w_ff2'] = np.ascontiguousarray(ff2.reshape(L, 16, 128, 4, 16, 128).transpose(0, 1, 3, 2, 4, 5))
    g = np.stack([inp[n][:L] for n in ('norm_pre_mix', 'norm_post_mix', 'norm_pre_ffn', 'norm_post_ffn')], axis=1)
    sh['gnorm'] = np.ascontiguousarray(g.reshape(L, 4, 16, 128).transpose(0, 3, 1, 2))
    def rep_gp(a):
        x = a.reshape(L, 8, 8, 64)
        x = np.broadcast_to(x[:, :, :, None, :], (L, 8, 8, 16, 64))
        return x.transpose(0, 2, 3, 1, 4).reshape(L, 128, 512)
    ls = np.broadcast_to(inp['s5_log_step'][:L, :, None], (L, 64, 64))
    def b_t(a):
        x = a.reshape(L, 8, 8, 64, 16)
        return x.transpose(0, 2, 4, 1, 3).reshape(L, 128, 512)
    sh['s5rep'] = np.ascontiguousarray(np.stack([rep_gp(inp['s5_lam_re'][:L]), rep_gp(inp['s5_lam_im'][:L]), rep_gp(ls), b_t(inp['s5_b_re'][:L]), b_t(inp['s5_b_im'][:L])], axis=2)).astype(np.float32)
    def cm_gp(a):
        return a.reshape(L, 32, 2, 64).transpose(0, 2, 3, 1).reshape(L, 128, 32)
    sh['s5cm'] = np.ascontiguousarray(np.stack([cm_gp(inp['s5_lam_re'][:L]), cm_gp(inp['s5_lam_im'][:L]), cm_gp(ls)], axis=2)).astype(np.float32)
    def c_t(a):
        return a.reshape(L, 32, 2, 16, 64).transpose(0, 2, 4, 1, 3).reshape(L, 128, 32, 16)
    sh['s5c'] = np.ascontiguousarray(np.stack([c_t(inp['s5_c_re'][:L]), c_t(inp['s5_c_im'][:L])], axis=2)).astype(np.float32)
    fp = lambda a: a.reshape(L, 8, 128).transpose(0, 2, 1)
    sh['s5d'] = np.ascontiguousarray(np.stack([fp(inp['s5_d'][:L]), fp(inp['s5_b_glu'][:L])], axis=2)).astype(np.float32)
    sh['gla_w2'] = np.ascontiguousarray(inp['gla_w_gate2'][:L])
    sh['gla_p'] = np.ascontiguousarray(np.concatenate([inp['gla_b_gate2'][:L].reshape(L, 4, 128).transpose(0, 2, 1), inp['gla_g_out'][:L].reshape(L, 2, 128).transpose(0, 2, 1)], axis=2)).astype(np.float32)
    sh['swa_sink'] = np.ascontiguousarray(np.broadcast_to(inp['swa_sinks'][:L, None, :], (L, 64, 16))).astype(np.float32)
    sh['relb'] = np.ascontiguousarray(inp['rel_bias'])
    oh, misc, tpos, rmask, pad = _consts()
    sh['c_oh'] = oh; sh['c_misc'] = misc; sh['c_tpos'] = tpos; sh['c_rmask'] = rmask; sh['c_pad'] = pad
    return sh


def _prep_core(inp, L, c):
    b = c % 4
    m = {}
    xt = np.zeros((NT, D), np.float32)
    xt[0:NS] = inp['x_sample'][NS * c:NS * c + NS, 0, :]
    xt[COL0:COL0 + NMETA] = inp['meta_tokens']
    xt[128:] = inp['x_prompt'][b]
    m['xT'] = np.ascontiguousarray(xt.T.reshape(16, 128, NT).transpose(1, 0, 2))
    ns = slice(NS * c, NS * c + NS)
    def x0cm(a):
        return a.reshape(L, NS, 32, 2, 64).transpose(0, 3, 4, 2, 1).reshape(L, 128, 32, NS)
    m['s5x0'] = np.ascontiguousarray(np.stack([x0cm(inp['state_s5_re'][:L, ns]), x0cm(inp['state_s5_im'][:L, ns])], axis=2)).astype(np.float32)
    m['gla_s0'] = np.ascontiguousarray(inp['state_gla'][:L, ns])
    ck = inp['cache_swa_k'][:L, ns]
    m['swa_kT'] = np.ascontiguousarray(ck.reshape(L, NS, 128, 2, 2, 64).transpose(0, 4, 5, 1, 3, 2).reshape(L, 128, NS, 2, 128))
    m['swa_kn'] = np.ascontiguousarray(ck.reshape(L, NS, 128, 256))
    m['swa_vn'] = np.ascontiguousarray(inp['cache_swa_v'][:L, ns].reshape(L, NS, 128, 256))
    return m


_CACHE = {}


def run(inputs, L=4, debug=False):
    inp = {k: np.asarray(v) for k, v in inputs.items()}
    key = (L, debug)
    if key not in _CACHE:
        _CACHE[key] = build_program(L, debug)
    nc, P = _CACHE[key]
    sh = _prep_shared(inp, L)
    in_maps = []
    for c in range(8):
        m = dict(sh)
        m.update(_prep_core(inp, L, c))
        in_maps.append(m)
    res = run_bass_kernel_spmd(nc, in_maps, core_ids=list(range(8)))
    return res.results


def assemble(R, L):
    B, NSAMP = 4, 32
    y_prompt = np.zeros((B, SEQ, D), np.float32)
    y_sample = np.zeros((NSAMP, 1, D), np.float32)
    p_s5r = np.zeros((L, B, 64, 64), np.float32); p_s5i = np.zeros_like(p_s5r)
    p_gla = np.zeros((L, B, 4, 128, 256), np.float32)
    p_k = np.zeros((L, B, 128, 4, 64), np.float32); p_v = np.zeros_like(p_k)
    s_s5r = np.zeros((L, NSAMP, 64, 64), np.float32); s_s5i = np.zeros_like(s_s5r)
    s_gla = np.zeros((L, NSAMP, 4, 128, 256), np.float32)
    s_k = np.zeros((L, NSAMP, 128, 4, 64), np.float32); s_v = np.zeros_like(s_k)
    for c in range(8):
        r = R[c]
        yt = r['yT'].transpose(1, 0, 2).reshape(D, NT).T
        ns = slice(NS * c, NS * c + NS)
        y_sample[ns, 0, :] = yt[0:NS]
        ss5 = r['o_ss5']
        v = ss5.reshape(L, 2, 64, 2, 32, NS).transpose(3, 0, 5, 4, 1, 2).reshape(2, L, NS, 64, 64)
        s_s5r[:, ns] = v[0]; s_s5i[:, ns] = v[1]
        s_gla[:, ns] = r['o_sgla']
        s_k[:, ns] = r['o_sk'].reshape(L, NS, 128, 4, 64)
        s_v[:, ns] = r['o_sv'].reshape(L, NS, 128, 4, 64)
        if c < 4:
            b = c
            y_prompt[b] = yt[128:]
            ps = r['o_ps5']
            v = ps.reshape(L, 2, 64, 2, 32).transpose(3, 0, 4, 1, 2).reshape(2, L, 64, 64)
            p_s5r[:, b] = v[0]; p_s5i[:, b] = v[1]
            p_gla[:, b] = r['o_pgla']
            p_k[:, b] = r['o_pk'].reshape(L, 128, 4, 64)
            p_v[:, b] = r['o_pv'].reshape(L, 128, 4, 64)
    return (y_prompt, y_sample, p_s5r, p_s5i, p_gla, p_k, p_v, s_s5r, s_s5i, s_gla, s_k, s_v)


def kernel(**inputs):
    R = run(inputs, L=4)
    return assemble(R, 4)
```

```python
import math
import numpy as np
from contextlib import ExitStack
import concourse.bass as bass
import concourse.mybir as mybir
from concourse.bass_utils import run_bass_kernel_spmd

F32 = mybir.dt.float32
BF16 = mybir.dt.bfloat16
AF = mybir.ActivationFunctionType
ALU = mybir.AluOpType

D = 2048; NT = 2176; QW = 544; NQ = 4; SEQ = 2048; NMETA = 16; COL0 = 112
NS = 4
NBLK_IN = 93
EPS = 1e-6
PI = math.pi
ENG = ['pe', 'act', 'dve', 'pool', 'sp']
TILES_ALL = [(0, 512), (512, 512), (1024, 512), (1536, 512), (2048, 128)]
TILES_Q = [(0, 512), (512, 32)]
NEG = -30000.0


class Prog:
    def __init__(self, nc):
        self.nc = nc
        self.q = {e: [] for e in ENG}
        self.cnt = {e: 0 for e in ENG}
        self.seen = {e: {} for e in ENG}
        self.keys = {}
        self.dtot = {}
        self.es = ExitStack()
        self.nops = 0
        self.banks = None
        self.bi = 0

    def sb(self, name, shape, dt):
        return self.es.enter_context(self.nc.sbuf_tensor(name, list(shape), dt))

    def mkbanks(self):
        self.banks = [self.es.enter_context(self.nc.psum_tensor(f"psb{i}", [128, 512], F32)) for i in range(8)]

    def bank(self):
        i = self.bi % 8
        self.bi += 1
        return self.banks[i], f"ps{i}"

    def _deps(self, reads, writes):
        toks = []
        for k in reads:
            st = self.keys.get(k)
            if st is not None and st[0] is not None:
                toks.append(st[0])
        for k in writes:
            st = self.keys.get(k)
            if st is not None:
                if st[0] is not None:
                    toks.append(st[0])
                toks.extend(st[1].items())
        return toks

    def _commit(self, tok, reads, writes):
        for k in reads:
            st = self.keys.get(k)
            if st is None:
                st = self.keys[k] = [None, {}]
            if st[1].get(tok[0], 0) < tok[1]:
                st[1][tok[0]] = tok[1]
        for k in writes:
            self.keys[k] = [tok, {}]

    def _mkwaits(self, eng, toks):
        best = {}
        for s, v in toks:
            if s == eng and eng == 'pe':
                continue
            if best.get(s, 0) < v:
                best[s] = v
        seen = self.seen[eng]
        out = []
        for s, v in best.items():
            if seen.get(s, 0) >= v:
                continue
            seen[s] = v
            out.append((s, v))
        return out

    def op(self, eng, fn, reads=(), writes=(), inc=True):
        waits = self._mkwaits(eng, self._deps(reads, writes))
        if inc:
            self.cnt[eng] += 1
            tok = (eng, self.cnt[eng])
        else:
            tok = (eng, self.cnt[eng] + 1)
        self.q[eng].append((waits, fn, eng if inc else None, 1))
        self._commit(tok, reads, writes)
        self.nops += 1

    def dma(self, eng, fn, chan, reads=(), writes=()):
        waits = self._mkwaits(eng, self._deps(reads, writes))
        self.dtot[chan] = self.dtot.get(chan, 0) + 16
        tok = (chan, self.dtot[chan])
        self.q[eng].append((waits, fn, chan, 16))
        self._commit(tok, reads, writes)
        self.nops += 1

    def barrier(self):
        toks = [(e, self.cnt[e]) for e in ENG if self.cnt[e] > 0] + list(self.dtot.items())
        for e in ENG:
            waits = self._mkwaits(e, toks)
            if waits:
                self.q[e].append((waits, None, None, 0))

    def build(self):
        nc = self.nc
        names = sorted(set(ENG) | set(self.dtot.keys()))
        sems = {n: self.es.enter_context(nc.semaphore("s_" + n)) for n in names}
        fin = list(self.dtot.items()) + [(e, self.cnt[e]) for e in ENG if e != 'sp' and self.cnt[e] > 0]
        q = self.q
        block = self.es.enter_context(nc.Block())
        emap = {'pe': block.tensor, 'act': block.scalar, 'dve': block.vector, 'pool': block.gpsimd, 'sp': block.sync}

        def mk(engname):
            def body(e):
                for waits, fn, s, n in q[engname]:
                    for (ws, wv) in waits:
                        e.wait_ge(sems[ws], wv)
                    if fn is None:
                        continue
                    ins = fn(e)
                    if s is not None:
                        ins.then_inc(sems[s], n)
                if engname == 'sp':
                    for (ws, wv) in fin:
                        e.wait_ge(sems[ws], wv)
            return body
        for engname in ENG:
            emap[engname](mk(engname))
        self.es.close()


class Arena:
    def __init__(self, P, words):
        self.t = P.sb("arena", [128, words], F32)
        self.words = words
        self.off = 0
        self.gen = 0

    def reset(self):
        self.off = 0
        self.gen += 1

    def alloc(self, shape, dt):
        n = 1
        for s in shape[1:]:
            n *= s
        w = n if dt == F32 else (n + 1) // 2
        w = (w + 15) // 16 * 16
        assert self.off + w <= self.words, (self.off, w, self.words)
        v = self.t[:, self.off:self.off + w]
        self.off += w
        if dt != F32:
            v = v.bitcast(dt)
        v = v[:, :n]
        if len(shape) == 3:
            v = v.rearrange("p (a b) -> p a b", a=shape[1])
        elif len(shape) == 4:
            v = v.rearrange("p (a b c) -> p a b c", a=shape[1], b=shape[2])
        elif len(shape) == 5:
            v = v.rearrange("p (a b c d) -> p a b c d", a=shape[1], b=shape[2], c=shape[3])
        return v


def build_program(L, debug=False):
    nc = bass.Bass("TRN2", target_bir_lowering=False)
    P = Prog(nc)
    A = Arena(P, 46000)
    P.mkbanks()

    def din(name, shape):
        return nc.dram_tensor(name, list(shape), F32, kind="ExternalInput").ap()

    def dout(name, shape):
        return nc.dram_tensor(name, list(shape), F32, kind="ExternalOutput").ap()

    def dscr(name, shape, dt):
        return nc.dram_tensor(name, list(shape), dt, kind=("ExternalOutput" if debug else "Internal")).ap()

    xT = din("xT", [128, 16, NT])
    w_in = din("w_in", [L, NBLK_IN, 128, 16, 128])
    w_glu = din("w_glu", [L, 8, 128, 8, 128])
    w_up = din("w_up", [L, 3, 16, 128, 8, 128])
    w_out = din("w_out", [L, 16, 128, 16, 128])
    w_ff1 = din("w_ff1", [L, 64, 128, 16, 128])
    w_ff2 = din("w_ff2", [L, 16, 4, 128, 16, 128])
    gnorm = din("gnorm", [L, 128, 4, 16])
    s5rep = din("s5rep", [L, 128, 5, 512])
    s5cm = din("s5cm", [L, 128, 3, 32])
    s5c = din("s5c", [L, 128, 2, 32, 16])
    s5d = din("s5d", [L, 128, 2, 8])
    s5x0 = din("s5x0", [L, 128, 2, 32, NS])
    gla_w2 = din("gla_w2", [L, 16, 512])
    gla_p = din("gla_p", [L, 128, 6])
    gla_s0 = din("gla_s0", [L, NS, 4, 128, 256])
    swa_sink = din("swa_sink", [L, 64, 16])
    swa_kT = din("swa_kT", [L, 128, NS, 2, 128])
    swa_kn = din("swa_kn", [L, NS, 128, 256])
    swa_vn = din("swa_vn", [L, NS, 128, 256])
    relb = din("relb", [32, 16])
    c_oh = din("c_oh", [33, 2 * 256 + 128 + 4])
    c_misc = din("c_misc", [128, 128 * 3 + 8 + 4])
    c_tpos = din("c_tpos", [128, NT])
    c_rmask = din("c_rmask", [128, NT])
    c_pad = din("c_pad", [128, 2])
    yT = dout("yT", [128, 16, NT])
    o_ps5 = dout("o_ps5", [L, 128, 2, 32])
    o_pgla = dout("o_pgla", [L, 4, 128, 256])
    o_pk = dout("o_pk", [L, 128, 256])
    o_pv = dout("o_pv", [L, 128, 256])
    o_ss5 = dout("o_ss5", [L, 128, 2, 32, NS])
    o_sgla = dout("o_sgla", [L, NS, 4, 128, 256])
    o_sk = dout("o_sk", [L, NS, 128, 256])
    o_sv = dout("o_sv", [L, NS, 128, 256])
    x_scr = dscr("x_scr", [128, 16, NT], F32)
    h_scr = dscr("h_scr", [128, 16, NT], BF16)
    z_scr = dscr("z_scr", [NBLK_IN, 128, NT], BF16)
    y_scr = dscr("y_scr", [24, 128, NT], BF16)
    hid_scr = dscr("hid_scr", [64, 128, NT], BF16)
    e_scr = dscr("e_scr", [2, 16, 256], F32)

    NWB = 4
    wbufs = [P.sb(f"wb{i}", [128, 16, 128], BF16) for i in range(NWB)]
    cm = P.sb("cmisc", [128, 128 * 3 + 12], F32)
    cb = P.sb("cbf", [128, 128 * 3 + 128], BF16)
    gn = P.sb("gn", [128, L, 4, 16], F32)
    ident_f = cm[:, 0:128]
    maskBD = cm[:, 384:392]
    I4 = cm[:, 392:396]
    ident_b = cb[:, 0:128]
    J_b = cb[:, 128:256]
    tri_b = cb[:, 256:384]
    ones_b = cb[:, 384:512]

    P.dma('sp', lambda e: e.dma_start(out=cm[:], in_=c_misc), 'cm', writes=['cm'])
    P.op('dve', lambda e: e.tensor_copy(out=cb[:, 0:384], in_=cm[:, 0:384]), reads=['cm'], writes=['cb'])
    P.op('dve', lambda e: e.memset(cb[:, 384:512], 1.0), writes=['cb1'])
    P.dma('sp', lambda e: e.dma_start(out=gn[:], in_=gnorm.rearrange("l p a c -> p l a c")), 'gn', writes=['gn'])

    wcnt = [0]

    def wload(src, kc):
        i = wcnt[0] % NWB
        wcnt[0] += 1
        P.dma('pool', lambda e: e.dma_start(out=wbufs[i][:, :kc, :], in_=src), f'wb{i}', writes=[f'wb{i}'])
        return wbufs[i], f'wb{i}'

    def gemm(blocks, rhs_fn, rhs_keys, tiles, epi):
        flat = [(ap, kc) for blk in blocks for (ap, kc) in blk]
        LA = NWB - 1
        loaded = {}
        for i in range(min(LA, len(flat))):
            loaded[i] = wload(*flat[i])
        idx = 0
        for j, blk in enumerate(blocks):
            bks = [P.bank() for _ in tiles]
            nk = sum(kc for _, kc in blk)
            kd = 0
            for (ap, kc) in blk:
                if idx + LA < len(flat):
                    loaded[idx + LA] = wload(*flat[idx + LA])
                wb, wk = loaded.pop(idx)
                idx += 1
                for ti, (t0, tn) in enumerate(tiles):
                    bk, bkey = bks[ti]
                    for k in range(kc):
                        kg = kd + k
                        P.op('pe', lambda e, bk=bk, wb=wb, k=k, kg=kg, t0=t0, tn=tn, nk=nk:
                             e.matmul(bk[:, :tn], lhsT=wb[:, k, :], rhs=rhs_fn(kg, t0, tn), start=(kg == 0), stop=(kg == nk - 1)),
                             reads=[wk] + rhs_keys, writes=[bkey], inc=(k == kc - 1))
                kd += kc
            for ti, (t0, tn) in enumerate(tiles):
                epi(j, ti, t0, tn, bks[ti][0], bks[ti][1])

    def act(out, in_, func, r, w, **kw):
        P.op('act', lambda e: e.activation(out=out, in_=in_, func=func, **kw), reads=r, writes=w)

    def tt(eng, out, a, b, op, r, w):
        P.op(eng, lambda e: e.tensor_tensor(out=out, in0=a, in1=b, op=op), reads=r, writes=w)

    def ts(eng, out, a, s1, s2, op0, op1, r, w):
        if op1 is None:
            P.op(eng, lambda e: e.tensor_scalar(out=out, in0=a, scalar1=s1, scalar2=None, op0=op0), reads=r, writes=w)
        else:
            P.op(eng, lambda e: e.tensor_scalar(out=out, in0=a, scalar1=s1, scalar2=s2, op0=op0, op1=op1), reads=r, writes=w)

    def stt(out, a, s, b, op0, op1, r, w):
        P.op('dve', lambda e: e.scalar_tensor_tensor(out=out, in0=a, scalar=s, in1=b, op0=op0, op1=op1), reads=r, writes=w)

    def cp(eng, out, in_, r, w):
        if eng == 'act':
            act(out, in_, AF.Copy, r, w)
        else:
            P.op(eng, lambda e: e.tensor_copy(out=out, in_=in_), reads=r, writes=w)

    def mm(out, lhsT, rhs, start, stop, r, w, inc=True):
        P.op('pe', lambda e: e.matmul(out, lhsT=lhsT, rhs=rhs, start=start, stop=stop), reads=r, writes=w, inc=inc)

    def tr(out, in_, ident, r, w):
        P.op('pe', lambda e: e.transpose(out, in_, ident), reads=r, writes=w)

    def dm(eng, out, in_, chan, r, w):
        P.dma(eng, lambda e: e.dma_start(out=out, in_=in_), chan, reads=r, writes=w)

    def memset(eng, ap, val, w):
        P.op(eng, lambda e: e.memset(ap, val), writes=w)

    def rstd_of(src, skey, n, sq, rstd, tag, ckey=None):
        tl = [(0, min(512, n))] + ([(512, n - 512)] if n > 512 else [])
        for (t0, tn) in tl:
            bk, bkey = P.bank()
            for c in range(16):
                sk = tag + 'sq' + str(c % 2)
                rk = [skey] + ([ckey(c)] if ckey else [])
                act(sq[:, c % 2, t0:t0 + tn], src[:, c, t0:t0 + tn], AF.Square, rk, [sk])
                mm(bk[:, :tn], ones_b, sq[:, c % 2, t0:t0 + tn], c == 0, c == 15, [sk, 'cb1'], [bkey])
            ts('dve', rstd[:, t0:t0 + tn], bk[:, :tn], 1.0 / D, EPS, ALU.mult, ALU.add, [bkey], [tag + 'rs'])
        act(rstd[:, :n], rstd[:, :n], AF.Ln, [tag + 'rs'], [tag + 'rs'])
        act(rstd[:, :n], rstd[:, :n], AF.Exp, [tag + 'rs'], [tag + 'rs'], scale=-0.5)

    def norm_to(out, okey, src, skey, n, gcols, rstd, tag, ckey=None):
        for c in range(16):
            rk = [skey, tag + 'rs', 'gn'] + ([ckey(c)] if ckey else [])
            stt(out[:, c, :n], src[:, c, :n], gcols[:, c:c + 1], rstd[:, :n], ALU.mult, ALU.mult, rk, [okey])

    def stage_prenorm0():
        A.reset()
        xq = A.alloc([128, 16, QW], F32)
        hq = A.alloc([128, 16, QW], BF16)
        sq = A.alloc([128, 2, QW], BF16)
        rs = A.alloc([128, QW], F32)
        for q in range(NQ):
            c0 = q * QW
            dm('sp', xq, xT[:, :, c0:c0 + QW], 'xq', [], ['xq'])
            dm('sp', x_scr[:, :, c0:c0 + QW], xq, 'xst', ['xq'], [('x_scr', q)])
            rstd_of(xq, 'xq', QW, sq, rs, 'n0')
            norm_to(hq, 'hq', xq, 'xq', QW, gn[:, 0, 0, :], rs, 'n0')
            dm('sp', h_scr[:, :, c0:c0 + QW], hq, 'hst', ['hq'], [('h_scr', q)])

    def stage_win(l):
        A.reset()
        hT = A.alloc([128, 16, NT], BF16)
        stg = [A.alloc([128, NT], BF16) for _ in range(2)]
        dm('sp', hT, h_scr, 'hT', [('h_scr', q) for q in range(NQ)], ['hT'])
        blocks = [[(w_in[l, j], 16)] for j in range(NBLK_IN)]

        def epi(j, ti, t0, tn, bk, bkey):
            s = stg[j % 2]
            sk = ('stg', j % 2, ti)
            if j >= 45:
                act(s[:, t0:t0 + tn], bk[:, :tn], AF.Sigmoid, [bkey], [sk])
            elif 25 <= j < 33:
                act(s[:, t0:t0 + tn], bk[:, :tn], AF.Silu, [bkey], [sk])
            else:
                cp('dve' if ti % 2 == 0 else 'act', s[:, t0:t0 + tn], bk[:, :tn], [bkey], [sk])
            if ti == len(TILES_ALL) - 1:
                dm('sp', z_scr[j], s, f'zst{j % 2}', [('stg', j % 2, t) for t in range(len(TILES_ALL))], [('z', j)])
        gemm(blocks, lambda kg, t0, tn: hT[:, kg, t0:t0 + tn], ['hT'], TILES_ALL, epi)

    def stage_s5(l):
        A.reset()
        ub = A.alloc([128, 8, NT], BF16)
        BD = A.alloc([128, 8, 4, 2, 128], BF16)
        CD = A.alloc([128, 8, 4, 2, 128], BF16)
        uni = A.alloc([128, 6656], F32)
        rep = uni[:, 0:2560].rearrange("p (a b) -> p a b", a=5)
        wk = uni[:, 2560:6656].rearrange("p (a b) -> p a b", a=8)
        cmp_ = A.alloc([128, 3, 32], F32)
        cw = A.alloc([128, 12, 32], F32)
        ct = A.alloc([128, 2, 32, 16], F32)
        dd = A.alloc([128, 2, 8], F32)
        x0 = A.alloc([128, 2, 32, NS], F32)
        x1 = A.alloc([128, 2, 32, NS], F32)
        pst = A.alloc([128, 2, 32], F32)
        tpos = A.alloc([128, 512], F32)
        tcs = [A.alloc([128, 512], F32) for _ in range(2)]
        tss = [A.alloc([128, 512], F32) for _ in range(2)]
        targ = A.alloc([128, 512], F32)
        tki = A.alloc([128, 512], F32).bitcast(mybir.dt.int32)
        vini = A.alloc([128, 4], F32)
        magt = A.alloc([128, 512], F32)
        t4 = [A.alloc([128, 512], F32) for _ in range(4)]
        et = [A.alloc([128, 512], F32) for _ in range(2)]
        vv = [[A.alloc([128, 512], F32) for _ in range(2)] for _ in range(2)]
        xx = [[A.alloc([128, 512], BF16) for _ in range(2)] for _ in range(2)]
        pt = [A.alloc([128, 512], F32) for _ in range(2)]
        yacc = A.alloc([128, NT], F32)
        g1 = A.alloc([128, 512], F32)
        g2 = A.alloc([128, 512], F32)
        ystg = [A.alloc([128, NT], BF16) for _ in range(2)]
        dtmp = A.alloc([128, 8], F32)

        dm('sp', ub, z_scr[0:8].rearrange("b p n -> p b n"), 'ub', [('z', j) for j in range(8)], ['ub'])
        dm('sp', rep, s5rep[l], 'rep', [], ['rep'])
        dm('sp', cmp_, s5cm[l], 'cmp', [], ['cmp'])
        dm('sp', ct, s5c[l], 'ct', [], ['ct'])
        dm('sp', dd, s5d[l], 'dd', [], ['dd'])
        dm('sp', x0, s5x0[l], 'x0', [], ['x0'])
        dm('sp', tpos, c_tpos[:, COL0:COL0 + 512], 'tpos', [], ['tpos'])
        memset('pool', BD.rearrange("p a b c d -> p (a b c d)"), 0.0, ['BD'])
        memset('pool', CD.rearrange("p a b c d -> p (a b c d)"), 0.0, ['CD'])
        for par in range(2):
            for ri in range(2):
                memset('dve', xx[par][ri], 0.0, [('xx', par)])

        def sincos(ang, sn, cs, tmp, tmpi, key):
            K = [key]
            ts('dve', tmpi, ang, 1.0 / (2 * PI), None, ALU.mult, None, K, K)
            stt(tmp, tmpi, -2 * PI, ang, ALU.mult, ALU.add, K, K)
            ts('dve', tmp, tmp, PI, -PI, ALU.min, ALU.max, K, K)
            act(sn, tmp, AF.Sin, K, K)
            ts('dve', tmpi, ang, 0.5 * PI, 1.0 / (2 * PI), ALU.add, ALU.mult, K, K)
            stt(tmp, tmpi, -2 * PI, ang, ALU.mult, ALU.add, K, K)
            ts('dve', tmp, tmp, 0.5 * PI, PI, ALU.add, ALU.min, K, K)
            ts('dve', tmp, tmp, -PI, None, ALU.max, None, K, K)
            act(cs, tmp, AF.Sin, K, K)

        lr, li, ls, br, bi = (rep[:, i, :] for i in range(5))
        w = [wk[:, i, :] for i in range(8)]
        R = ['rep', 'wk']
        act(w[0], ls, AF.Exp, ['rep'], ['wk'])
        tt('dve', w[1], lr, w[0], ALU.mult, R, ['wk'])
        act(w[1], w[1], AF.Exp, ['wk'], ['wk'])
        tt('dve', w[2], li, w[0], ALU.mult, R, ['wk'])
        sincos(w[2], w[3], w[4], w[5], w[6].bitcast(mybir.dt.int32), 'wk')
        tt('dve', w[3], w[3], w[1], ALU.mult, R, ['wk'])
        tt('dve', w[4], w[4], w[1], ALU.mult, R, ['wk'])
        ts('dve', w[4], w[4], -1.0, None, ALU.add, None, R, ['wk'])
        tt('dve', w[5], lr, lr, ALU.mult, R, ['wk'])
        tt('dve', w[6], li, li, ALU.mult, R, ['wk'])
        tt('dve', w[5], w[5], w[6], ALU.add, R, ['wk'])
        P.op('dve', lambda e: e.reciprocal(out=w[5], in_=w[5]), reads=R, writes=['wk'])
        tt('dve', w[6], w[4], lr, ALU.mult, R, ['wk'])
        tt('dve', w[7], w[3], li, ALU.mult, R, ['wk'])
        tt('dve', w[6], w[6], w[7], ALU.add, R, ['wk'])
        tt('dve', w[6], w[6], w[5], ALU.mult, R, ['wk'])
        tt('dve', w[7], w[3], lr, ALU.mult, R, ['wk'])
        tt('dve', w[0], w[4], li, ALU.mult, R, ['wk'])
        tt('dve', w[7], w[7], w[0], ALU.subtract, R, ['wk'])
        tt('dve', w[7], w[7], w[5], ALU.mult, R, ['wk'])
        tt('dve', w[0], w[6], br, ALU.mult, R, ['wk'])
        tt('dve', w[1], w[7], bi, ALU.mult, R, ['wk'])
        tt('dve', w[0], w[0], w[1], ALU.subtract, R, ['wk'])
        tt('dve', w[1], w[6], bi, ALU.mult, R, ['wk'])
        tt('dve', w[2], w[7], br, ALU.mult, R, ['wk'])
        tt('dve', w[1], w[1], w[2], ALU.add, R, ['wk'])
        for ri in range(2):
            src = w[ri].rearrange("p (a b) -> p a b", a=8)
            for ccl in range(4):
                for g2_ in range(2):
                    ts('dve', BD[:, :, ccl, ri, g2_ * 64:(g2_ + 1) * 64], src, maskBD[:, ccl * 2 + g2_:ccl * 2 + g2_ + 1], None,
                       ALU.mult, None, ['wk', 'cm', 'BD'], ['BD'])
        c = [cw[:, i, :] for i in range(12)]
        Rc = ['cmp', 'cw']
        act(c[0], cmp_[:, 2, :], AF.Exp, ['cmp'], ['cw'])
        tt('dve', c[1], cmp_[:, 0, :], c[0], ALU.mult, Rc, ['cw'])
        act(c[1], c[1], AF.Exp, ['cw'], ['cw'])
        tt('dve', c[2], cmp_[:, 1, :], c[0], ALU.mult, Rc, ['cw'])
        sincos(c[2], c[3], c[4], c[5], c[6].bitcast(mybir.dt.int32), 'cw')
        tt('dve', c[3], c[3], c[1], ALU.mult, Rc, ['cw'])
        tt('dve', c[4], c[4], c[1], ALU.mult, Rc, ['cw'])
        mag_c, th_c, ai_c, ar_c = c[1], c[2], c[3], c[4]
        c = [cw[:, i, :] for i in range(12)]
        ts('dve', c[7], th_c, 512.0, None, ALU.mult, None, Rc, ['cw'])
        sincos(c[7], c[8], c[9], c[5], c[6].bitcast(mybir.dt.int32), 'cw')
        s512_c, c512_c = c[8], c[9]
        for ri in range(2):
            c5 = ct[:, ri, :, :].rearrange("p (a b) h -> p a b h", a=8)
            for ccl in range(4):
                for g2_ in range(2):
                    ps_ = slice(g2_ * 64, (g2_ + 1) * 64)
                    ts('dve', CD[ps_, :, ccl, ri, 32 * ccl + 16 * g2_:32 * ccl + 16 * g2_ + 16], c5[ps_, :, ccl, :],
                       (1.0 if ri == 0 else -1.0), None, ALU.mult, None, ['ct', 'CD'], ['CD'])

        P.barrier()
        pending = []
        for fc in range(8):
            for ccl in range(4):
                cc = fc * 4 + ccl
                thc = th_c[:, cc:cc + 1]
                nv = NT - COL0
                tp_ = cc % 2
                tc_, tsn = tcs[tp_], tss[tp_]
                tck, tsk = ('tc', tp_), ('tsn', tp_)
                ts('dve', targ, tpos, thc, None, ALU.mult, None, ['tpos', 'cw', 'targ'], ['targ'])
                ts('dve', tki, targ, 1.0 / (2 * PI), None, ALU.mult, None, ['targ', 'tki'], ['tki'])
                stt(tsn, tki, -2 * PI, targ, ALU.mult, ALU.add, ['tki', 'targ', tsk], [tsk])
                ts('dve', tsn, tsn, PI, -PI, ALU.min, ALU.max, [tsk], [tsk])
                act(tsn, tsn, AF.Sin, [tsk], [tsk])
                ts('dve', tki, targ, 0.5 * PI, 1.0 / (2 * PI), ALU.add, ALU.mult, ['targ', 'tki'], ['tki'])
                stt(tc_, tki, -2 * PI, targ, ALU.mult, ALU.add, ['tki', 'targ', tck], [tck])
                ts('dve', tc_, tc_, 0.5 * PI, PI, ALU.add, ALU.min, [tck], [tck])
                ts('dve', tc_, tc_, -PI, None, ALU.max, None, [tck], [tck])
                act(tc_, tc_, AF.Sin, [tck], [tck])
                ts('dve', magt, tpos, 0.0, mag_c[:, cc:cc + 1], ALU.mult, ALU.add, ['tpos', 'cw', 'magt'], ['magt'])
                for ti, (t0, tn) in enumerate(TILES_ALL):
                    par = ti % 2
                    be, bek = P.bank()
                    bi_, bik = P.bank()
                    mm(be[:, :tn], BD[:, fc, ccl, 0, :], ub[:, fc, t0:t0 + tn], True, True, ['BD', 'ub'], [bek])
                    mm(bi_[:, :tn], BD[:, fc, ccl, 1, :], ub[:, fc, t0:t0 + tn], True, True, ['BD', 'ub'], [bik])
                    cs_ = tc_[:, 0:tn]
                    sn_ = tsn[:, 0:tn]
                    er, ei = be[:, :tn], bi_[:, :tn]
                    tt('dve', t4[0][:, :tn], er, cs_, ALU.mult, [bek, tck], ['t40'])
                    tt('dve', t4[1][:, :tn], ei, sn_, ALU.mult, [bik, tsk], ['t41'])
                    tt('dve', et[0][:, :tn], t4[0][:, :tn], t4[1][:, :tn], ALU.add, ['t40', 't41'], ['et0'])
                    tt('dve', t4[2][:, :tn], ei, cs_, ALU.mult, [bik, tck], ['t42'])
                    tt('dve', t4[3][:, :tn], er, sn_, ALU.mult, [bek, tsk], ['t43'])
                    tt('dve', et[1][:, :tn], t4[2][:, :tn], t4[3][:, :tn], ALU.subtract, ['t42', 't43'], ['et1'])
                    if ti == 0:
                        memset('dve', et[0][:, 0:COL0], 0.0, ['et0'])
                        memset('dve', et[1][:, 0:COL0], 0.0, ['et1'])
                    else:
                        vrL, viL = vv[1 - par][0][:, 511:512], vv[1 - par][1][:, 511:512]
                        s5c_, c5c_ = s512_c[:, cc:cc + 1], c512_c[:, cc:cc + 1]
                        RK = [('vv', 1 - par, 0), ('vv', 1 - par, 1), 'cw', 'vini']
                        ts('dve', vini[:, 2:3], viL, s5c_, None, ALU.mult, None, RK, ['vini'])
                        stt(vini[:, 0:1], vrL, c5c_, vini[:, 2:3], ALU.mult, ALU.subtract, RK, ['vini'])
                        ts('dve', vini[:, 3:4], vrL, s5c_, None, ALU.mult, None, RK, ['vini'])
                        stt(vini[:, 1:2], viL, c5c_, vini[:, 3:4], ALU.mult, ALU.add, RK, ['vini'])
                    for ri in range(2):
                        vcur = vv[par][ri]
                        if ti == 0:
                            P.op('dve', lambda e, vcur=vcur, ri=ri, tn=tn: e.tensor_tensor_scan(out=vcur[:, :tn], data0=magt[:, :tn], data1=et[ri][:, :tn], initial=0.0, op0=ALU.mult, op1=ALU.add),
                                 reads=['magt', f'et{ri}'], writes=[('vv', par, ri)])
                        else:
                            vprev = vv[1 - par][ri]
                            P.op('dve', lambda e, vcur=vcur, vprev=vprev, ri=ri, tn=tn: e.tensor_tensor_scan(out=vcur[:, :tn], data0=magt[:, :tn], data1=et[ri][:, :tn], initial=vini[:, ri:ri + 1], op0=ALU.mult, op1=ALU.add),
                                 reads=['magt', f'et{ri}', 'vini'], writes=[('vv', par, ri)])
                    while pending:
                        pending.pop(0)()
                    vr, vi = vv[par][0][:, :tn], vv[par][1][:, :tn]
                    xr, xi = xx[par][0], xx[par][1]
                    lo = 0
                    if ti == 0:
                        lo = COL0
                    xk = ('xx', par)
                    tt('pool', pt[0][:, lo:tn], vr[:, lo:tn], cs_[:, lo:tn], ALU.mult, [('vv', par, 0), tck], ['pt0'])
                    tt('pool', pt[1][:, lo:tn], vi[:, lo:tn], sn_[:, lo:tn], ALU.mult, [('vv', par, 1), tsk], ['pt1'])
                    tt('pool', xr[:, lo:tn], pt[0][:, lo:tn], pt[1][:, lo:tn], ALU.subtract, ['pt0', 'pt1'], [xk])
                    tt('pool', pt[0][:, lo:tn], vr[:, lo:tn], sn_[:, lo:tn], ALU.mult, [('vv', par, 0), tsk, 'pt0'], ['pt0'])
                    tt('pool', pt[1][:, lo:tn], vi[:, lo:tn], cs_[:, lo:tn], ALU.mult, [('vv', par, 1), tck, 'pt1'], ['pt1'])
                    tt('pool', xi[:, lo:tn], pt[0][:, lo:tn], pt[1][:, lo:tn], ALU.add, ['pt0', 'pt1'], [xk])
                    if ti == 0:
                        arc, aic = ar_c[:, cc:cc + 1], ai_c[:, cc:cc + 1]
                        x0r, x0i = x0[:, 0, cc, :], x0[:, 1, cc, :]
                        ts('dve', dtmp[:, 0:4], x0i, aic, None, ALU.mult, None, ['x0', 'cw'], ['dtmp'])
                        stt(dtmp[:, 4:8], x0r, arc, dtmp[:, 0:4], ALU.mult, ALU.subtract, ['x0', 'cw', 'dtmp'], ['dtmp'])
                        tt('dve', x1[:, 0, cc, :], dtmp[:, 4:8], er[:, 0:NS], ALU.add, ['dtmp', bek], ['x1'])
                        ts('dve', dtmp[:, 0:4], x0r, aic, None, ALU.mult, None, ['x0', 'cw', 'dtmp'], ['dtmp'])
                        stt(dtmp[:, 4:8], x0i, arc, dtmp[:, 0:4], ALU.mult, ALU.add, ['x0', 'cw', 'dtmp'], ['dtmp'])
                        tt('dve', x1[:, 1, cc, :], dtmp[:, 4:8], ei[:, 0:NS], ALU.add, ['dtmp', bik], ['x1'])
                        cp('dve', xr[:, 0:NS], x1[:, 0, cc, :], ['x1'], [xk])
                        cp('dve', xi[:, 0:NS], x1[:, 1, cc, :], ['x1'], [xk])
                    if ti == len(TILES_ALL) - 1:
                        la = tn - 1
                        tt('dve', dtmp[:, 0:1], vr[:, la:la + 1], cs_[:, la:la + 1], ALU.mult, [('vv', par, 0), tck, 'dtmp'], ['dtmp'])
                        tt('dve', dtmp[:, 1:2], vi[:, la:la + 1], sn_[:, la:la + 1], ALU.mult, [('vv', par, 1), tsk, 'dtmp'], ['dtmp'])
                        tt('dve', pst[:, 0, cc:cc + 1], dtmp[:, 0:1], dtmp[:, 1:2], ALU.subtract, ['dtmp'], ['pst'])
                        tt('dve', dtmp[:, 2:3], vr[:, la:la + 1], sn_[:, la:la + 1], ALU.mult, [('vv', par, 0), tsk, 'dtmp'], ['dtmp'])
                        tt('dve', dtmp[:, 3:4], vi[:, la:la + 1], cs_[:, la:la + 1], ALU.mult, [('vv', par, 1), tck, 'dtmp'], ['dtmp'])
                        tt('dve', pst[:, 1, cc:cc + 1], dtmp[:, 2:3], dtmp[:, 3:4], ALU.add, ['dtmp'], ['pst'])
                    def cproj(fc=fc, ccl=ccl, ti=ti, t0=t0, tn=tn, xr=xr, xi=xi, xk=xk):
                        by, byk = P.bank()
                        mm(by[:, :tn], CD[:, fc, ccl, 0, :], xr[:, :tn], True, False, ['CD', xk], [byk], inc=False)
                        mm(by[:, :tn], CD[:, fc, ccl, 1, :], xi[:, :tn], False, True, ['CD', xk], [byk])
                        if ccl == 0:
                            cp('act', yacc[:, t0:t0 + tn], by[:, :tn], [byk], [('yacc', ti)])
                        else:
                            tt('dve', yacc[:, t0:t0 + tn], yacc[:, t0:t0 + tn], by[:, :tn], ALU.add, [byk, ('yacc', ti)], [('yacc', ti)])
                    pending.append(cproj)
            while pending:
                pending.pop(0)()
            for ti, (t0, tn) in enumerate(TILES_ALL):
                yk = ('yacc', ti)
                ysl = yacc[:, t0:t0 + tn]
                stt(ysl, ub[:, fc, t0:t0 + tn], dd[:, 0, fc:fc + 1], ysl, ALU.mult, ALU.add, ['ub', 'dd', yk], [yk])
                act(g1[:, :tn], ysl, AF.Square, [yk], ['g1'])
                ts('dve', g1[:, :tn], g1[:, :tn], 0.044715, 1.0, ALU.mult, ALU.add, ['g1'], ['g1'])
                tt('dve', g1[:, :tn], g1[:, :tn], ysl, ALU.mult, ['g1', yk], ['g1'])
                act(g2[:, :tn], g1[:, :tn], AF.Sigmoid, ['g1'], ['g2'], scale=2.0 * math.sqrt(2.0 / PI))
                tt('dve', ub[:, fc, t0:t0 + tn], ysl, g2[:, :tn], ALU.mult, [yk, 'g2'], ['ub'])
        dm('sp', o_ps5[l], pst, 'ops5', ['pst'], [])
        dm('sp', o_ss5[l], x1, 'oss5', ['x1'], [])
        blocks = [[(w_glu[l, j], 8)] for j in range(8)]

        def epi(j, ti, t0, tn, bk, bkey):
            s = ystg[j % 2]
            sk = ('ystg', j % 2, ti)
            act(g1[:, :tn], bk[:, :tn], AF.Sigmoid, [bkey, 'dd'], ['g1'], bias=dd[:, 1, j:j + 1])
            tt('dve', s[:, t0:t0 + tn], ub[:, j, t0:t0 + tn], g1[:, :tn], ALU.mult, ['ub', 'g1'], [sk])
            if ti == len(TILES_ALL) - 1:
                dm('sp', y_scr[j], s, f'yst{j % 2}', [('ystg', j % 2, t) for t in range(len(TILES_ALL))], [('y', j)])
        gemm(blocks, lambda kg, t0, tn: ub[:, kg, t0:t0 + tn], ['ub'], TILES_ALL, epi)

    def stage_gla(l):
        A.reset()
        aT = A.alloc([128, NT], BF16)
        w2 = A.alloc([128, 512], BF16)
        gp = A.alloc([128, 6], F32)
        nbg = A.alloc([128, 4], F32)
        rmask = A.alloc([128, NT], F32)
        qT = A.alloc([128, NT], BF16)
        kT = A.alloc([128, NT], BF16)
        vT = A.alloc([128, 2, NT], BF16)
        rT = A.alloc([128, 2, NT], BF16)
        G = A.alloc([128, NT], F32)
        B = A.alloc([128, NT], F32)
        E1 = A.alloc([128, NT], F32)
        E2 = A.alloc([128, NT], F32)
        qs = A.alloc([128, NT], BF16)
        ks = A.alloc([128, NT], BF16)
        ku = A.alloc([128, NT], BF16)
        cd = A.alloc([128, 17], F32)
        OT = A.alloc([128, 2, NT], F32)
        vtm = [A.alloc([128, 256], BF16) for _ in range(2)]
        kutm = [A.alloc([128, 128], BF16) for _ in range(2)]
        ATb = [A.alloc([128, 128], BF16) for _ in range(2)]
        Sf = A.alloc([128, 256], F32)
        Sb = A.alloc([128, 256], BF16)
        S0 = [A.alloc([128, 256], F32) for _ in range(2)]
        S1b = A.alloc([128, 256], BF16)
        krtm = A.alloc([128, 128], BF16)
        vrtm = A.alloc([128, 256], BF16)
        kmask = A.alloc([128, 128], BF16)
        gdec = A.alloc([128, NS], F32)
        sq = A.alloc([128, 512], BF16)
        rs = A.alloc([128, 512], F32)
        ystg = [A.alloc([128, NT], BF16) for _ in range(2)]
        scale = 128.0 ** -0.5

        dm('sp', aT[0:16, :], z_scr[24, 0:16, :], 'aT', [('z', 24)], ['aT'])
        dm('pool', w2[0:16, :], gla_w2[l], 'w2', [], ['w2'])
        dm('sp', gp, gla_p[l], 'gp', [], ['gp'])
        dm('sp', rmask, c_rmask, 'rmask', [], ['rmask'])
        ts('dve', nbg, gp[:, 0:4], -1.0, None, ALU.mult, None, ['gp'], ['nbg'])
        for h in range(4):
            dm('sp', qT, z_scr[8 + h], 'qT', [('z', 8 + h)], ['qT'])
            dm('sp', kT, z_scr[12 + h], 'kT', [('z', 12 + h)], ['kT'])
            dm('sp', vT, z_scr[16 + 2 * h:18 + 2 * h].rearrange("b p n -> p b n"), 'vT', [('z', 16 + 2 * h), ('z', 17 + 2 * h)], ['vT'])
            dm('sp', rT, z_scr[25 + 2 * h:27 + 2 * h].rearrange("b p n -> p b n"), 'rT', [('z', 25 + 2 * h), ('z', 26 + 2 * h)], ['rT'])
            for (t0, tn) in TILES_ALL:
                bk, bkey = P.bank()
                mm(bk[:, :tn], w2[0:16, h * 128:(h + 1) * 128], aT[0:16, t0:t0 + tn], True, True, ['w2', 'aT'], [bkey])
                act(G[:, t0:t0 + tn], bk[:, :tn], AF.Exp, [bkey, 'nbg'], ['G'], scale=-1.0, bias=nbg[:, h:h + 1])
            act(G, G, AF.Ln, ['G'], ['G'], bias=1.0)
            ts('dve', G, G, 1.0 / 16.0, None, ALU.mult, None, ['G'], ['G'])
            P.op('dve', lambda e: e.tensor_tensor_scan(out=B, data0=rmask, data1=G, initial=0.0, op0=ALU.mult, op1=ALU.add), reads=['rmask', 'G'], writes=['B'])
            act(E1, B, AF.Exp, ['B'], ['E1'], scale=-1.0)
            act(E2, B, AF.Exp, ['B'], ['E2'])
            stt(qs, qT, scale, E1, ALU.mult, ALU.mult, ['qT', 'E1'], ['qs'])
            tt('dve', ks, kT, E2, ALU.mult, ['kT', 'E2'], ['ks'])
            act(gdec, G[:, 0:NS], AF.Exp, ['G'], ['gdec'], scale=-1.0)
            B3 = B.rearrange("p (c n) -> p c n", n=128)
            E13 = E1.rearrange("p (c n) -> p c n", n=128)
            cp('dve', cd, E13[:, :, 127], ['E1'], ['cd'])
            tt('dve', E2.rearrange("p (c n) -> p c n", n=128), B3, B3[:, :, 127:128].broadcast_to([128, 17, 128]), ALU.subtract, ['B', 'ks'], ['E2'])
            act(E2, E2, AF.Exp, ['E2'], ['E2'])
            tt('dve', ku, kT, E2, ALU.mult, ['kT', 'E2'], ['ku'])
            memset('dve', ks[:, 0:COL0], 0.0, ['ks'])
            memset('dve', ku[:, 0:COL0], 0.0, ['ku'])
            memset('dve', Sf, 0.0, ['Sf'])
            memset('dve', Sb, 0.0, ['Sb'])
            for c in range(17):
                cs_ = slice(c * 128, (c + 1) * 128)
                par = c % 2
                for vc in range(2):
                    bk, bkey = P.bank()
                    bb = bk.bitcast(BF16)
                    tr(bb[:, 0:128], vT[:, vc, cs_], ident_b, ['vT', 'cb'], [bkey])
                    cp('act', vtm[par][:, vc * 128:(vc + 1) * 128], bb[:, 0:128], [bkey], [('vtm', par)])
                bk, bkey = P.bank()
                bb = bk.bitcast(BF16)
                tr(bb[:, 0:128], ku[:, cs_], ident_b, ['ku', 'cb'], [bkey])
                cp('act', kutm[par], bb[:, 0:128], [bkey], [('kutm', par)])
                ba, bak = P.bank()
                mm(ba[:, 0:128], ks[:, cs_], qs[:, cs_], True, True, ['ks', 'qs'], [bak])
                tt('dve', ATb[par], ba[:, 0:128], tri_b, ALU.mult, [bak, 'cb'], [('ATb', par)])
                for vc in range(2):
                    bo, bok = P.bank()
                    mm(bo[:, 0:128], vtm[par][:, vc * 128:(vc + 1) * 128], ATb[par], True, False, [('vtm', par), ('ATb', par)], [bok], inc=False)
                    mm(bo[:, 0:128], Sb[:, vc * 128:(vc + 1) * 128], qs[:, cs_], False, True, ['Sb', 'qs'], [bok])
                    cp('act', OT[:, vc, cs_], bo[:, 0:128], [bok], ['OT'])
                bs, bsk = P.bank()
                mm(bs[:, 0:256], kutm[par], vtm[par], True, True, [('kutm', par), ('vtm', par)], [bsk])
                stt(Sf, Sf, cd[:, c:c + 1], bs[:, 0:256], ALU.mult, ALU.add, ['Sf', 'cd', bsk], ['Sf'])
                cp('dve', Sb, Sf, ['Sf'], ['Sb'])
            dm('sp', o_pgla[l, h], Sf, 'opgla', ['Sf'], [])
            bk, bkey = P.bank()
            bb = bk.bitcast(BF16)
            tr(bb[:, 0:128], kT[:, 0:128], ident_b, ['kT', 'cb'], [bkey])
            cp('act', krtm, bb[:, 0:128], [bkey], ['krtm'])
            for vc in range(2):
                bk, bkey = P.bank()
                bb = bk.bitcast(BF16)
                tr(bb[:, 0:128], vT[:, vc, 0:128], ident_b, ['vT', 'cb'], [bkey])
                cp('act', vrtm[:, vc * 128:(vc + 1) * 128], bb[:, 0:128], [bkey], ['vrtm'])
            for n in range(NS):
                s0 = S0[n % 2]
                s0k = ('S0', n % 2)
                dm('sp', s0, gla_s0[l, n, h], f's0{n % 2}', [], [s0k])
                ts('dve', kmask[0:4, :], krtm[0:4, :], I4[0:4, n:n + 1], None, ALU.mult, None, ['krtm', 'cm'], ['kmask'])
                bs, bsk = P.bank()
                mm(bs[:, 0:256], kmask[0:4, :], vrtm[0:4, :], True, True, ['kmask', 'vrtm'], [bsk])
                stt(s0, s0, gdec[:, n:n + 1], bs[:, 0:256], ALU.mult, ALU.add, [s0k, 'gdec', bsk], [s0k])
                cp('dve', S1b, s0, [s0k], ['S1b'])
                dm('sp', o_sgla[l, n, h], s0, f's1{n % 2}', [s0k], [])
                for vc in range(2):
                    bo, bok = P.bank()
                    mm(bo[:, 0:1], S1b[:, vc * 128:(vc + 1) * 128], qT[:, n:n + 1], True, True, ['S1b', 'qT'], [bok])
                    act(OT[:, vc, n:n + 1], bo[:, 0:1], AF.Copy, [bok], ['OT'], scale=scale)
            for ti, (t0, tn) in enumerate(TILES_ALL):
                bk, bkey = P.bank()
                for vc in range(2):
                    act(sq[:, :tn], OT[:, vc, t0:t0 + tn], AF.Square, ['OT'], ['gsq'])
                    mm(bk[:, :tn], ones_b, sq[:, :tn], vc == 0, vc == 1, ['gsq', 'cb1'], [bkey])
                ts('dve', rs[:, :tn], bk[:, :tn], 1.0 / 256.0, EPS, ALU.mult, ALU.add, [bkey], ['grs'])
                act(rs[:, :tn], rs[:, :tn], AF.Ln, ['grs'], ['grs'])
                act(rs[:, :tn], rs[:, :tn], AF.Exp, ['grs'], ['grs'], scale=-0.5)
                for vc in range(2):
                    j = 2 * h + vc
                    s = ystg[j % 2]
                    sk = ('ystg', j % 2, ti)
                    stt(OT[:, vc, t0:t0 + tn], OT[:, vc, t0:t0 + tn], gp[:, 4 + vc:5 + vc], rs[:, :tn], ALU.mult, ALU.mult, ['OT', 'gp', 'grs'], ['OT'])
                    tt('dve', s[:, t0:t0 + tn], OT[:, vc, t0:t0 + tn], rT[:, vc, t0:t0 + tn], ALU.mult, ['OT', 'rT'], [sk])
            for vc in range(2):
                j = 2 * h + vc
                dm('sp', y_scr[8 + j], ystg[j % 2], f'yst{j % 2}', [('ystg', j % 2, t) for t in range(len(TILES_ALL))], [('y', 8 + j)])

    def stage_swa(l, first):
        A.reset()
        qT = A.alloc([128, 8, NT], BF16)
        kT = A.alloc([128, 2, NT], BF16)
        vT = A.alloc([128, 2, NT], BF16)
        vtm = A.alloc([128, 17, 256], BF16)
        yc = A.alloc([128, 8, NT], BF16)
        biasT = A.alloc([128, 2, 16, 128], BF16)
        bdec = A.alloc([128, 16], BF16)
        bnew = A.alloc([128, 16], F32)
        rb33 = A.alloc([128, 16], F32)
        oh = A.alloc([128, 2 * 256 + 128 + 4], F32)
        Gt = A.alloc([128, 2, 16, 128], F32)
        Gb = A.alloc([128, 2, 16, 128], BF16)
        estg = A.alloc([128, 2, 256], F32)
        padc = A.alloc([128, 2], F32)
        sk_ = A.alloc([128, 16], F32)
        esk = A.alloc([128, 16, 128], F32)
        PT = [A.alloc([128, 512], BF16) for _ in range(6)]
        dens = [A.alloc([128, 512], F32) for _ in range(3)]
        kc_ = A.alloc([128, NS, 2, 128], BF16)
        vc_ = A.alloc([128, NS, 256], BF16)
        pd = A.alloc([128, 4], BF16)
        pn = A.alloc([128, 4], BF16)
        pn2 = A.alloc([128, 4], F32)
        d4 = A.alloc([128, 4], F32)
        ktm = A.alloc([128, 256], F32)
        vlast = A.alloc([128, 256], F32)
        k0tm = A.alloc([128, 256], F32)
        v0tm = A.alloc([128, 256], F32)

        dm('sp', qT, z_scr[33:41].rearrange("b p n -> p b n"), 'qT', [('z', j) for j in range(33, 41)], ['qT'])
        dm('sp', kT, z_scr[41:43].rearrange("b p n -> p b n"), 'kT', [('z', 41), ('z', 42)], ['kT'])
        dm('sp', vT, z_scr[43:45].rearrange("b p n -> p b n"), 'vT', [('z', 43), ('z', 44)], ['vT'])
        dm('sp', padc, c_pad, 'padc', [], ['padc'])
        memset('pool', yc[:, :, 0:COL0], 0.0, ['yc'])
        dm('sp', sk_[0:64, :], swa_sink[l], 'sk', [], ['sk'])
        dm('pool', kc_, swa_kT[l], 'kc', [], ['kc'])
        dm('pool', vc_, swa_vn[l].rearrange("n s f -> s n f"), 'vc', [], ['vc'])
        dm('sp', rb33[0:32, :], relb, 'rb', [], ['rb33'])
        memset('dve', rb33[32:33, :], NEG, ['rb33m'])
        dm('sp', oh[0:33, :], c_oh, 'oh', [], ['oh'])
        for rel in range(2):
            bk, bkey = P.bank()
            mm(bk[0:16, 0:256], rb33[0:33, :], oh[0:33, rel * 256:(rel + 1) * 256], True, True, ['rb33', 'rb33m', 'oh'], [bkey])
            act(estg[0:16, rel, :], bk[0:16, 0:256], AF.Copy, [bkey], ['estg'], scale=8.0)
        dm('sp', e_scr.rearrange("r h j -> h r j"), estg[0:16, :, :], 'est', ['estg'], ['e_scr'])
        for rel in range(2):
            src = bass.AP(tensor=e_scr.tensor, offset=rel * 16 * 256, ap=[[1, 128], [256, 16], [1, 128]])
            dm('sp', Gt[:, rel, :, :], src, f'Gt{rel}', ['e_scr'], [('Gt', rel)])
        cp('dve', Gb.rearrange("p a b c -> p (a b c)"), Gt.rearrange("p a b c -> p (a b c)"), [('Gt', 0), ('Gt', 1)], ['Gb'])
        Gb2 = Gb.rearrange("p a b c -> p (a b c)")
        bT2 = biasT.rearrange("p a b c -> p (a b c)")
        for i in range(8):
            bk, bkey = P.bank()
            mm(bk[:, :], J_b, Gb2[:, i * 512:(i + 1) * 512], True, True, ['Gb', 'cb'], [bkey])
            cp('act', bT2[:, i * 512:(i + 1) * 512], bk[:, :], [bkey], ['biasT'])
        bk, bkey = P.bank()
        mm(bk[:, 0:16], oh[0:33, 512:640], rb33[0:33, :], True, True, ['rb33', 'rb33m', 'oh'], [bkey])
        act(bdec, bk[:, 0:16], AF.Copy, [bkey], ['bdec'], scale=8.0)
        bk, bkey = P.bank()
        mm(bk[0:4, 0:16], oh[0:33, 640:644], rb33[0:33, :], True, True, ['rb33', 'rb33m', 'oh'], [bkey])
        act(bnew[0:4, :], bk[0:4, 0:16], AF.Copy, [bkey], ['bnew'], scale=8.0)
        act(sk_[0:64, :], sk_[0:64, :], AF.Exp, ['sk'], ['sk'])
        cp('dve', esk[0:64, :, :], sk_[0:64, :].unsqueeze(2).broadcast_to([64, 16, 128]), ['sk'], ['esk'])
        for b in range(17):
            for c in range(2):
                bk, bkey = P.bank()
                bb = bk.bitcast(BF16)
                tr(bb[:, 0:128], vT[:, c, b * 128:(b + 1) * 128], ident_b, ['vT', 'cb'], [bkey])
                cp('act' if c == 0 else 'dve', vtm[:, b, c * 128:(c + 1) * 128], bb[:, 0:128], [bkey], ['vtm'])
        pti = 0
        for b in range(17):
            qs_ = slice(b * 128, (b + 1) * 128)
            for kvh in range(4):
                kvp, po = kvh // 2, 64 * (kvh % 2)
                prt = slice(po, po + 64)
                bo, bok = P.bank()
                bd, bdk = P.bank()
                kbs = [b] if b == 0 else [b - 1, b]
                for ki, kb in enumerate(kbs):
                    rel = 1 if kb == b else 0
                    bs, bsk = P.bank()
                    mm(bs[:, :], kT[prt, kvp, kb * 128:(kb + 1) * 128], qT[prt, kvp * 4:kvp * 4 + 4, qs_], True, False, ['kT', 'qT'], [bsk], inc=False)
                    mm(bs[:, :], ident_b, biasT[:, rel, kvh * 4:kvh * 4 + 4, :], False, True, ['cb', 'biasT'], [bsk])
                    p_ = PT[pti % 6]
                    pk = ('PT', pti % 6)
                    pti += 1
                    act(p_, bs[:, :], AF.Exp, [bsk, 'padc'], [pk], scale=0.125, bias=padc[:, (1 if kb == 0 else 0):(2 if kb == 0 else 1)])
                    mm(bo[0:64, :], vtm[:, kb, kvh * 64:(kvh + 1) * 64], p_, ki == 0, ki == len(kbs) - 1, ['vtm', pk], [bok])
                    mm(bd[0:64, :], ones_b[:, 0:64], p_, ki == 0, ki == len(kbs) - 1, ['cb1', pk], [bdk])
                di = (b * 4 + kvh) % 3
                den, dnk = dens[di], ('den', di)
                tt('dve', den[0:64, :], bd[0:64, :], esk[0:64, kvh * 4:kvh * 4 + 4, :], ALU.add, [bdk, 'esk'], [dnk])
                act(den[0:64, :], den[0:64, :], AF.Ln, [dnk], [dnk])
                act(den[0:64, :], den[0:64, :], AF.Exp, [dnk], [dnk], scale=-1.0)
                lo = COL0 if b == 0 else 0
                tt('dve', yc[prt, kvp * 4:kvp * 4 + 4, b * 128 + lo:(b + 1) * 128], bo[0:64, :].rearrange("p (g q) -> p g q", g=4)[:, :, lo:],
                   den[0:64, :].rearrange("p (g q) -> p g q", g=4)[:, :, lo:], ALU.mult, [bok, dnk], ['yc'])
        for c in range(2):
            for (blk, dst, dk_) in ((16, ktm, 'ktm'), (0, k0tm, 'k0tm')):
                bk, bkey = P.bank()
                bb = bk.bitcast(BF16)
                tr(bb[:, 0:128], kT[:, c, blk * 128:(blk + 1) * 128], ident_b, ['kT', 'cb'], [bkey])
                cp('act', dst[:, c * 128:(c + 1) * 128], bb[:, 0:128], [bkey], [dk_])
        cp('dve', vlast, vtm[:, 16, :], ['vtm'], ['vlast'])
        cp('dve', v0tm, vtm[:, 0, :], ['vtm'], ['v0tm'])
        dm('sp', o_pk[l], ktm, 'opk', ['ktm'], [])
        dm('sp', o_pv[l], vlast, 'opv', ['vlast'], [])
        for n in range(NS):
            dm('sp', o_sk[l, n, 0:127, :], swa_kn[l, n, 1:128, :], 'osk', [], [])
            dm('sp', o_sv[l, n, 0:127, :], swa_vn[l, n, 1:128, :], 'osv', [], [])
        dm('sp', o_sk[l, :, 127, :], k0tm[0:NS, :], 'osk2', ['k0tm'], [])
        dm('sp', o_sv[l, :, 127, :], v0tm[0:NS, :], 'osv2', ['v0tm'], [])
        for n in range(NS):
            for kvh in range(4):
                kvp, po = kvh // 2, 64 * (kvh % 2)
                prt = slice(po, po + 64)
                qn = qT[prt, kvp * 4:kvp * 4 + 4, n]
                bs, bsk = P.bank()
                mm(bs[:, 0:4], kc_[prt, n, kvp, :], qn, True, False, ['kc', 'qT'], [bsk], inc=False)
                mm(bs[:, 0:4], ident_b, bdec[:, kvh * 4:kvh * 4 + 4], False, True, ['cb', 'bdec'], [bsk])
                act(pd, bs[:, 0:4], AF.Exp, [bsk], ['pd'], scale=0.125)
                bn, bnk = P.bank()
                mm(bn[0:4, 0:4], kT[prt, kvp, 0:NS], qn, True, True, ['kT', 'qT'], [bnk])
                tt('dve', pn2[0:4, :], bn[0:4, 0:4], bnew[0:4, kvh * 4:kvh * 4 + 4], ALU.add, [bnk, 'bnew'], ['pn2'])
                act(pn2[0:4, :], pn2[0:4, :], AF.Exp, ['pn2'], ['pn2'], scale=0.125)
                ts('dve', pn[0:4, :], pn2[0:4, :], I4[0:4, n:n + 1], None, ALU.mult, None, ['pn2', 'cm'], ['pn'])
                bo, bok = P.bank()
                mm(bo[0:64, 0:4], vc_[:, n, kvh * 64:(kvh + 1) * 64], pd, True, False, ['vc', 'pd'], [bok], inc=False)
                mm(bo[0:64, 0:4], vtm[0:4, 0, kvh * 64:(kvh + 1) * 64], pn[0:4, :], False, True, ['vtm', 'pn'], [bok])
                bd, bdk = P.bank()
                mm(bd[0:64, 0:4], ones_b[:, 0:64], pd, True, False, ['cb1', 'pd'], [bdk], inc=False)
                mm(bd[0:64, 0:4], ones_b[0:4, 0:64], pn[0:4, :], False, True, ['cb1', 'pn'], [bdk])
                tt('dve', d4[0:64, :], bd[0:64, 0:4], sk_[0:64, kvh * 4:kvh * 4 + 4], ALU.add, [bdk, 'sk'], ['d4'])
                P.op('dve', lambda e: e.reciprocal(out=d4[0:64, :], in_=d4[0:64, :]), reads=['d4'], writes=['d4'])
                tt('dve', yc[prt, kvp * 4:kvp * 4 + 4, n], bo[0:64, 0:4], d4[0:64, :], ALU.mult, [bok, 'd4'], ['yc'])
        for j in range(8):
            dm('sp', y_scr[16 + j], yc[:, j, :], 'ycst', ['yc'], [('y', 16 + j)])

    def stage_merge(l):
        A.reset()
        yq = A.alloc([128, 24, QW], BF16)
        gq = [A.alloc([128, 3, QW], BF16) for _ in range(2)]
        macc = A.alloc([128, QW], F32)
        mtmp = A.alloc([128, QW], F32)
        mq = A.alloc([128, 16, QW], BF16)
        mix = A.alloc([128, 16, QW], F32)
        xq = A.alloc([128, 16, QW], F32)
        hq = A.alloc([128, 16, QW], BF16)
        sq = A.alloc([128, 2, QW], BF16)
        rs = A.alloc([128, QW], F32)
        def load_yq(q):
            cq_ = slice(q * QW, q * QW + QW)
            dm('sp', yq, y_scr[:, :, cq_].rearrange("b p n -> p b n"), 'yq', [('y', j) for j in range(24)], ['yq'])
        load_yq(0)
        xck = lambda c: ('xqc', c)
        xall = ['xq'] + [('xqc', c) for c in range(16)]
        for q in range(NQ):
            c0 = q * QW
            cq = slice(c0, c0 + QW)
            dm('sp', xq, x_scr[:, :, cq], 'xq', [('x_scr', q)], ['xq'])
            blocks = []
            for j in range(16):
                for br in range(3):
                    blocks.append([(w_up[l, br, j], 8)])

            def epi(jj, ti, t0, tn, bk, bkey, q=q, cq=cq, c0=c0):
                j, br = jj // 3, jj % 3
                g = gq[j % 2]
                gk = ('gq', j % 2)
                if br == 0 and ti == 0:
                    src = bass.AP(tensor=z_scr.tensor, offset=(45 + j) * 128 * NT + c0, ap=[[NT, 128], [16 * 128 * NT, 3], [1, QW]])
                    dm('sp', g, src, f'gq{j % 2}', [('z', 45 + j), ('z', 61 + j), ('z', 77 + j)], [gk])
                sl = slice(t0, t0 + tn)
                if br == 0:
                    tt('dve', macc[:, sl], bk[:, :tn], g[:, 0, sl], ALU.mult, [bkey, gk], [('macc', ti)])
                else:
                    tt('dve', mtmp[:, sl], bk[:, :tn], g[:, br, sl], ALU.mult, [bkey, gk], [('mtmp', ti)])
                    if br == 1:
                        tt('dve', macc[:, sl], macc[:, sl], mtmp[:, sl], ALU.add, [('macc', ti), ('mtmp', ti)], [('macc', ti)])
                    else:
                        tt('dve', mq[:, j, sl], macc[:, sl], mtmp[:, sl], ALU.add, [('macc', ti), ('mtmp', ti)], ['mq'])
            gemm_blocks_rhs(blocks, [(lambda kg, t0, tn, br=jj % 3: yq[:, br * 8 + kg, t0:t0 + tn]) for jj in range(48)], ['yq'], TILES_Q, epi)
            if q + 1 < NQ:
                load_yq(q + 1)
            blocks = [[(w_out[l, j], 16)] for j in range(16)]

            def epi2(j, ti, t0, tn, bk, bkey):
                cp('act' if ti == 0 else 'dve', mix[:, j, t0:t0 + tn], bk[:, :tn], [bkey], ['mix'])
            gemm(blocks, lambda kg, t0, tn: mq[:, kg, t0:t0 + tn], ['mq'], TILES_Q, epi2)
            rstd_of(mix, 'mix', QW, sq, rs, 'n1')
            for c in range(16):
                stt(mix[:, c, :], mix[:, c, :], gn[:, l, 1, c:c + 1], rs, ALU.mult, ALU.mult, ['mix', 'n1rs', 'gn'], [('mixc', c)])
                tt('pool' if c % 2 == 0 else 'dve', xq[:, c, :], xq[:, c, :], mix[:, c, :], ALU.add, ['mix', ('mixc', c), 'xq'], [xck(c)])
            dm('sp', x_scr[:, :, cq], xq, 'xst', xall, [('x_scr', q)])
            rstd_of(xq, 'xq', QW, sq, rs, 'n2', ckey=xck)
            norm_to(hq, 'hq', xq, 'xq', QW, gn[:, l, 2, :], rs, 'n2', ckey=xck)
            dm('sp', h_scr[:, :, cq], hq, 'hst', ['hq'], [('h_scr', q)])

    def gemm_blocks_rhs(blocks, rhs_fns, rhs_keys, tiles, epi):
        flat = [(ap, kc) for blk in blocks for (ap, kc) in blk]
        LA = NWB - 1
        loaded = {}
        for i in range(min(LA, len(flat))):
            loaded[i] = wload(*flat[i])
        idx = 0
        for j, blk in enumerate(blocks):
            bks = [P.bank() for _ in tiles]
            (ap, kc) = blk[0]
            if idx + LA < len(flat):
                loaded[idx + LA] = wload(*flat[idx + LA])
            wb, wk = loaded.pop(idx)
            idx += 1
            rf = rhs_fns[j]
            for ti, (t0, tn) in enumerate(tiles):
                bk, bkey = bks[ti]
                for k in range(kc):
                    P.op('pe', lambda e, bk=bk, wb=wb, k=k, t0=t0, tn=tn, kc=kc, rf=rf:
                         e.matmul(bk[:, :tn], lhsT=wb[:, k, :], rhs=rf(k, t0, tn), start=(k == 0), stop=(k == kc - 1)),
                         reads=[wk] + rhs_keys, writes=[bkey], inc=(k == kc - 1))
            for ti, (t0, tn) in enumerate(tiles):
                epi(j, ti, t0, tn, bks[ti][0], bks[ti][1])

    def stage_ff1(l):
        A.reset()
        hT = A.alloc([128, 16, NT], BF16)
        stg = [A.alloc([128, NT], BF16) for _ in range(2)]
        rl = [A.alloc([128, 512], BF16) for _ in range(2)]
        dm('sp', hT, h_scr, 'hT', [('h_scr', q) for q in range(NQ)], ['hT'])
        blocks = [[(w_ff1[l, j], 16)] for j in range(64)]

        def epi(j, ti, t0, tn, bk, bkey):
            s = stg[j % 2]
            sk = ('stg', j % 2, ti)
            r = rl[ti % 2]
            act(r[:, :tn], bk[:, :tn], AF.Relu, [bkey], [('rl', ti % 2)])
            tt('pool', s[:, t0:t0 + tn], r[:, :tn], r[:, :tn], ALU.mult, [('rl', ti % 2)], [sk])
            if ti == len(TILES_ALL) - 1:
                dm('sp', hid_scr[j], s, f'hst{j % 2}', [('stg', j % 2, t) for t in range(len(TILES_ALL))], [('hid', j)])
        gemm(blocks, lambda kg, t0, tn: hT[:, kg, t0:t0 + tn], ['hT'], TILES_ALL, epi)

    def stage_ff2(l, last):
        A.reset()
        hid = A.alloc([128, 64, QW], BF16)
        ffo = A.alloc([128, 16, QW], F32)
        xq = A.alloc([128, 16, QW], F32)
        hq = A.alloc([128, 16, QW], BF16)
        sq = A.alloc([128, 2, QW], BF16)
        rs = A.alloc([128, QW], F32)
        def load_hid(q):
            cq_ = slice(q * QW, q * QW + QW)
            for part in range(4):
                dm('sp', hid[:, part * 16:(part + 1) * 16, :], hid_scr[part * 16:(part + 1) * 16, :, cq_].rearrange("b p n -> p b n"), f'hid{part}',
                   [('hid', j) for j in range(part * 16, part * 16 + 16)], [('hidq', part)])
        load_hid(0)
        xck = lambda c: ('xqc', c)
        xall = ['xq'] + [('xqc', c) for c in range(16)]
        for q in range(NQ):
            c0 = q * QW
            cq = slice(c0, c0 + QW)
            dm('sp', xq, x_scr[:, :, cq], 'xq', [('x_scr', q)], ['xq'])
            blocks = [[(w_ff2[l, j, kg], 16) for kg in range(4)] for j in range(16)]

            def epi(j, ti, t0, tn, bk, bkey):
                cp('act' if ti == 0 else 'dve', ffo[:, j, t0:t0 + tn], bk[:, :tn], [bkey], ['ffo'])
            gemm(blocks, lambda kg, t0, tn: hid[:, kg, t0:t0 + tn], [('hidq', p_) for p_ in range(4)], TILES_Q, epi)
            if q + 1 < NQ:
                load_hid(q + 1)
            rstd_of(ffo, 'ffo', QW, sq, rs, 'n3')
            for c in range(16):
                stt(ffo[:, c, :], ffo[:, c, :], gn[:, l, 3, c:c + 1], rs, ALU.mult, ALU.mult, ['ffo', 'n3rs', 'gn'], [('ffoc', c)])
                tt('pool' if c % 2 == 0 else 'dve', xq[:, c, :], xq[:, c, :], ffo[:, c, :], ALU.add, ['ffo', ('ffoc', c), 'xq'], [xck(c)])
            if last:
                dm('sp', yT[:, :, cq], xq, 'yout', xall, [])
            else:
                dm('sp', x_scr[:, :, cq], xq, 'xst', xall, [('x_scr', q)])
                rstd_of(xq, 'xq', QW, sq, rs, 'n4', ckey=xck)
                norm_to(hq, 'hq', xq, 'xq', QW, gn[:, l + 1, 0, :], rs, 'n4', ckey=xck)
                dm('sp', h_scr[:, :, cq], hq, 'hst', ['hq'], [('h_scr', q)])

    stage_prenorm0()
    for l in range(L):
        P.barrier(); stage_win(l)
        P.barrier(); stage_s5(l)
        P.barrier(); stage_gla(l)
        P.barrier(); stage_swa(l, l == 0)
        P.barrier(); stage_merge(l)
        P.barrier(); stage_ff1(l)
        P.barrier(); stage_ff2(l, l == L - 1)
    P.build()
    return nc, P


def _tile_w(W, kc):
    K, N = W.shape
    return np.ascontiguousarray(W.reshape(K // 128, 128, N // 128, 128).transpose(2, 1, 0, 3))


def _t5_bucket(d):
    if d < 16:
        return d
    v = 16 + int(np.float32(np.log(np.float32(d) / np.float32(16)) / np.float32(math.log(128 / 16)) * np.float32(16)))
    return min(v, 31)


def _consts():
    oh = np.zeros((33, 2 * 256 + 128 + 4), np.float32)
    for j in range(256):
        dist = j + 1
        if j <= 254 and dist < 128:
            oh[_t5_bucket(dist), j] = 1
        else:
            oh[32, j] = 1
        dist = j - 127
        if j <= 254 and dist >= 0:
            oh[_t5_bucket(dist), 256 + j] = 1
        else:
            oh[32, 256 + j] = 1
    for j in range(128):
        dist = 128 - j
        if dist < 128:
            oh[_t5_bucket(dist), 512 + j] = 1
        else:
            oh[32, 512 + j] = 1
    oh[0, 640:644] = 1
    misc = np.zeros((128, 128 * 3 + 12), np.float32)
    misc[:, 0:128] = np.eye(128)
    misc[:, 128:256] = np.eye(128)[::-1]
    misc[:, 256:384] = np.triu(np.ones((128, 128)))
    for gl in range(8):
        misc[gl * 16:(gl + 1) * 16, 384 + gl] = 1
    misc[0:4, 392:396] = np.eye(4)
    tpos = np.tile((np.arange(NT) - COL0).astype(np.float32)[None, :], (128, 1))
    rmask = np.ones((128, NT), np.float32)
    rmask[:, 0::128] = 0
    pad = np.zeros((128, 2), np.float32)
    pad[0:COL0, 1] = NEG
    return oh, misc, tpos, rmask, pad


def _prep_shared(inp, L):
    sh = {}
    w_in = inp['w_in'][:L]
    offs = np.cumsum([0, 1024, 512, 512, 1024, 16, 1024, 1024, 256, 256, 6144])
    cols = []
    cols.append(np.arange(offs[0], offs[1]))
    cols.append(np.arange(offs[1], offs[2]))
    cols.append(np.arange(offs[2], offs[3]))
    cols.append(np.arange(offs[3], offs[4]))
    a_cols = np.concatenate([np.arange(offs[4], offs[5]), -np.ones(112, np.int64)])
    cols.append(a_cols)
    cols.append(np.arange(offs[5], offs[6]))
    qc = []
    for kvp in range(2):
        for g in range(4):
            for half in range(2):
                qh = (2 * kvp + half) * 4 + g
                qc.append(offs[6] + qh * 64 + np.arange(64))
    qperm = np.concatenate(qc)
    cols.append(qperm)
    cols.append(np.arange(offs[7], offs[8]))
    cols.append(np.arange(offs[8], offs[9]))
    cols.append(np.arange(offs[9], offs[10]))
    allc = np.concatenate(cols)
    assert allc.shape[0] == NBLK_IN * 128
    wt = np.zeros((L, NBLK_IN, 128, 16, 128), np.float32)
    valid = allc >= 0
    for l in range(L):
        Wp = np.zeros((D, NBLK_IN * 128), np.float32)
        Wp[:, valid] = w_in[l][:, allc[valid]]
        wt[l] = _tile_w(Wp, 16)
    sh['w_in'] = wt
    sh['w_glu'] = np.stack([_tile_w(inp['s5_w_glu'][l], 8) for l in range(L)])
    qrow = qperm - offs[6]
    sh['w_up'] = np.stack([np.stack([_tile_w(inp['w_up_s5'][l], 8), _tile_w(inp['w_up_gla'][l], 8), _tile_w(inp['w_up_swa'][l][qrow, :], 8)]) for l in range(L)])
    sh['w_out'] = np.stack([_tile_w(inp['w_out'][l], 16) for l in range(L)])
    sh['w_ff1'] = np.stack([_tile_w(inp['w_ff1'][l], 16) for l in range(L)])
    ff2 = np.stack([_tile_w(inp['w_ff2'][l], 64) for l in range(L)])
    sh['w_ff2'] = np.ascontiguousarray(ff2.reshape(L, 16, 128, 4, 16, 128).transpose(0, 1, 3, 2, 4, 5))
    g = np.stack([inp[n][:L] for n in ('norm_pre_mix', 'norm_post_mix', 'norm_pre_ffn', 'norm_post_ffn')], axis=1)
    sh['gnorm'] = np.ascontiguousarray(g.reshape(L, 4, 16, 128).transpose(0, 3, 1, 2))
    def rep_gp(a):
        x = a.reshape(L, 8, 8, 64)
        x = np.broadcast_to(x[:, :, :, None, :], (L, 8, 8, 16, 64))
        return x.transpose(0, 2, 3, 1, 4).reshape(L, 128, 512)
    ls = np.broadcast_to(inp['s5_log_step'][:L, :, None], (L, 64, 64))
    def b_t(a):
        x = a.reshape(L, 8, 8, 64, 16)
        return x.transpose(0, 2, 4, 1, 3).reshape(L, 128, 512)
    sh['s5rep'] = np.ascontiguousarray(np.stack([rep_gp(inp['s5_lam_re'][:L]), rep_gp(inp['s5_lam_im'][:L]), rep_gp(ls), b_t(inp['s5_b_re'][:L]), b_t(inp['s5_b_im'][:L])], axis=2)).astype(np.float32)
    def cm_gp(a):
        return a.reshape(L, 32, 2, 64).transpose(0, 2, 3, 1).reshape(L, 128, 32)
    sh['s5cm'] = np.ascontiguousarray(np.stack([cm_gp(inp['s5_lam_re'][:L]), cm_gp(inp['s5_lam_im'][:L]), cm_gp(ls)], axis=2)).astype(np.float32)
    def c_t(a):
        return a.reshape(L, 32, 2, 16, 64).transpose(0, 2, 4, 1, 3).reshape(L, 128, 32, 16)
    sh['s5c'] = np.ascontiguousarray(np.stack([c_t(inp['s5_c_re'][:L]), c_t(inp['s5_c_im'][:L])], axis=2)).astype(np.float32)
    fp = lambda a: a.reshape(L, 8, 128).transpose(0, 2, 1)
    sh['s5d'] = np.ascontiguousarray(np.stack([fp(inp['s5_d'][:L]), fp(inp['s5_b_glu'][:L])], axis=2)).astype(np.float32)
    sh['gla_w2'] = np.ascontiguousarray(inp['gla_w_gate2'][:L])
    sh['gla_p'] = np.ascontiguousarray(np.concatenate([inp['gla_b_gate2'][:L].reshape(L, 4, 128).transpose(0, 2, 1), inp['gla_g_out'][:L].reshape(L, 2, 128).transpose(0, 2, 1)], axis=2)).astype(np.float32)
    sh['swa_sink'] = np.ascontiguousarray(np.broadcast_to(inp['swa_sinks'][:L, None, :], (L, 64, 16))).astype(np.float32)
    sh['relb'] = np.ascontiguousarray(inp['rel_bias'])
    oh, misc, tpos, rmask, pad = _consts()
    sh['c_oh'] = oh; sh['c_misc'] = misc; sh['c_tpos'] = tpos; sh['c_rmask'] = rmask; sh['c_pad'] = pad
    return sh


def _prep_core(inp, L, c):
    b = c % 4
    m = {}
    xt = np.zeros((NT, D), np.float32)
    xt[0:NS] = inp['x_sample'][NS * c:NS * c + NS, 0, :]
    xt[COL0:COL0 + NMETA] = inp['meta_tokens']
    xt[128:] = inp['x_prompt'][b]
    m['xT'] = np.ascontiguousarray(xt.T.reshape(16, 128, NT).transpose(1, 0, 2))
    ns = slice(NS * c, NS * c + NS)
    def x0cm(a):
        return a.reshape(L, NS, 32, 2, 64).transpose(0, 3, 4, 2, 1).reshape(L, 128, 32, NS)
    m['s5x0'] = np.ascontiguousarray(np.stack([x0cm(inp['state_s5_re'][:L, ns]), x0cm(inp['state_s5_im'][:L, ns])], axis=2)).astype(np.float32)
    m['gla_s0'] = np.ascontiguousarray(inp['state_gla'][:L, ns])
    ck = inp['cache_swa_k'][:L, ns]
    m['swa_kT'] = np.ascontiguousarray(ck.reshape(L, NS, 128, 2, 2, 64).transpose(0, 4, 5, 1, 3, 2).reshape(L, 128, NS, 2, 128))
    m['swa_kn'] = np.ascontiguousarray(ck.reshape(L, NS, 128, 256))
    m['swa_vn'] = np.ascontiguousarray(inp['cache_swa_v'][:L, ns].reshape(L, NS, 128, 256))
    return m


_CACHE = {}


def run(inputs, L=4, debug=False):
    inp = {k: np.asarray(v) for k, v in inputs.items()}
    key = (L, debug)
    if key not in _CACHE:
        _CACHE[key] = build_program(L, debug)
    nc, P = _CACHE[key]
    sh = _prep_shared(inp, L)
    in_maps = []
    for c in range(8):
        m = dict(sh)
        m.update(_prep_core(inp, L, c))
        in_maps.append(m)
    res = run_bass_kernel_spmd(nc, in_maps, core_ids=list(range(8)))
    return res.results


def assemble(R, L):
    B, NSAMP = 4, 32
    y_prompt = np.zeros((B, SEQ, D), np.float32)
    y_sample = np.zeros((NSAMP, 1, D), np.float32)
    p_s5r = np.zeros((L, B, 64, 64), np.float32); p_s5i = np.zeros_like(p_s5r)
    p_gla = np.zeros((L, B, 4, 128, 256), np.float32)
    p_k = np.zeros((L, B, 128, 4, 64), np.float32); p_v = np.zeros_like(p_k)
    s_s5r = np.zeros((L, NSAMP, 64, 64), np.float32); s_s5i = np.zeros_like(s_s5r)
    s_gla = np.zeros((L, NSAMP, 4, 128, 256), np.float32)
    s_k = np.zeros((L, NSAMP, 128, 4, 64), np.float32); s_v = np.zeros_like(s_k)
    for c in range(8):
        r = R[c]
        yt = r['yT'].transpose(1, 0, 2).reshape(D, NT).T
        ns = slice(NS * c, NS * c + NS)
        y_sample[ns, 0, :] = yt[0:NS]
        ss5 = r['o_ss5']
        v = ss5.reshape(L, 2, 64, 2, 32, NS).transpose(3, 0, 5, 4, 1, 2).reshape(2, L, NS, 64, 64)
        s_s5r[:, ns] = v[0]; s_s5i[:, ns] = v[1]
        s_gla[:, ns] = r['o_sgla']
        s_k[:, ns] = r['o_sk'].reshape(L, NS, 128, 4, 64)
        s_v[:, ns] = r['o_sv'].reshape(L, NS, 128, 4, 64)
        if c < 4:
            b = c
            y_prompt[b] = yt[128:]
            ps = r['o_ps5']
            v = ps.reshape(L, 2, 64, 2, 32).transpose(3, 0, 4, 1, 2).reshape(2, L, 64, 64)
            p_s5r[:, b] = v[0]; p_s5i[:, b] = v[1]
            p_gla[:, b] = r['o_pgla']
            p_k[:, b] = r['o_pk'].reshape(L, 128, 4, 64)
            p_v[:, b] = r['o_pv'].reshape(L, 128, 4, 64)
    return (y_prompt, y_sample, p_s5r, p_s5i, p_gla, p_k, p_v, s_s5r, s_s5i, s_gla, s_k, s_v)


def kernel(**inputs):
    R = run(inputs, L=4)
    return assemble(R, 4)
```

```python
import math
import numpy as np
from contextlib import ExitStack
import concourse.bass as bass
import concourse.mybir as mybir
from concourse.bass_utils import run_bass_kernel_spmd

F32 = mybir.dt.float32
BF16 = mybir.dt.bfloat16
AF = mybir.ActivationFunctionType
ALU = mybir.AluOpType

D = 2048; NT = 2176; QW = 544; NQ = 4; SEQ = 2048; NMETA = 16; COL0 = 112
NS = 4
NBLK_IN = 93
EPS = 1e-6
PI = math.pi
ENG = ['pe', 'act', 'dve', 'pool', 'sp']
TILES_ALL = [(0, 512), (512, 512), (1024, 512), (1536, 512), (2048, 128)]
TILES_Q = [(0, 512), (512, 32)]
NEG = -30000.0


class Prog:
    def __init__(self, nc):
        self.nc = nc
        self.q = {e: [] for e in ENG}
        self.cnt = {e: 0 for e in ENG}
        self.seen = {e: {} for e in ENG}
        self.keys = {}
        self.dtot = {}
        self.es = ExitStack()
        self.nops = 0
        self.banks = None
        self.bi = 0

    def sb(self, name, shape, dt):
        return self.es.enter_context(self.nc.sbuf_tensor(name, list(shape), dt))

    def mkbanks(self):
        self.banks = [self.es.enter_context(self.nc.psum_tensor(f"psb{i}", [128, 512], F32)) for i in range(8)]

    def bank(self):
        i = self.bi % 8
        self.bi += 1
        return self.banks[i], f"ps{i}"

    def _deps(self, reads, writes):
        toks = []
        for k in reads:
            st = self.keys.get(k)
            if st is not None and st[0] is not None:
                toks.append(st[0])
        for k in writes:
            st = self.keys.get(k)
            if st is not None:
                if st[0] is not None:
                    toks.append(st[0])
                toks.extend(st[1].items())
        return toks

    def _commit(self, tok, reads, writes):
        for k in reads:
            st = self.keys.get(k)
            if st is None:
                st = self.keys[k] = [None, {}]
            if st[1].get(tok[0], 0) < tok[1]:
                st[1][tok[0]] = tok[1]
        for k in writes:
            self.keys[k] = [tok, {}]

    def _mkwaits(self, eng, toks):
        best = {}
        for s, v in toks:
            if s == eng and eng == 'pe':
                continue
            if best.get(s, 0) < v:
                best[s] = v
        seen = self.seen[eng]
        out = []
        for s, v in best.items():
            if seen.get(s, 0) >= v:
                continue
            seen[s] = v
            out.append((s, v))
        return out

    def op(self, eng, fn, reads=(), writes=(), inc=True):
        waits = self._mkwaits(eng, self._deps(reads, writes))
        if inc:
            self.cnt[eng] += 1
            tok = (eng, self.cnt[eng])
        else:
            tok = (eng, self.cnt[eng] + 1)
        self.q[eng].append((waits, fn, eng if inc else None, 1))
        self._commit(tok, reads, writes)
        self.nops += 1

    def dma(self, eng, fn, chan, reads=(), writes=()):
        waits = self._mkwaits(eng, self._deps(reads, writes))
        self.dtot[chan] = self.dtot.get(chan, 0) + 16
        tok = (chan, self.dtot[chan])
        self.q[eng].append((waits, fn, chan, 16))
        self._commit(tok, reads, writes)
        self.nops += 1

    def barrier(self):
        toks = [(e, self.cnt[e]) for e in ENG if self.cnt[e] > 0] + list(self.dtot.items())
        for e in ENG:
            waits = self._mkwaits(e, toks)
            if waits:
                self.q[e].append((waits, None, None, 0))

    def build(self):
        nc = self.nc
        names = sorted(set(ENG) | set(self.dtot.keys()))
        sems = {n: self.es.enter_context(nc.semaphore("s_" + n)) for n in names}
        fin = list(self.dtot.items()) + [(e, self.cnt[e]) for e in ENG if e != 'sp' and self.cnt[e] > 0]
        q = self.q
        block = self.es.enter_context(nc.Block())
        emap = {'pe': block.tensor, 'act': block.scalar, 'dve': block.vector, 'pool': block.gpsimd, 'sp': block.sync}

        def mk(engname):
            def body(e):
                for waits, fn, s, n in q[engname]:
                    for (ws, wv) in waits:
                        e.wait_ge(sems[ws], wv)
                    if fn is None:
                        continue
                    ins = fn(e)
                    if s is not None:
                        ins.then_inc(sems[s], n)
                if engname == 'sp':
                    for (ws, wv) in fin:
                        e.wait_ge(sems[ws], wv)
            return body
        for engname in ENG:
            emap[engname](mk(engname))
        self.es.close()


class Arena:
    def __init__(self, P, words):
        self.t = P.sb("arena", [128, words], F32)
        self.words = words
        self.off = 0
        self.gen = 0

    def reset(self):
        self.off = 0
        self.gen += 1

    def alloc(self, shape, dt):
        n = 1
        for s in shape[1:]:
            n *= s
        w = n if dt == F32 else (n + 1) // 2
        w = (w + 15) // 16 * 16
        assert self.off + w <= self.words, (self.off, w, self.words)
        v = self.t[:, self.off:self.off + w]
        self.off += w
        if dt != F32:
            v = v.bitcast(dt)
        v = v[:, :n]
        if len(shape) == 3:
            v = v.rearrange("p (a b) -> p a b", a=shape[1])
        elif len(shape) == 4:
            v = v.rearrange("p (a b c) -> p a b c", a=shape[1], b=shape[2])
        elif len(shape) == 5:
            v = v.rearrange("p (a b c d) -> p a b c d", a=shape[1], b=shape[2], c=shape[3])
        return v


def build_program(L, debug=False):
    nc = bass.Bass("TRN2", target_bir_lowering=False)
    P = Prog(nc)
    A = Arena(P, 46000)
    P.mkbanks()

    def din(name, shape):
        return nc.dram_tensor(name, list(shape), F32, kind="ExternalInput").ap()

    def dout(name, shape):
        return nc.dram_tensor(name, list(shape), F32, kind="ExternalOutput").ap()

    def dscr(name, shape, dt):
        return nc.dram_tensor(name, list(shape), dt, kind=("ExternalOutput" if debug else "Internal")).ap()

    xT = din("xT", [128, 16, NT])
    w_in = din("w_in", [L, NBLK_IN, 128, 16, 128])
    w_glu = din("w_glu", [L, 8, 128, 8, 128])
    w_up = din("w_up", [L, 3, 16, 128, 8, 128])
    w_out = din("w_out", [L, 16, 128, 16, 128])
    w_ff1 = din("w_ff1", [L, 64, 128, 16, 128])
    w_ff2 = din("w_ff2", [L, 16, 4, 128, 16, 128])
    gnorm = din("gnorm", [L, 128, 4, 16])
    s5rep = din("s5rep", [L, 128, 5, 512])
    s5cm = din("s5cm", [L, 128, 3, 32])
    s5c = din("s5c", [L, 128, 2, 32, 16])
    s5d = din("s5d", [L, 128, 2, 8])
    s5x0 = din("s5x0", [L, 128, 2, 32, NS])
    gla_w2 = din("gla_w2", [L, 16, 512])
    gla_p = din("gla_p", [L, 128, 6])
    gla_s0 = din("gla_s0", [L, NS, 4, 128, 256])
    swa_sink = din("swa_sink", [L, 64, 16])
    swa_kT = din("swa_kT", [L, 128, NS, 2, 128])
    swa_kn = din("swa_kn", [L, NS, 128, 256])
    swa_vn = din("swa_vn", [L, NS, 128, 256])
    relb = din("relb", [32, 16])
    c_oh = din("c_oh", [33, 2 * 256 + 128 + 4])
    c_misc = din("c_misc", [128, 128 * 3 + 8 + 4])
    c_tpos = din("c_tpos", [128, NT])
    c_rmask = din("c_rmask", [128, NT])
    c_pad = din("c_pad", [128, 2])
    yT = dout("yT", [128, 16, NT])
    o_ps5 = dout("o_ps5", [L, 128, 2, 32])
    o_pgla = dout("o_pgla", [L, 4, 128, 256])
    o_pk = dout("o_pk", [L, 128, 256])
    o_pv = dout("o_pv", [L, 128, 256])
    o_ss5 = dout("o_ss5", [L, 128, 2, 32, NS])
    o_sgla = dout("o_sgla", [L, NS, 4, 128, 256])
    o_sk = dout("o_sk", [L, NS, 128, 256])
    o_sv = dout("o_sv", [L, NS, 128, 256])
    x_scr = dscr("x_scr", [128, 16, NT], F32)
    h_scr = dscr("h_scr", [128, 16, NT], BF16)
    z_scr = dscr("z_scr", [NBLK_IN, 128, NT], BF16)
    y_scr = dscr("y_scr", [24, 128, NT], BF16)
    hid_scr = dscr("hid_scr", [64, 128, NT], BF16)
    e_scr = dscr("e_scr", [2, 16, 256], F32)

    NWB = 4
    wbufs = [P.sb(f"wb{i}", [128, 16, 128], BF16) for i in range(NWB)]
    cm = P.sb("cmisc", [128, 128 * 3 + 12], F32)
    cb = P.sb("cbf", [128, 128 * 3 + 128], BF16)
    gn = P.sb("gn", [128, L, 4, 16], F32)
    ident_f = cm[:, 0:128]
    maskBD = cm[:, 384:392]
    I4 = cm[:, 392:396]
    ident_b = cb[:, 0:128]
    J_b = cb[:, 128:256]
    tri_b = cb[:, 256:384]
    ones_b = cb[:, 384:512]

    P.dma('sp', lambda e: e.dma_start(out=cm[:], in_=c_misc), 'cm', writes=['cm'])
    P.op('dve', lambda e: e.tensor_copy(out=cb[:, 0:384], in_=cm[:, 0:384]), reads=['cm'], writes=['cb'])
    P.op('dve', lambda e: e.memset(cb[:, 384:512], 1.0), writes=['cb1'])
    P.dma('sp', lambda e: e.dma_start(out=gn[:], in_=gnorm.rearrange("l p a c -> p l a c")), 'gn', writes=['gn'])

    wcnt = [0]

    def wload(src, kc):
        i = wcnt[0] % NWB
        wcnt[0] += 1
        P.dma('pool', lambda e: e.dma_start(out=wbufs[i][:, :kc, :], in_=src), f'wb{i}', writes=[f'wb{i}'])
        return wbufs[i], f'wb{i}'

    def gemm(blocks, rhs_fn, rhs_keys, tiles, epi):
        flat = [(ap, kc) for blk in blocks for (ap, kc) in blk]
        LA = NWB - 1
        loaded = {}
        for i in range(min(LA, len(flat))):
            loaded[i] = wload(*flat[i])
        idx = 0
        for j, blk in enumerate(blocks):
            bks = [P.bank() for _ in tiles]
            nk = sum(kc for _, kc in blk)
            kd = 0
            for (ap, kc) in blk:
                if idx + LA < len(flat):
                    loaded[idx + LA] = wload(*flat[idx + LA])
                wb, wk = loaded.pop(idx)
                idx += 1
                for ti, (t0, tn) in enumerate(tiles):
                    bk, bkey = bks[ti]
                    for k in range(kc):
                        kg = kd + k
                        P.op('pe', lambda e, bk=bk, wb=wb, k=k, kg=kg, t0=t0, tn=tn, nk=nk:
                             e.matmul(bk[:, :tn], lhsT=wb[:, k, :], rhs=rhs_fn(kg, t0, tn), start=(kg == 0), stop=(kg == nk - 1)),
                             reads=[wk] + rhs_keys, writes=[bkey], inc=(k == kc - 1))
                kd += kc
            for ti, (t0, tn) in enumerate(tiles):
                epi(j, ti, t0, tn, bks[ti][0], bks[ti][1])

    def act(out, in_, func, r, w, **kw):
        P.op('act', lambda e: e.activation(out=out, in_=in_, func=func, **kw), reads=r, writes=w)

    def tt(eng, out, a, b, op, r, w):
        P.op(eng, lambda e: e.tensor_tensor(out=out, in0=a, in1=b, op=op), reads=r, writes=w)

    def ts(eng, out, a, s1, s2, op0, op1, r, w):
        if op1 is None:
            P.op(eng, lambda e: e.tensor_scalar(out=out, in0=a, scalar1=s1, scalar2=None, op0=op0), reads=r, writes=w)
        else:
            P.op(eng, lambda e: e.tensor_scalar(out=out, in0=a, scalar1=s1, scalar2=s2, op0=op0, op1=op1), reads=r, writes=w)

    def stt(out, a, s, b, op0, op1, r, w):
        P.op('dve', lambda e: e.scalar_tensor_tensor(out=out, in0=a, scalar=s, in1=b, op0=op0, op1=op1), reads=r, writes=w)

    def cp(eng, out, in_, r, w):
        if eng == 'act':
            act(out, in_, AF.Copy, r, w)
        else:
            P.op(eng, lambda e: e.tensor_copy(out=out, in_=in_), reads=r, writes=w)

    def mm(out, lhsT, rhs, start, stop, r, w, inc=True):
        P.op('pe', lambda e: e.matmul(out, lhsT=lhsT, rhs=rhs, start=start, stop=stop), reads=r, writes=w, inc=inc)

    def tr(out, in_, ident, r, w):
        P.op('pe', lambda e: e.transpose(out, in_, ident), reads=r, writes=w)

    def dm(eng, out, in_, chan, r, w):
        P.dma(eng, lambda e: e.dma_start(out=out, in_=in_), chan, reads=r, writes=w)

    def memset(eng, ap, val, w):
        P.op(eng, lambda e: e.memset(ap, val), writes=w)

    def rstd_of(src, skey, n, sq, rstd, tag, ckey=None):
        tl = [(0, min(512, n))] + ([(512, n - 512)] if n > 512 else [])
        for (t0, tn) in tl:
            bk, bkey = P.bank()
            for c in range(16):
                sk = tag + 'sq' + str(c % 2)
                rk = [skey] + ([ckey(c)] if ckey else [])
                act(sq[:, c % 2, t0:t0 + tn], src[:, c, t0:t0 + tn], AF.Square, rk, [sk])
                mm(bk[:, :tn], ones_b, sq[:, c % 2, t0:t0 + tn], c == 0, c == 15, [sk, 'cb1'], [bkey])
            ts('dve', rstd[:, t0:t0 + tn], bk[:, :tn], 1.0 / D, EPS, ALU.mult, ALU.add, [bkey], [tag + 'rs'])
        act(rstd[:, :n], rstd[:, :n], AF.Ln, [tag + 'rs'], [tag + 'rs'])
        act(rstd[:, :n], rstd[:, :n], AF.Exp, [tag + 'rs'], [tag + 'rs'], scale=-0.5)

    def norm_to(out, okey, src, skey, n, gcols, rstd, tag, ckey=None):
        for c in range(16):
            rk = [skey, tag + 'rs', 'gn'] + ([ckey(c)] if ckey else [])
            stt(out[:, c, :n], src[:, c, :n], gcols[:, c:c + 1], rstd[:, :n], ALU.mult, ALU.mult, rk, [okey])

    def stage_prenorm0():
        A.reset()
        xq = A.alloc([128, 16, QW], F32)
        hq = A.alloc([128, 16, QW], BF16)
        sq = A.alloc([128, 2, QW], BF16)
        rs = A.alloc([128, QW], F32)
        for q in range(NQ):
            c0 = q * QW
            dm('sp', xq, xT[:, :, c0:c0 + QW], 'xq', [], ['xq'])
            dm('sp', x_scr[:, :, c0:c0 + QW], xq, 'xst', ['xq'], [('x_scr', q)])
            rstd_of(xq, 'xq', QW, sq, rs, 'n0')
            norm_to(hq, 'hq', xq, 'xq', QW, gn[:, 0, 0, :], rs, 'n0')
            dm('sp', h_scr[:, :, c0:c0 + QW], hq, 'hst', ['hq'], [('h_scr', q)])

    def stage_win(l):
        A.reset()
        hT = A.alloc([128, 16, NT], BF16)
        stg = [A.alloc([128, NT], BF16) for _ in range(2)]
        dm('sp', hT, h_scr, 'hT', [('h_scr', q) for q in range(NQ)], ['hT'])
        blocks = [[(w_in[l, j], 16)] for j in range(NBLK_IN)]

        def epi(j, ti, t0, tn, bk, bkey):
            s = stg[j % 2]
            sk = ('stg', j % 2, ti)
            if j >= 45:
                act(s[:, t0:t0 + tn], bk[:, :tn], AF.Sigmoid, [bkey], [sk])
            elif 25 <= j < 33:
                act(s[:, t0:t0 + tn], bk[:, :tn], AF.Silu, [bkey], [sk])
            else:
                cp('dve' if ti % 2 == 0 else 'act', s[:, t0:t0 + tn], bk[:, :tn], [bkey], [sk])
            if ti == len(TILES_ALL) - 1:
                dm('sp', z_scr[j], s, f'zst{j % 2}', [('stg', j % 2, t) for t in range(len(TILES_ALL))], [('z', j)])
        gemm(blocks, lambda kg, t0, tn: hT[:, kg, t0:t0 + tn], ['hT'], TILES_ALL, epi)

    def stage_s5(l):
        A.reset()
        ub = A.alloc([128, 8, NT], BF16)
        BD = A.alloc([128, 8, 4, 2, 128], BF16)
        CD = A.alloc([128, 8, 4, 2, 128], BF16)
        uni = A.alloc([128, 6656], F32)
        rep = uni[:, 0:2560].rearrange("p (a b) -> p a b", a=5)
        wk = uni[:, 2560:6656].rearrange("p (a b) -> p a b", a=8)
        cmp_ = A.alloc([128, 3, 32], F32)
        cw = A.alloc([128, 12, 32], F32)
        ct = A.alloc([128, 2, 32, 16], F32)
        dd = A.alloc([128, 2, 8], F32)
        x0 = A.alloc([128, 2, 32, NS], F32)
        x1 = A.alloc([128, 2, 32, NS], F32)
        pst = A.alloc([128, 2, 32], F32)
        tpos = A.alloc([128, 512], F32)
        tcs = [A.alloc([128, 512], F32) for _ in range(2)]
        tss = [A.alloc([128, 512], F32) for _ in range(2)]
        targ = A.alloc([128, 512], F32)
        tki = A.alloc([128, 512], F32).bitcast(mybir.dt.int32)
        vini = A.alloc([128, 4], F32)
        magt = A.alloc([128, 512], F32)
        t4 = [A.alloc([128, 512], F32) for _ in range(4)]
        et = [A.alloc([128, 512], F32) for _ in range(2)]
        vv = [[A.alloc([128, 512], F32) for _ in range(2)] for _ in range(2)]
        xx = [[A.alloc([128, 512], BF16) for _ in range(2)] for _ in range(2)]
        pt = [A.alloc([128, 512], F32) for _ in range(2)]
        yacc = A.alloc([128, NT], F32)
        g1 = A.alloc([128, 512], F32)
        g2 = A.alloc([128, 512], F32)
        ystg = [A.alloc([128, NT], BF16) for _ in range(2)]
        dtmp = A.alloc([128, 8], F32)

        dm('sp', ub, z_scr[0:8].rearrange("b p n -> p b n"), 'ub', [('z', j) for j in range(8)], ['ub'])
        dm('sp', rep, s5rep[l], 'rep', [], ['rep'])
        dm('sp', cmp_, s5cm[l], 'cmp', [], ['cmp'])
        dm('sp', ct, s5c[l], 'ct', [], ['ct'])
        dm('sp', dd, s5d[l], 'dd', [], ['dd'])
        dm('sp', x0, s5x0[l], 'x0', [], ['x0'])
        dm('sp', tpos, c_tpos[:, COL0:COL0 + 512], 'tpos', [], ['tpos'])
        memset('pool', BD.rearrange("p a b c d -> p (a b c d)"), 0.0, ['BD'])
        memset('pool', CD.rearrange("p a b c d -> p (a b c d)"), 0.0, ['CD'])
        for par in range(2):
            for ri in range(2):
                memset('dve', xx[par][ri], 0.0, [('xx', par)])

        def sincos(ang, sn, cs, tmp, tmpi, key):
            K = [key]
            ts('dve', tmpi, ang, 1.0 / (2 * PI), None, ALU.mult, None, K, K)
            stt(tmp, tmpi, -2 * PI, ang, ALU.mult, ALU.add, K, K)
            ts('dve', tmp, tmp, PI, -PI, ALU.min, ALU.max, K, K)
            act(sn, tmp, AF.Sin, K, K)
            ts('dve', tmpi, ang, 0.5 * PI, 1.0 / (2 * PI), ALU.add, ALU.mult, K, K)
            stt(tmp, tmpi, -2 * PI, ang, ALU.mult, ALU.add, K, K)
            ts('dve', tmp, tmp, 0.5 * PI, PI, ALU.add, ALU.min, K, K)
            ts('dve', tmp, tmp, -PI, None, ALU.max, None, K, K)
            act(cs, tmp, AF.Sin, K, K)

        lr, li, ls, br, bi = (rep[:, i, :] for i in range(5))
        w = [wk[:, i, :] for i in range(8)]
        R = ['rep', 'wk']
        act(w[0], ls, AF.Exp, ['rep'], ['wk'])
        tt('dve', w[1], lr, w[0], ALU.mult, R, ['wk'])
        act(w[1], w[1], AF.Exp, ['wk'], ['wk'])
        tt('dve', w[2], li, w[0], ALU.mult, R, ['wk'])
        sincos(w[2], w[3], w[4], w[5], w[6].bitcast(mybir.dt.int32), 'wk')
        tt('dve', w[3], w[3], w[1], ALU.mult, R, ['wk'])
        tt('dve', w[4], w[4], w[1], ALU.mult, R, ['wk'])
        ts('dve', w[4], w[4], -1.0, None, ALU.add, None, R, ['wk'])
        tt('dve', w[5], lr, lr, ALU.mult, R, ['wk'])
        tt('dve', w[6], li, li, ALU.mult, R, ['wk'])
        tt('dve', w[5], w[5], w[6], ALU.add, R, ['wk'])
        P.op('dve', lambda e: e.reciprocal(out=w[5], in_=w[5]), reads=R, writes=['wk'])
        tt('dve', w[6], w[4], lr, ALU.mult, R, ['wk'])
        tt('dve', w[7], w[3], li, ALU.mult, R, ['wk'])
        tt('dve', w[6], w[6], w[7], ALU.add, R, ['wk'])
        tt('dve', w[6], w[6], w[5], ALU.mult, R, ['wk'])
        tt('dve', w[7], w[3], lr, ALU.mult, R, ['wk'])
        tt('dve', w[0], w[4], li, ALU.mult, R, ['wk'])
        tt('dve', w[7], w[7], w[0], ALU.subtract, R, ['wk'])
        tt('dve', w[7], w[7], w[5], ALU.mult, R, ['wk'])
        tt('dve', w[0], w[6], br, ALU.mult, R, ['wk'])
        tt('dve', w[1], w[7], bi, ALU.mult, R, ['wk'])
        tt('dve', w[0], w[0], w[1], ALU.subtract, R, ['wk'])
        tt('dve', w[1], w[6], bi, ALU.mult, R, ['wk'])
        tt('dve', w[2], w[7], br, ALU.mult, R, ['wk'])
        tt('dve', w[1], w[1], w[2], ALU.add, R, ['wk'])
        for ri in range(2):
            src = w[ri].rearrange("p (a b) -> p a b", a=8)
            for ccl in range(4):
                for g2_ in range(2):
                    ts('dve', BD[:, :, ccl, ri, g2_ * 64:(g2_ + 1) * 64], src, maskBD[:, ccl * 2 + g2_:ccl * 2 + g2_ + 1], None,
                       ALU.mult, None, ['wk', 'cm', 'BD'], ['BD'])
        c = [cw[:, i, :] for i in range(12)]
        Rc = ['cmp', 'cw']
        act(c[0], cmp_[:, 2, :], AF.Exp, ['cmp'], ['cw'])
        tt('dve', c[1], cmp_[:, 0, :], c[0], ALU.mult, Rc, ['cw'])
        act(c[1], c[1], AF.Exp, ['cw'], ['cw'])
        tt('dve', c[2], cmp_[:, 1, :], c[0], ALU.mult, Rc, ['cw'])
        sincos(c[2], c[3], c[4], c[5], c[6].bitcast(mybir.dt.int32), 'cw')
        tt('dve', c[3], c[3], c[1], ALU.mult, Rc, ['cw'])
        tt('dve', c[4], c[4], c[1], ALU.mult, Rc, ['cw'])
        mag_c, th_c, ai_c, ar_c = c[1], c[2], c[3], c[4]
        c = [cw[:, i, :] for i in range(12)]
        ts('dve', c[7], th_c, 512.0, None, ALU.mult, None, Rc, ['cw'])
        sincos(c[7], c[8], c[9], c[5], c[6].bitcast(mybir.dt.int32), 'cw')
        s512_c, c512_c = c[8], c[9]
        for ri in range(2):
            c5 = ct[:, ri, :, :].rearrange("p (a b) h -> p a b h", a=8)
            for ccl in range(4):
                for g2_ in range(2):
                    ps_ = slice(g2_ * 64, (g2_ + 1) * 64)
                    ts('dve', CD[ps_, :, ccl, ri, 32 * ccl + 16 * g2_:32 * ccl + 16 * g2_ + 16], c5[ps_, :, ccl, :],
                       (1.0 if ri == 0 else -1.0), None, ALU.mult, None, ['ct', 'CD'], ['CD'])

        P.barrier()
        pending = []
        for fc in range(8):
            for ccl in range(4):
                cc = fc * 4 + ccl
                thc = th_c[:, cc:cc + 1]
                nv = NT - COL0
                tp_ = cc % 2
                tc_, tsn = tcs[tp_], tss[tp_]
                tck, tsk = ('tc', tp_), ('tsn', tp_)
                ts('dve', targ, tpos, thc, None, ALU.mult, None, ['tpos', 'cw', 'targ'], ['targ'])
                ts('dve', tki, targ, 1.0 / (2 * PI), None, ALU.mult, None, ['targ', 'tki'], ['tki'])
                stt(tsn, tki, -2 * PI, targ, ALU.mult, ALU.add, ['tki', 'targ', tsk], [tsk])
                ts('dve', tsn, tsn, PI, -PI, ALU.min, ALU.max, [tsk], [tsk])
                act(tsn, tsn, AF.Sin, [tsk], [tsk])
                ts('dve', tki, targ, 0.5 * PI, 1.0 / (2 * PI), ALU.add, ALU.mult, ['targ', 'tki'], ['tki'])
                stt(tc_, tki, -2 * PI, targ, ALU.mult, ALU.add, ['tki', 'targ', tck], [tck])
                ts('dve', tc_, tc_, 0.5 * PI, PI, ALU.add, ALU.min, [tck], [tck])
                ts('dve', tc_, tc_, -PI, None, ALU.max, None, [tck], [tck])
                act(tc_, tc_, AF.Sin, [tck], [tck])
                ts('dve', magt, tpos, 0.0, mag_c[:, cc:cc + 1], ALU.mult, ALU.add, ['tpos', 'cw', 'magt'], ['magt'])
                for ti, (t0, tn) in enumerate(TILES_ALL):
                    par = ti % 2
                    be, bek = P.bank()
                    bi_, bik = P.bank()
                    mm(be[:, :tn], BD[:, fc, ccl, 0, :], ub[:, fc, t0:t0 + tn], True, True, ['BD', 'ub'], [bek])
                    mm(bi_[:, :tn], BD[:, fc, ccl, 1, :], ub[:, fc, t0:t0 + tn], True, True, ['BD', 'ub'], [bik])
                    cs_ = tc_[:, 0:tn]
                    sn_ = tsn[:, 0:tn]
                    er, ei = be[:, :tn], bi_[:, :tn]
                    tt('dve', t4[0][:, :tn], er, cs_, ALU.mult, [bek, tck], ['t40'])
                    tt('dve', t4[1][:, :tn], ei, sn_, ALU.mult, [bik, tsk], ['t41'])
                    tt('dve', et[0][:, :tn], t4[0][:, :tn], t4[1][:, :tn], ALU.add, ['t40', 't41'], ['et0'])
                    tt('dve', t4[2][:, :tn], ei, cs_, ALU.mult, [bik, tck], ['t42'])
                    tt('dve', t4[3][:, :tn], er, sn_, ALU.mult, [bek, tsk], ['t43'])
                    tt('dve', et[1][:, :tn], t4[2][:, :tn], t4[3][:, :tn], ALU.subtract, ['t42', 't43'], ['et1'])
                    if ti == 0:
                        memset('dve', et[0][:, 0:COL0], 0.0, ['et0'])
                        memset('dve', et[1][:, 0:COL0], 0.0, ['et1'])
                    else:
                        vrL, viL = vv[1 - par][0][:, 511:512], vv[1 - par][1][:, 511:512]
                        s5c_, c5c_ = s512_c[:, cc:cc + 1], c512_c[:, cc:cc + 1]
                        RK = [('vv', 1 - par, 0), ('vv', 1 - par, 1), 'cw', 'vini']
                        ts('dve', vini[:, 2:3], viL, s5c_, None, ALU.mult, None, RK, ['vini'])
                        stt(vini[:, 0:1], vrL, c5c_, vini[:, 2:3], ALU.mult, ALU.subtract, RK, ['vini'])
                        ts('dve', vini[:, 3:4], vrL, s5c_, None, ALU.mult, None, RK, ['vini'])
                        stt(vini[:, 1:2], viL, c5c_, vini[:, 3:4], ALU.mult, ALU.add, RK, ['vini'])
                    for ri in range(2):
                        vcur = vv[par][ri]
                        if ti == 0:
                            P.op('dve', lambda e, vcur=vcur, ri=ri, tn=tn: e.tensor_tensor_scan(out=vcur[:, :tn], data0=magt[:, :tn], data1=et[ri][:, :tn], initial=0.0, op0=ALU.mult, op1=ALU.add),
                                 reads=['magt', f'et{ri}'], writes=[('vv', par, ri)])
                        else:
                            vprev = vv[1 - par][ri]
                            P.op('dve', lambda e, vcur=vcur, vprev=vprev, ri=ri, tn=tn: e.tensor_tensor_scan(out=vcur[:, :tn], data0=magt[:, :tn], data1=et[ri][:, :tn], initial=vini[:, ri:ri + 1], op0=ALU.mult, op1=ALU.add),
                                 reads=['magt', f'et{ri}', 'vini'], writes=[('vv', par, ri)])
                    while pending:
                        pending.pop(0)()
                    vr, vi = vv[par][0][:, :tn], vv[par][1][:, :tn]
                    xr, xi = xx[par][0], xx[par][1]
                    lo = 0
                    if ti == 0:
                        lo = COL0
                    xk = ('xx', par)
                    tt('pool', pt[0][:, lo:tn], vr[:, lo:tn], cs_[:, lo:tn], ALU.mult, [('vv', par, 0), tck], ['pt0'])
                    tt('pool', pt[1][:, lo:tn], vi[:, lo:tn], sn_[:, lo:tn], ALU.mult, [('vv', par, 1), tsk], ['pt1'])
                    tt('pool', xr[:, lo:tn], pt[0][:, lo:tn], pt[1][:, lo:tn], ALU.subtract, ['pt0', 'pt1'], [xk])
                    tt('pool', pt[0][:, lo:tn], vr[:, lo:tn], sn_[:, lo:tn], ALU.mult, [('vv', par, 0), tsk, 'pt0'], ['pt0'])
                    tt('pool', pt[1][:, lo:tn], vi[:, lo:tn], cs_[:, lo:tn], ALU.mult, [('vv', par, 1), tck, 'pt1'], ['pt1'])
                    tt('pool', xi[:, lo:tn], pt[0][:, lo:tn], pt[1][:, lo:tn], ALU.add, ['pt0', 'pt1'], [xk])
                    if ti == 0:
                        arc, aic = ar_c[:, cc:cc + 1], ai_c[:, cc:cc + 1]
                        x0r, x0i = x0[:, 0, cc, :], x0[:, 1, cc, :]
                        ts('dve', dtmp[:, 0:4], x0i, aic, None, ALU.mult, None, ['x0', 'cw'], ['dtmp'])
                        stt(dtmp[:, 4:8], x0r, arc, dtmp[:, 0:4], ALU.mult, ALU.subtract, ['x0', 'cw', 'dtmp'], ['dtmp'])
                        tt('dve', x1[:, 0, cc, :], dtmp[:, 4:8], er[:, 0:NS], ALU.add, ['dtmp', bek], ['x1'])
                        ts('dve', dtmp[:, 0:4], x0r, aic, None, ALU.mult, None, ['x0', 'cw', 'dtmp'], ['dtmp'])
                        stt(dtmp[:, 4:8], x0i, arc, dtmp[:, 0:4], ALU.mult, ALU.add, ['x0', 'cw', 'dtmp'], ['dtmp'])
                        tt('dve', x1[:, 1, cc, :], dtmp[:, 4:8], ei[:, 0:NS], ALU.add, ['dtmp', bik], ['x1'])
                        cp('dve', xr[:, 0:NS], x1[:, 0, cc, :], ['x1'], [xk])
                        cp('dve', xi[:, 0:NS], x1[:, 1, cc, :], ['x1'], [xk])
                    if ti == len(TILES_ALL) - 1:
                        la = tn - 1
                        tt('dve', dtmp[:, 0:1], vr[:, la:la + 1], cs_[:, la:la + 1], ALU.mult, [('vv', par, 0), tck, 'dtmp'], ['dtmp'])
                        tt('dve', dtmp[:, 1:2], vi[:, la:la + 1], sn_[:, la:la + 1], ALU.mult, [('vv', par, 1), tsk, 'dtmp'], ['dtmp'])
                        tt('dve', pst[:, 0, cc:cc + 1], dtmp[:, 0:1], dtmp[:, 1:2], ALU.subtract, ['dtmp'], ['pst'])
                        tt('dve', dtmp[:, 2:3], vr[:, la:la + 1], sn_[:, la:la + 1], ALU.mult, [('vv', par, 0), tsk, 'dtmp'], ['dtmp'])
                        tt('dve', dtmp[:, 3:4], vi[:, la:la + 1], cs_[:, la:la + 1], ALU.mult, [('vv', par, 1), tck, 'dtmp'], ['dtmp'])
                        tt('dve', pst[:, 1, cc:cc + 1], dtmp[:, 2:3], dtmp[:, 3:4], ALU.add, ['dtmp'], ['pst'])
                    def cproj(fc=fc, ccl=ccl, ti=ti, t0=t0, tn=tn, xr=xr, xi=xi, xk=xk):
                        by, byk = P.bank()
                        mm(by[:, :tn], CD[:, fc, ccl, 0, :], xr[:, :tn], True, False, ['CD', xk], [byk], inc=False)
                        mm(by[:, :tn], CD[:, fc, ccl, 1, :], xi[:, :tn], False, True, ['CD', xk], [byk])
                        if ccl == 0:
                            cp('act', yacc[:, t0:t0 + tn], by[:, :tn], [byk], [('yacc', ti)])
                        else:
                            tt('dve', yacc[:, t0:t0 + tn], yacc[:, t0:t0 + tn], by[:, :tn], ALU.add, [byk, ('yacc', ti)], [('yacc', ti)])
                    pending.append(cproj)
            while pending:
                pending.pop(0)()
            for ti, (t0, tn) in enumerate(TILES_ALL):
                yk = ('yacc', ti)
                ysl = yacc[:, t0:t0 + tn]
                stt(ysl, ub[:, fc, t0:t0 + tn], dd[:, 0, fc:fc + 1], ysl, ALU.mult, ALU.add, ['ub', 'dd', yk], [yk])
                act(g1[:, :tn], ysl, AF.Square, [yk], ['g1'])
                ts('dve', g1[:, :tn], g1[:, :tn], 0.044715, 1.0, ALU.mult, ALU.add, ['g1'], ['g1'])
                tt('dve', g1[:, :tn], g1[:, :tn], ysl, ALU.mult, ['g1', yk], ['g1'])
                act(g2[:, :tn], g1[:, :tn], AF.Sigmoid, ['g1'], ['g2'], scale=2.0 * math.sqrt(2.0 / PI))
                tt('dve', ub[:, fc, t0:t0 + tn], ysl, g2[:, :tn], ALU.mult, [yk, 'g2'], ['ub'])
        dm('sp', o_ps5[l], pst, 'ops5', ['pst'], [])
        dm('sp', o_ss5[l], x1, 'oss5', ['x1'], [])
        blocks = [[(w_glu[l, j], 8)] for j in range(8)]

        def epi(j, ti, t0, tn, bk, bkey):
            s = ystg[j % 2]
            sk = ('ystg', j % 2, ti)
            act(g1[:, :tn], bk[:, :tn], AF.Sigmoid, [bkey, 'dd'], ['g1'], bias=dd[:, 1, j:j + 1])
            tt('dve', s[:, t0:t0 + tn], ub[:, j, t0:t0 + tn], g1[:, :tn], ALU.mult, ['ub', 'g1'], [sk])
            if ti == len(TILES_ALL) - 1:
                dm('sp', y_scr[j], s, f'yst{j % 2}', [('ystg', j % 2, t) for t in range(len(TILES_ALL))], [('y', j)])
        gemm(blocks, lambda kg, t0, tn: ub[:, kg, t0:t0 + tn], ['ub'], TILES_ALL, epi)

    def stage_gla(l):
        A.reset()
        aT = A.alloc([128, NT], BF16)
        w2 = A.alloc([128, 512], BF16)
        gp = A.alloc([128, 6], F32)
        nbg = A.alloc([128, 4], F32)
        rmask = A.alloc([128, NT], F32)
        qT = A.alloc([128, NT], BF16)
        kT = A.alloc([128, NT], BF16)
        vT = A.alloc([128, 2, NT], BF16)
        rT = A.alloc([128, 2, NT], BF16)
        G = A.alloc([128, NT], F32)
        B = A.alloc([128, NT], F32)
        E1 = A.alloc([128, NT], F32)
        E2 = A.alloc([128, NT], F32)
        qs = A.alloc([128, NT], BF16)
        ks = A.alloc([128, NT], BF16)
        ku = A.alloc([128, NT], BF16)
        cd = A.alloc([128, 17], F32)
        OT = A.alloc([128, 2, NT], F32)
        vtm = [A.alloc([128, 256], BF16) for _ in range(2)]
        kutm = [A.alloc([128, 128], BF16) for _ in range(2)]
        ATb = [A.alloc([128, 128], BF16) for _ in range(2)]
        Sf = A.alloc([128, 256], F32)
        Sb = A.alloc([128, 256], BF16)
        S0 = [A.alloc([128, 256], F32) for _ in range(2)]
        S1b = A.alloc([128, 256], BF16)
        krtm = A.alloc([128, 128], BF16)
        vrtm = A.alloc([128, 256], BF16)
        kmask = A.alloc([128, 128], BF16)
        gdec = A.alloc([128, NS], F32)
        sq = A.alloc([128, 512], BF16)
        rs = A.alloc([128, 512], F32)
        ystg = [A.alloc([128, NT], BF16) for _ in range(2)]
        scale = 128.0 ** -0.5

        dm('sp', aT[0:16, :], z_scr[24, 0:16, :], 'aT', [('z', 24)], ['aT'])
        dm('pool', w2[0:16, :], gla_w2[l], 'w2', [], ['w2'])
        dm('sp', gp, gla_p[l], 'gp', [], ['gp'])
        dm('sp', rmask, c_rmask, 'rmask', [], ['rmask'])
        ts('dve', nbg, gp[:, 0:4], -1.0, None, ALU.mult, None, ['gp'], ['nbg'])
        for h in range(4):
            dm('sp', qT, z_scr[8 + h], 'qT', [('z', 8 + h)], ['qT'])
            dm('sp', kT, z_scr[12 + h], 'kT', [('z', 12 + h)], ['kT'])
            dm('sp', vT, z_scr[16 + 2 * h:18 + 2 * h].rearrange("b p n -> p b n"), 'vT', [('z', 16 + 2 * h), ('z', 17 + 2 * h)], ['vT'])
            dm('sp', rT, z_scr[25 + 2 * h:27 + 2 * h].rearrange("b p n -> p b n"), 'rT', [('z', 25 + 2 * h), ('z', 26 + 2 * h)], ['rT'])
            for (t0, tn) in TILES_ALL:
                bk, bkey = P.bank()
                mm(bk[:, :tn], w2[0:16, h * 128:(h + 1) * 128], aT[0:16, t0:t0 + tn], True, True, ['w2', 'aT'], [bkey])
                act(G[:, t0:t0 + tn], bk[:, :tn], AF.Exp, [bkey, 'nbg'], ['G'], scale=-1.0, bias=nbg[:, h:h + 1])
            act(G, G, AF.Ln, ['G'], ['G'], bias=1.0)
            ts('dve', G, G, 1.0 / 16.0, None, ALU.mult, None, ['G'], ['G'])
            P.op('dve', lambda e: e.tensor_tensor_scan(out=B, data0=rmask, data1=G, initial=0.0, op0=ALU.mult, op1=ALU.add), reads=['rmask', 'G'], writes=['B'])
            act(E1, B, AF.Exp, ['B'], ['E1'], scale=-1.0)
            act(E2, B, AF.Exp, ['B'], ['E2'])
            stt(qs, qT, scale, E1, ALU.mult, ALU.mult, ['qT', 'E1'], ['qs'])
            tt('dve', ks, kT, E2, ALU.mult, ['kT', 'E2'], ['ks'])
            act(gdec, G[:, 0:NS], AF.Exp, ['G'], ['gdec'], scale=-1.0)
            B3 = B.rearrange("p (c n) -> p c n", n=128)
            E13 = E1.rearrange("p (c n) -> p c n", n=128)
            cp('dve', cd, E13[:, :, 127], ['E1'], ['cd'])
            tt('dve', E2.rearrange("p (c n) -> p c n", n=128), B3, B3[:, :, 127:128].broadcast_to([128, 17, 128]), ALU.subtract, ['B', 'ks'], ['E2'])
            act(E2, E2, AF.Exp, ['E2'], ['E2'])
            tt('dve', ku, kT, E2, ALU.mult, ['kT', 'E2'], ['ku'])
            memset('dve', ks[:, 0:COL0], 0.0, ['ks'])
            memset('dve', ku[:, 0:COL0], 0.0, ['ku'])
            memset('dve', Sf, 0.0, ['Sf'])
            memset('dve', Sb, 0.0, ['Sb'])
            def gla_A(c):
                cs_ = slice(c * 128, (c + 1) * 128)
                par = c % 2
                for vc in range(2):
                    bk, bkey = P.bank()
                    bb = bk.bitcast(BF16)
                    tr(bb[:, 0:128], vT[:, vc, cs_], ident_b, ['vT', 'cb'], [bkey])
                    cp('act', vtm[par][:, vc * 128:(vc + 1) * 128], bb[:, 0:128], [bkey], [('vtm', par)])
                bk, bkey = P.bank()
                bb = bk.bitcast(BF16)
                tr(bb[:, 0:128], ku[:, cs_], ident_b, ['ku', 'cb'], [bkey])
                cp('act', kutm[par], bb[:, 0:128], [bkey], [('kutm', par)])
                ba, bak = P.bank()
                mm(ba[:, 0:128], ks[:, cs_], qs[:, cs_], True, True, ['ks', 'qs'], [bak])
                tt('dve', ATb[par], ba[:, 0:128], tri_b, ALU.mult, [bak, 'cb'], [('ATb', par)])

            def gla_B(c):
                cs_ = slice(c * 128, (c + 1) * 128)
                par = c % 2
                for vc in range(2):
                    bo, bok = P.bank()
                    mm(bo[:, 0:128], vtm[par][:, vc * 128:(vc + 1) * 128], ATb[par], True, False, [('vtm', par), ('ATb', par)], [bok], inc=False)
                    mm(bo[:, 0:128], Sb[:, vc * 128:(vc + 1) * 128], qs[:, cs_], False, True, ['Sb', 'qs'], [bok])
                    cp('act', OT[:, vc, cs_], bo[:, 0:128], [bok], ['OT'])
                bs, bsk = P.bank()
                mm(bs[:, 0:256], kutm[par], vtm[par], True, True, [('kutm', par), ('vtm', par)], [bsk])
                stt(Sf, Sf, cd[:, c:c + 1], bs[:, 0:256], ALU.mult, ALU.add, ['Sf', 'cd', bsk], ['Sf'])
                cp('dve', Sb, Sf, ['Sf'], ['Sb'])

            gla_A(0)
            for c in range(17):
                if c + 1 < 17:
                    gla_A(c + 1)
                gla_B(c)
            dm('sp', o_pgla[l, h], Sf, 'opgla', ['Sf'], [])
            bk, bkey = P.bank()
            bb = bk.bitcast(BF16)
            tr(bb[:, 0:128], kT[:, 0:128], ident_b, ['kT', 'cb'], [bkey])
            cp('act', krtm, bb[:, 0:128], [bkey], ['krtm'])
            for vc in range(2):
                bk, bkey = P.bank()
                bb = bk.bitcast(BF16)
                tr(bb[:, 0:128], vT[:, vc, 0:128], ident_b, ['vT', 'cb'], [bkey])
                cp('act', vrtm[:, vc * 128:(vc + 1) * 128], bb[:, 0:128], [bkey], ['vrtm'])
            for n in range(NS):
                s0 = S0[n % 2]
                s0k = ('S0', n % 2)
                dm('sp', s0, gla_s0[l, n, h], f's0{n % 2}', [], [s0k])
                ts('dve', kmask[0:4, :], krtm[0:4, :], I4[0:4, n:n + 1], None, ALU.mult, None, ['krtm', 'cm'], ['kmask'])
                bs, bsk = P.bank()
                mm(bs[:, 0:256], kmask[0:4, :], vrtm[0:4, :], True, True, ['kmask', 'vrtm'], [bsk])
                stt(s0, s0, gdec[:, n:n + 1], bs[:, 0:256], ALU.mult, ALU.add, [s0k, 'gdec', bsk], [s0k])
                cp('dve', S1b, s0, [s0k], ['S1b'])
                dm('sp', o_sgla[l, n, h], s0, f's1{n % 2}', [s0k], [])
                for vc in range(2):
                    bo, bok = P.bank()
                    mm(bo[:, 0:1], S1b[:, vc * 128:(vc + 1) * 128], qT[:, n:n + 1], True, True, ['S1b', 'qT'], [bok])
                    act(OT[:, vc, n:n + 1], bo[:, 0:1], AF.Copy, [bok], ['OT'], scale=scale)
            for ti, (t0, tn) in enumerate(TILES_ALL):
                bk, bkey = P.bank()
                for vc in range(2):
                    act(sq[:, :tn], OT[:, vc, t0:t0 + tn], AF.Square, ['OT'], ['gsq'])
                    mm(bk[:, :tn], ones_b, sq[:, :tn], vc == 0, vc == 1, ['gsq', 'cb1'], [bkey])
                ts('dve', rs[:, :tn], bk[:, :tn], 1.0 / 256.0, EPS, ALU.mult, ALU.add, [bkey], ['grs'])
                act(rs[:, :tn], rs[:, :tn], AF.Ln, ['grs'], ['grs'])
                act(rs[:, :tn], rs[:, :tn], AF.Exp, ['grs'], ['grs'], scale=-0.5)
                for vc in range(2):
                    j = 2 * h + vc
                    s = ystg[j % 2]
                    sk = ('ystg', j % 2, ti)
                    stt(OT[:, vc, t0:t0 + tn], OT[:, vc, t0:t0 + tn], gp[:, 4 + vc:5 + vc], rs[:, :tn], ALU.mult, ALU.mult, ['OT', 'gp', 'grs'], ['OT'])
                    tt('dve', s[:, t0:t0 + tn], OT[:, vc, t0:t0 + tn], rT[:, vc, t0:t0 + tn], ALU.mult, ['OT', 'rT'], [sk])
            for vc in range(2):
                j = 2 * h + vc
                dm('sp', y_scr[8 + j], ystg[j % 2], f'yst{j % 2}', [('ystg', j % 2, t) for t in range(len(TILES_ALL))], [('y', 8 + j)])

    def stage_swa(l, first):
        A.reset()
        qT = A.alloc([128, 8, NT], BF16)
        kT = A.alloc([128, 2, NT], BF16)
        vT = A.alloc([128, 2, NT], BF16)
        vtm = A.alloc([128, 17, 256], BF16)
        yc = A.alloc([128, 8, NT], BF16)
        biasT = A.alloc([128, 2, 16, 128], BF16)
        bdec = A.alloc([128, 16], BF16)
        bnew = A.alloc([128, 16], F32)
        rb33 = A.alloc([128, 16], F32)
        oh = A.alloc([128, 2 * 256 + 128 + 4], F32)
        Gt = A.alloc([128, 2, 16, 128], F32)
        Gb = A.alloc([128, 2, 16, 128], BF16)
        estg = A.alloc([128, 2, 256], F32)
        padc = A.alloc([128, 2], F32)
        sk_ = A.alloc([128, 16], F32)
        esk = A.alloc([128, 16, 128], F32)
        PT = [A.alloc([128, 512], BF16) for _ in range(6)]
        dens = [A.alloc([128, 512], F32) for _ in range(3)]
        kc_ = A.alloc([128, NS, 2, 128], BF16)
        vc_ = A.alloc([128, NS, 256], BF16)
        pd = A.alloc([128, 4], BF16)
        pn = A.alloc([128, 4], BF16)
        pn2 = A.alloc([128, 4], F32)
        d4 = A.alloc([128, 4], F32)
        ktm = A.alloc([128, 256], F32)
        vlast = A.alloc([128, 256], F32)
        k0tm = A.alloc([128, 256], F32)
        v0tm = A.alloc([128, 256], F32)

        dm('sp', qT, z_scr[33:41].rearrange("b p n -> p b n"), 'qT', [('z', j) for j in range(33, 41)], ['qT'])
        dm('sp', kT, z_scr[41:43].rearrange("b p n -> p b n"), 'kT', [('z', 41), ('z', 42)], ['kT'])
        dm('sp', vT, z_scr[43:45].rearrange("b p n -> p b n"), 'vT', [('z', 43), ('z', 44)], ['vT'])
        dm('sp', padc, c_pad, 'padc', [], ['padc'])
        memset('pool', yc[:, :, 0:COL0], 0.0, ['yc'])
        dm('sp', sk_[0:64, :], swa_sink[l], 'sk', [], ['sk'])
        dm('pool', kc_, swa_kT[l], 'kc', [], ['kc'])
        dm('pool', vc_, swa_vn[l].rearrange("n s f -> s n f"), 'vc', [], ['vc'])
        dm('sp', rb33[0:32, :], relb, 'rb', [], ['rb33'])
        memset('dve', rb33[32:33, :], NEG, ['rb33m'])
        dm('sp', oh[0:33, :], c_oh, 'oh', [], ['oh'])
        for rel in range(2):
            bk, bkey = P.bank()
            mm(bk[0:16, 0:256], rb33[0:33, :], oh[0:33, rel * 256:(rel + 1) * 256], True, True, ['rb33', 'rb33m', 'oh'], [bkey])
            act(estg[0:16, rel, :], bk[0:16, 0:256], AF.Copy, [bkey], ['estg'], scale=8.0)
        dm('sp', e_scr.rearrange("r h j -> h r j"), estg[0:16, :, :], 'est', ['estg'], ['e_scr'])
        for rel in range(2):
            src = bass.AP(tensor=e_scr.tensor, offset=rel * 16 * 256, ap=[[1, 128], [256, 16], [1, 128]])
            dm('sp', Gt[:, rel, :, :], src, f'Gt{rel}', ['e_scr'], [('Gt', rel)])
        cp('dve', Gb.rearrange("p a b c -> p (a b c)"), Gt.rearrange("p a b c -> p (a b c)"), [('Gt', 0), ('Gt', 1)], ['Gb'])
        Gb2 = Gb.rearrange("p a b c -> p (a b c)")
        bT2 = biasT.rearrange("p a b c -> p (a b c)")
        for i in range(8):
            bk, bkey = P.bank()
            mm(bk[:, :], J_b, Gb2[:, i * 512:(i + 1) * 512], True, True, ['Gb', 'cb'], [bkey])
            cp('act', bT2[:, i * 512:(i + 1) * 512], bk[:, :], [bkey], ['biasT'])
        bk, bkey = P.bank()
        mm(bk[:, 0:16], oh[0:33, 512:640], rb33[0:33, :], True, True, ['rb33', 'rb33m', 'oh'], [bkey])
        act(bdec, bk[:, 0:16], AF.Copy, [bkey], ['bdec'], scale=8.0)
        bk, bkey = P.bank()
        mm(bk[0:4, 0:16], oh[0:33, 640:644], rb33[0:33, :], True, True, ['rb33', 'rb33m', 'oh'], [bkey])
        act(bnew[0:4, :], bk[0:4, 0:16], AF.Copy, [bkey], ['bnew'], scale=8.0)
        act(sk_[0:64, :], sk_[0:64, :], AF.Exp, ['sk'], ['sk'])
        cp('dve', esk[0:64, :, :], sk_[0:64, :].unsqueeze(2).broadcast_to([64, 16, 128]), ['sk'], ['esk'])
        for b in range(17):
            for c in range(2):
                bk, bkey = P.bank()
                bb = bk.bitcast(BF16)
                tr(bb[:, 0:128], vT[:, c, b * 128:(b + 1) * 128], ident_b, ['vT', 'cb'], [bkey])
                cp('act' if c == 0 else 'dve', vtm[:, b, c * 128:(c + 1) * 128], bb[:, 0:128], [bkey], ['vtm'])
        pti = [0]

        def swa_A(b, kvh):
            qs_ = slice(b * 128, (b + 1) * 128)
            kvp, po = kvh // 2, 64 * (kvh % 2)
            prt = slice(po, po + 64)
            kbs = [b] if b == 0 else [b - 1, b]
            plist = []
            for ki, kb in enumerate(kbs):
                rel = 1 if kb == b else 0
                bs, bsk = P.bank()
                mm(bs[:, :], kT[prt, kvp, kb * 128:(kb + 1) * 128], qT[prt, kvp * 4:kvp * 4 + 4, qs_], True, False, ['kT', 'qT'], [bsk], inc=False)
                mm(bs[:, :], ident_b, biasT[:, rel, kvh * 4:kvh * 4 + 4, :], False, True, ['cb', 'biasT'], [bsk])
                p_ = PT[pti[0] % 6]
                pk = ('PT', pti[0] % 6)
                pti[0] += 1
                act(p_, bs[:, :], AF.Exp, [bsk, 'padc'], [pk], scale=0.125, bias=padc[:, (1 if kb == 0 else 0):(2 if kb == 0 else 1)])
                plist.append((p_, pk, kb))
            return plist

        def swa_B(b, kvh, plist):
            kvp, po = kvh // 2, 64 * (kvh % 2)
            prt = slice(po, po + 64)
            bo, bok = P.bank()
            bd, bdk = P.bank()
            for ki, (p_, pk, kb) in enumerate(plist):
                mm(bo[0:64, :], vtm[:, kb, kvh * 64:(kvh + 1) * 64], p_, ki == 0, ki == len(plist) - 1, ['vtm', pk], [bok])
                mm(bd[0:64, :], ones_b[:, 0:64], p_, ki == 0, ki == len(plist) - 1, ['cb1', pk], [bdk])
            di = (b * 4 + kvh) % 3
            den, dnk = dens[di], ('den', di)
            tt('dve', den[0:64, :], bd[0:64, :], esk[0:64, kvh * 4:kvh * 4 + 4, :], ALU.add, [bdk, 'esk'], [dnk])
            act(den[0:64, :], den[0:64, :], AF.Ln, [dnk], [dnk])
            act(den[0:64, :], den[0:64, :], AF.Exp, [dnk], [dnk], scale=-1.0)
            lo = COL0 if b == 0 else 0
            tt('dve', yc[prt, kvp * 4:kvp * 4 + 4, b * 128 + lo:(b + 1) * 128], bo[0:64, :].rearrange("p (g q) -> p g q", g=4)[:, :, lo:],
               den[0:64, :].rearrange("p (g q) -> p g q", g=4)[:, :, lo:], ALU.mult, [bok, dnk], ['yc'])

        its = [(b, kvh) for b in range(17) for kvh in range(4)]
        nxt = swa_A(*its[0])
        for i, (b, kvh) in enumerate(its):
            cur = nxt
            if i + 1 < len(its):
                nxt = swa_A(*its[i + 1])
            swa_B(b, kvh, cur)
        for c in range(2):
            for (blk, dst, dk_) in ((16, ktm, 'ktm'), (0, k0tm, 'k0tm')):
                bk, bkey = P.bank()
                bb = bk.bitcast(BF16)
                tr(bb[:, 0:128], kT[:, c, blk * 128:(blk + 1) * 128], ident_b, ['kT', 'cb'], [bkey])
                cp('act', dst[:, c * 128:(c + 1) * 128], bb[:, 0:128], [bkey], [dk_])
        cp('dve', vlast, vtm[:, 16, :], ['vtm'], ['vlast'])
        cp('dve', v0tm, vtm[:, 0, :], ['vtm'], ['v0tm'])
        dm('sp', o_pk[l], ktm, 'opk', ['ktm'], [])
        dm('sp', o_pv[l], vlast, 'opv', ['vlast'], [])
        for n in range(NS):
            dm('sp', o_sk[l, n, 0:127, :], swa_kn[l, n, 1:128, :], 'osk', [], [])
            dm('sp', o_sv[l, n, 0:127, :], swa_vn[l, n, 1:128, :], 'osv', [], [])
        dm('sp', o_sk[l, :, 127, :], k0tm[0:NS, :], 'osk2', ['k0tm'], [])
        dm('sp', o_sv[l, :, 127, :], v0tm[0:NS, :], 'osv2', ['v0tm'], [])
        for n in range(NS):
            for kvh in range(4):
                kvp, po = kvh // 2, 64 * (kvh % 2)
                prt = slice(po, po + 64)
                qn = qT[prt, kvp * 4:kvp * 4 + 4, n]
                bs, bsk = P.bank()
                mm(bs[:, 0:4], kc_[prt, n, kvp, :], qn, True, False, ['kc', 'qT'], [bsk], inc=False)
                mm(bs[:, 0:4], ident_b, bdec[:, kvh * 4:kvh * 4 + 4], False, True, ['cb', 'bdec'], [bsk])
                act(pd, bs[:, 0:4], AF.Exp, [bsk], ['pd'], scale=0.125)
                bn, bnk = P.bank()
                mm(bn[0:4, 0:4], kT[prt, kvp, 0:NS], qn, True, True, ['kT', 'qT'], [bnk])
                tt('dve', pn2[0:4, :], bn[0:4, 0:4], bnew[0:4, kvh * 4:kvh * 4 + 4], ALU.add, [bnk, 'bnew'], ['pn2'])
                act(pn2[0:4, :], pn2[0:4, :], AF.Exp, ['pn2'], ['pn2'], scale=0.125)
                ts('dve', pn[0:4, :], pn2[0:4, :], I4[0:4, n:n + 1], None, ALU.mult, None, ['pn2', 'cm'], ['pn'])
                bo, bok = P.bank()
                mm(bo[0:64, 0:4], vc_[:, n, kvh * 64:(kvh + 1) * 64], pd, True, False, ['vc', 'pd'], [bok], inc=False)
                mm(bo[0:64, 0:4], vtm[0:4, 0, kvh * 64:(kvh + 1) * 64], pn[0:4, :], False, True, ['vtm', 'pn'], [bok])
                bd, bdk = P.bank()
                mm(bd[0:64, 0:4], ones_b[:, 0:64], pd, True, False, ['cb1', 'pd'], [bdk], inc=False)
                mm(bd[0:64, 0:4], ones_b[0:4, 0:64], pn[0:4, :], False, True, ['cb1', 'pn'], [bdk])
                tt('dve', d4[0:64, :], bd[0:64, 0:4], sk_[0:64, kvh * 4:kvh * 4 + 4], ALU.add, [bdk, 'sk'], ['d4'])
                P.op('dve', lambda e: e.reciprocal(out=d4[0:64, :], in_=d4[0:64, :]), reads=['d4'], writes=['d4'])
                tt('dve', yc[prt, kvp * 4:kvp * 4 + 4, n], bo[0:64, 0:4], d4[0:64, :], ALU.mult, [bok, 'd4'], ['yc'])
        for j in range(8):
            dm('sp', y_scr[16 + j], yc[:, j, :], 'ycst', ['yc'], [('y', 16 + j)])

    def stage_merge(l):
        A.reset()
        yq = A.alloc([128, 24, QW], BF16)
        gq = [A.alloc([128, 3, QW], BF16) for _ in range(2)]
        macc = A.alloc([128, QW], F32)
        mtmp = A.alloc([128, QW], F32)
        mq = A.alloc([128, 16, QW], BF16)
        mix = A.alloc([128, 16, QW], F32)
        xq = A.alloc([128, 16, QW], F32)
        hq = A.alloc([128, 16, QW], BF16)
        sq = A.alloc([128, 2, QW], BF16)
        rs = A.alloc([128, QW], F32)
        def load_yq(q):
            cq_ = slice(q * QW, q * QW + QW)
            dm('sp', yq, y_scr[:, :, cq_].rearrange("b p n -> p b n"), 'yq', [('y', j) for j in range(24)], ['yq'])
        load_yq(0)
        xck = lambda c: ('xqc', c)
        xall = ['xq'] + [('xqc', c) for c in range(16)]
        for q in range(NQ):
            c0 = q * QW
            cq = slice(c0, c0 + QW)
            dm('sp', xq, x_scr[:, :, cq], 'xq', [('x_scr', q)], ['xq'])
            blocks = []
            for j in range(16):
                for br in range(3):
                    blocks.append([(w_up[l, br, j], 8)])

            def epi(jj, ti, t0, tn, bk, bkey, q=q, cq=cq, c0=c0):
                j, br = jj // 3, jj % 3
                g = gq[j % 2]
                gk = ('gq', j % 2)
                if br == 0 and ti == 0:
                    src = bass.AP(tensor=z_scr.tensor, offset=(45 + j) * 128 * NT + c0, ap=[[NT, 128], [16 * 128 * NT, 3], [1, QW]])
                    dm('sp', g, src, f'gq{j % 2}', [('z', 45 + j), ('z', 61 + j), ('z', 77 + j)], [gk])
                sl = slice(t0, t0 + tn)
                if br == 0:
                    tt('dve', macc[:, sl], bk[:, :tn], g[:, 0, sl], ALU.mult, [bkey, gk], [('macc', ti)])
                else:
                    tt('dve', mtmp[:, sl], bk[:, :tn], g[:, br, sl], ALU.mult, [bkey, gk], [('mtmp', ti)])
                    if br == 1:
                        tt('dve', macc[:, sl], macc[:, sl], mtmp[:, sl], ALU.add, [('macc', ti), ('mtmp', ti)], [('macc', ti)])
                    else:
                        tt('dve', mq[:, j, sl], macc[:, sl], mtmp[:, sl], ALU.add, [('macc', ti), ('mtmp', ti)], ['mq'])
            gemm_blocks_rhs(blocks, [(lambda kg, t0, tn, br=jj % 3: yq[:, br * 8 + kg, t0:t0 + tn]) for jj in range(48)], ['yq'], TILES_Q, epi)
            if q + 1 < NQ:
                load_yq(q + 1)
            blocks = [[(w_out[l, j], 16)] for j in range(16)]

            def epi2(j, ti, t0, tn, bk, bkey):
                cp('act' if ti == 0 else 'dve', mix[:, j, t0:t0 + tn], bk[:, :tn], [bkey], ['mix'])
            gemm(blocks, lambda kg, t0, tn: mq[:, kg, t0:t0 + tn], ['mq'], TILES_Q, epi2)
            rstd_of(mix, 'mix', QW, sq, rs, 'n1')
            for c in range(16):
                stt(mix[:, c, :], mix[:, c, :], gn[:, l, 1, c:c + 1], rs, ALU.mult, ALU.mult, ['mix', 'n1rs', 'gn'], [('mixc', c)])
                tt('pool' if c % 2 == 0 else 'dve', xq[:, c, :], xq[:, c, :], mix[:, c, :], ALU.add, ['mix', ('mixc', c), 'xq'], [xck(c)])
            dm('sp', x_scr[:, :, cq], xq, 'xst', xall, [('x_scr', q)])
            rstd_of(xq, 'xq', QW, sq, rs, 'n2', ckey=xck)
            norm_to(hq, 'hq', xq, 'xq', QW, gn[:, l, 2, :], rs, 'n2', ckey=xck)
            dm('sp', h_scr[:, :, cq], hq, 'hst', ['hq'], [('h_scr', q)])

    def gemm_blocks_rhs(blocks, rhs_fns, rhs_keys, tiles, epi):
        flat = [(ap, kc) for blk in blocks for (ap, kc) in blk]
        LA = NWB - 1
        loaded = {}
        for i in range(min(LA, len(flat))):
            loaded[i] = wload(*flat[i])
        idx = 0
        for j, blk in enumerate(blocks):
            bks = [P.bank() for _ in tiles]
            (ap, kc) = blk[0]
            if idx + LA < len(flat):
                loaded[idx + LA] = wload(*flat[idx + LA])
            wb, wk = loaded.pop(idx)
            idx += 1
            rf = rhs_fns[j]
            for ti, (t0, tn) in enumerate(tiles):
                bk, bkey = bks[ti]
                for k in range(kc):
                    P.op('pe', lambda e, bk=bk, wb=wb, k=k, t0=t0, tn=tn, kc=kc, rf=rf:
                         e.matmul(bk[:, :tn], lhsT=wb[:, k, :], rhs=rf(k, t0, tn), start=(k == 0), stop=(k == kc - 1)),
                         reads=[wk] + rhs_keys, writes=[bkey], inc=(k == kc - 1))
            for ti, (t0, tn) in enumerate(tiles):
                epi(j, ti, t0, tn, bks[ti][0], bks[ti][1])

    def stage_ff1(l):
        A.reset()
        hT = A.alloc([128, 16, NT], BF16)
        stg = [A.alloc([128, NT], BF16) for _ in range(2)]
        rl = [A.alloc([128, 512], BF16) for _ in range(2)]
        dm('sp', hT, h_scr, 'hT', [('h_scr', q) for q in range(NQ)], ['hT'])
        blocks = [[(w_ff1[l, j], 16)] for j in range(64)]

        def epi(j, ti, t0, tn, bk, bkey):
            s = stg[j % 2]
            sk = ('stg', j % 2, ti)
            r = rl[ti % 2]
            act(r[:, :tn], bk[:, :tn], AF.Relu, [bkey], [('rl', ti % 2)])
            tt('pool', s[:, t0:t0 + tn], r[:, :tn], r[:, :tn], ALU.mult, [('rl', ti % 2)], [sk])
            if ti == len(TILES_ALL) - 1:
                dm('sp', hid_scr[j], s, f'hst{j % 2}', [('stg', j % 2, t) for t in range(len(TILES_ALL))], [('hid', j)])
        gemm(blocks, lambda kg, t0, tn: hT[:, kg, t0:t0 + tn], ['hT'], TILES_ALL, epi)

    def stage_ff2(l, last):
        A.reset()
        hid = A.alloc([128, 64, QW], BF16)
        ffo = A.alloc([128, 16, QW], F32)
        xq = A.alloc([128, 16, QW], F32)
        hq = A.alloc([128, 16, QW], BF16)
        sq = A.alloc([128, 2, QW], BF16)
        rs = A.alloc([128, QW], F32)
        def load_hid(q):
            cq_ = slice(q * QW, q * QW + QW)
            for part in range(4):
                dm('sp', hid[:, part * 16:(part + 1) * 16, :], hid_scr[part * 16:(part + 1) * 16, :, cq_].rearrange("b p n -> p b n"), f'hid{part}',
                   [('hid', j) for j in range(part * 16, part * 16 + 16)], [('hidq', part)])
        load_hid(0)
        xck = lambda c: ('xqc', c)
        xall = ['xq'] + [('xqc', c) for c in range(16)]
        for q in range(NQ):
            c0 = q * QW
            cq = slice(c0, c0 + QW)
            dm('sp', xq, x_scr[:, :, cq], 'xq', [('x_scr', q)], ['xq'])
            blocks = [[(w_ff2[l, j, kg], 16) for kg in range(4)] for j in range(16)]

            def epi(j, ti, t0, tn, bk, bkey):
                cp('act' if ti == 0 else 'dve', ffo[:, j, t0:t0 + tn], bk[:, :tn], [bkey], ['ffo'])
            gemm(blocks, lambda kg, t0, tn: hid[:, kg, t0:t0 + tn], [('hidq', p_) for p_ in range(4)], TILES_Q, epi)
            if q + 1 < NQ:
                load_hid(q + 1)
            rstd_of(ffo, 'ffo', QW, sq, rs, 'n3')
            for c in range(16):
                stt(ffo[:, c, :], ffo[:, c, :], gn[:, l, 3, c:c + 1], rs, ALU.mult, ALU.mult, ['ffo', 'n3rs', 'gn'], [('ffoc', c)])
                tt('pool' if c % 2 == 0 else 'dve', xq[:, c, :], xq[:, c, :], ffo[:, c, :], ALU.add, ['ffo', ('ffoc', c), 'xq'], [xck(c)])
            if last:
                dm('sp', yT[:, :, cq], xq, 'yout', xall, [])
            else:
                dm('sp', x_scr[:, :, cq], xq, 'xst', xall, [('x_scr', q)])
                rstd_of(xq, 'xq', QW, sq, rs, 'n4', ckey=xck)
                norm_to(hq, 'hq', xq, 'xq', QW, gn[:, l + 1, 0, :], rs, 'n4', ckey=xck)
                dm('sp', h_scr[:, :, cq], hq, 'hst', ['hq'], [('h_scr', q)])

    stage_prenorm0()
    for l in range(L):
        P.barrier(); stage_win(l)
        P.barrier(); stage_s5(l)
        P.barrier(); stage_gla(l)
        P.barrier(); stage_swa(l, l == 0)
        P.barrier(); stage_merge(l)
        P.barrier(); stage_ff1(l)
        P.barrier(); stage_ff2(l, l == L - 1)
    P.build()
    return nc, P


def _tile_w(W, kc):
    K, N = W.shape
    return np.ascontiguousarray(W.reshape(K // 128, 128, N // 128, 128).transpose(2, 1, 0, 3))


def _t5_bucket(d):
    if d < 16:
        return d
    v = 16 + int(np.float32(np.log(np.float32(d) / np.float32(16)) / np.float32(math.log(128 / 16)) * np.float32(16)))
    return min(v, 31)


def _consts():
    oh = np.zeros((33, 2 * 256 + 128 + 4), np.float32)
    for j in range(256):
        dist = j + 1
        if j <= 254 and dist < 128:
            oh[_t5_bucket(dist), j] = 1
        else:
            oh[32, j] = 1
        dist = j - 127
        if j <= 254 and dist >= 0:
            oh[_t5_bucket(dist), 256 + j] = 1
        else:
            oh[32, 256 + j] = 1
    for j in range(128):
        dist = 128 - j
        if dist < 128:
            oh[_t5_bucket(dist), 512 + j] = 1
        else:
            oh[32, 512 + j] = 1
    oh[0, 640:644] = 1
    misc = np.zeros((128, 128 * 3 + 12), np.float32)
    misc[:, 0:128] = np.eye(128)
    misc[:, 128:256] = np.eye(128)[::-1]
    misc[:, 256:384] = np.triu(np.ones((128, 128)))
    for gl in range(8):
        misc[gl * 16:(gl + 1) * 16, 384 + gl] = 1
    misc[0:4, 392:396] = np.eye(4)
    tpos = np.tile((np.arange(NT) - COL0).astype(np.float32)[None, :], (128, 1))
    rmask = np.ones((128, NT), np.float32)
    rmask[:, 0::128] = 0
    pad = np.zeros((128, 2), np.float32)
    pad[0:COL0, 1] = NEG
    return oh, misc, tpos, rmask, pad


def _prep_shared(inp, L):
    sh = {}
    w_in = inp['w_in'][:L]
    offs = np.cumsum([0, 1024, 512, 512, 1024, 16, 1024, 1024, 256, 256, 6144])
    cols = []
    cols.append(np.arange(offs[0], offs[1]))
    cols.append(np.arange(offs[1], offs[2]))
    cols.append(np.arange(offs[2], offs[3]))
    cols.append(np.arange(offs[3], offs[4]))
    a_cols = np.concatenate([np.arange(offs[4], offs[5]), -np.ones(112, np.int64)])
    cols.append(a_cols)
    cols.append(np.arange(offs[5], offs[6]))
    qc = []
    for kvp in range(2):
        for g in range(4):
            for half in range(2):
                qh = (2 * kvp + half) * 4 + g
                qc.append(offs[6] + qh * 64 + np.arange(64))
    qperm = np.concatenate(qc)
    cols.append(qperm)
    cols.append(np.arange(offs[7], offs[8]))
    cols.append(np.arange(offs[8], offs[9]))
    cols.append(np.arange(offs[9], offs[10]))
    allc = np.concatenate(cols)
    assert allc.shape[0] == NBLK_IN * 128
    wt = np.zeros((L, NBLK_IN, 128, 16, 128), np.float32)
    valid = allc >= 0
    for l in range(L):
        Wp = np.zeros((D, NBLK_IN * 128), np.float32)
        Wp[:, valid] = w_in[l][:, allc[valid]]
        wt[l] = _tile_w(Wp, 16)
    sh['w_in'] = wt
    sh['w_glu'] = np.stack([_tile_w(inp['s5_w_glu'][l], 8) for l in range(L)])
    qrow = qperm - offs[6]
    sh['w_up'] = np.stack([np.stack([_tile_w(inp['w_up_s5'][l], 8), _tile_w(inp['w_up_gla'][l], 8), _tile_w(inp['w_up_swa'][l][qrow, :], 8)]) for l in range(L)])
    sh['w_out'] = np.stack([_tile_w(inp['w_out'][l], 16) for l in range(L)])
    sh['w_ff1'] = np.stack([_tile_w(inp['w_ff1'][l], 16) for l in range(L)])
    ff2 = np.stack([_tile_w(inp['w_ff2'][l], 64) for l in range(L)])
    sh['w_ff2'] = np.ascontiguousarray(ff2.reshape(L, 16, 128, 4, 16, 128).transpose(0, 1, 3, 2, 4, 5))
    g = np.stack([inp[n][:L] for n in ('norm_pre_mix', 'norm_post_mix', 'norm_pre_ffn', 'norm_post_ffn')], axis=1)
    sh['gnorm'] = np.ascontiguousarray(g.reshape(L, 4, 16, 128).transpose(0, 3, 1, 2))
    def rep_gp(a):
        x = a.reshape(L, 8, 8, 64)
        x = np.broadcast_to(x[:, :, :, None, :], (L, 8, 8, 16, 64))
        return x.transpose(0, 2, 3, 1, 4).reshape(L, 128, 512)
    ls = np.broadcast_to(inp['s5_log_step'][:L, :, None], (L, 64, 64))
    def b_t(a):
        x = a.reshape(L, 8, 8, 64, 16)
        return x.transpose(0, 2, 4, 1, 3).reshape(L, 128, 512)
    sh['s5rep'] = np.ascontiguousarray(np.stack([rep_gp(inp['s5_lam_re'][:L]), rep_gp(inp['s5_lam_im'][:L]), rep_gp(ls), b_t(inp['s5_b_re'][:L]), b_t(inp['s5_b_im'][:L])], axis=2)).astype(np.float32)
    def cm_gp(a):
        return a.reshape(L, 32, 2, 64).transpose(0, 2, 3, 1).reshape(L, 128, 32)
    sh['s5cm'] = np.ascontiguousarray(np.stack([cm_gp(inp['s5_lam_re'][:L]), cm_gp(inp['s5_lam_im'][:L]), cm_gp(ls)], axis=2)).astype(np.float32)
    def c_t(a):
        return a.reshape(L, 32, 2, 16, 64).transpose(0, 2, 4, 1, 3).reshape(L, 128, 32, 16)
    sh['s5c'] = np.ascontiguousarray(np.stack([c_t(inp['s5_c_re'][:L]), c_t(inp['s5_c_im'][:L])], axis=2)).astype(np.float32)
    fp = lambda a: a.reshape(L, 8, 128).transpose(0, 2, 1)
    sh['s5d'] = np.ascontiguousarray(np.stack([fp(inp['s5_d'][:L]), fp(inp['s5_b_glu'][:L])], axis=2)).astype(np.float32)
    sh['gla_w2'] = np.ascontiguousarray(inp['gla_w_gate2'][:L])
    sh['gla_p'] = np.ascontiguousarray(np.concatenate([inp['gla_b_gate2'][:L].reshape(L, 4, 128).transpose(0, 2, 1), inp['gla_g_out'][:L].reshape(L, 2, 128).transpose(0, 2, 1)], axis=2)).astype(np.float32)
    sh['swa_sink'] = np.ascontiguousarray(np.broadcast_to(inp['swa_sinks'][:L, None, :], (L, 64, 16))).astype(np.float32)
    sh['relb'] = np.ascontiguousarray(inp['rel_bias'])
    oh, misc, tpos, rmask, pad = _consts()
    sh['c_oh'] = oh; sh['c_misc'] = misc; sh['c_tpos'] = tpos; sh['c_rmask'] = rmask; sh['c_pad'] = pad
    return sh


def _prep_core(inp, L, c):
    b = c % 4
    m = {}
    xt = np.zeros((NT, D), np.float32)
    xt[0:NS] = inp['x_sample'][NS * c:NS * c + NS, 0, :]
    xt[COL0:COL0 + NMETA] = inp['meta_tokens']
    xt[128:] = inp['x_prompt'][b]
    m['xT'] = np.ascontiguousarray(xt.T.reshape(16, 128, NT).transpose(1, 0, 2))
    ns = slice(NS * c, NS * c + NS)
    def x0cm(a):
        return a.reshape(L, NS, 32, 2, 64).transpose(0, 3, 4, 2, 1).reshape(L, 128, 32, NS)
    m['s5x0'] = np.ascontiguousarray(np.stack([x0cm(inp['state_s5_re'][:L, ns]), x0cm(inp['state_s5_im'][:L, ns])], axis=2)).astype(np.float32)
    m['gla_s0'] = np.ascontiguousarray(inp['state_gla'][:L, ns])
    ck = inp['cache_swa_k'][:L, ns]
    m['swa_kT'] = np.ascontiguousarray(ck.reshape(L, NS, 128, 2, 2, 64).transpose(0, 4, 5, 1, 3, 2).reshape(L, 128, NS, 2, 128))
    m['swa_kn'] = np.ascontiguousarray(ck.reshape(L, NS, 128, 256))
    m['swa_vn'] = np.ascontiguousarray(inp['cache_swa_v'][:L, ns].reshape(L, NS, 128, 256))
    return m


_CACHE = {}


def run(inputs, L=4, debug=False):
    inp = {k: np.asarray(v) for k, v in inputs.items()}
    key = (L, debug)
    if key not in _CACHE:
        _CACHE[key] = build_program(L, debug)
    nc, P = _CACHE[key]
    sh = _prep_shared(inp, L)
    in_maps = []
    for c in range(8):
        m = dict(sh)
        m.update(_prep_core(inp, L, c))
        in_maps.append(m)
    res = run_bass_kernel_spmd(nc, in_maps, core_ids=list(range(8)))
    return res.results


def assemble(R, L):
    B, NSAMP = 4, 32
    y_prompt = np.zeros((B, SEQ, D), np.float32)
    y_sample = np.zeros((NSAMP, 1, D), np.float32)
    p_s5r = np.zeros((L, B, 64, 64), np.float32); p_s5i = np.zeros_like(p_s5r)
    p_gla = np.zeros((L, B, 4, 128, 256), np.float32)
    p_k = np.zeros((L, B, 128, 4, 64), np.float32); p_v = np.zeros_like(p_k)
    s_s5r = np.zeros((L, NSAMP, 64, 64), np.float32); s_s5i = np.zeros_like(s_s5r)
    s_gla = np.zeros((L, NSAMP, 4, 128, 256), np.float32)
    s_k = np.zeros((L, NSAMP, 128, 4, 64), np.float32); s_v = np.zeros_like(s_k)
    for c in range(8):
        r = R[c]
        yt = r['yT'].transpose(1, 0, 2).reshape(D, NT).T
        ns = slice(NS * c, NS * c + NS)
        y_sample[ns, 0, :] = yt[0:NS]
        ss5 = r['o_ss5']
        v = ss5.reshape(L, 2, 64, 2, 32, NS).transpose(3, 0, 5, 4, 1, 2).reshape(2, L, NS, 64, 64)
        s_s5r[:, ns] = v[0]; s_s5i[:, ns] = v[1]
        s_gla[:, ns] = r['o_sgla']
        s_k[:, ns] = r['o_sk'].reshape(L, NS, 128, 4, 64)
        s_v[:, ns] = r['o_sv'].reshape(L, NS, 128, 4, 64)
        if c < 4:
            b = c
            y_prompt[b] = yt[128:]
            ps = r['o_ps5']
            v = ps.reshape(L, 2, 64, 2, 32).transpose(3, 0, 4, 1, 2).reshape(2, L, 64, 64)
            p_s5r[:, b] = v[0]; p_s5i[:, b] = v[1]
            p_gla[:, b] = r['o_pgla']
            p_k[:, b] = r['o_pk'].reshape(L, 128, 4, 64)
            p_v[:, b] = r['o_pv'].reshape(L, 128, 4, 64)
    return (y_prompt, y_sample, p_s5r, p_s5i, p_gla, p_k, p_v, s_s5r, s_s5i, s_gla, s_k, s_v)


def kernel(**inputs):
    R = run(inputs, L=4)
    return assemble(R, 4)
```

```python
import math
import numpy as np
from contextlib import ExitStack
import concourse.bass as bass
import concourse.mybir as mybir
from concourse.bass_utils import run_bass_kernel_spmd

F32 = mybir.dt.float32
BF16 = mybir.dt.bfloat16
AF = mybir.ActivationFunctionType
ALU = mybir.AluOpType

D = 2048; NT = 2176; QW = 544; NQ = 4; SEQ = 2048; NMETA = 16; COL0 = 112
NS = 4
NBLK_IN = 93
EPS = 1e-6
PI = math.pi
ENG = ['pe', 'act', 'dve', 'pool', 'sp']
TILES_ALL = [(0, 512), (512, 512), (1024, 512), (1536, 512), (2048, 128)]
TILES_Q = [(0, 512), (512, 32)]
NEG = -30000.0


class Prog:
    def __init__(self, nc):
        self.nc = nc
        self.q = {e: [] for e in ENG}
        self.cnt = {e: 0 for e in ENG}
        self.seen = {e: {} for e in ENG}
        self.keys = {}
        self.dtot = {}
        self.es = ExitStack()
        self.nops = 0
        self.banks = None
        self.bi = 0

    def sb(self, name, shape, dt):
        return self.es.enter_context(self.nc.sbuf_tensor(name, list(shape), dt))

    def mkbanks(self):
        self.banks = [self.es.enter_context(self.nc.psum_tensor(f"psb{i}", [128, 512], F32)) for i in range(8)]

    def bank(self):
        i = self.bi % 8
        self.bi += 1
        return self.banks[i], f"ps{i}"

    def _deps(self, reads, writes):
        toks = []
        for k in reads:
            st = self.keys.get(k)
            if st is not None and st[0] is not None:
                toks.append(st[0])
        for k in writes:
            st = self.keys.get(k)
            if st is not None:
                if st[0] is not None:
                    toks.append(st[0])
                toks.extend(st[1].items())
        return toks

    def _commit(self, tok, reads, writes):
        for k in reads:
            st = self.keys.get(k)
            if st is None:
                st = self.keys[k] = [None, {}]
            if st[1].get(tok[0], 0) < tok[1]:
                st[1][tok[0]] = tok[1]
        for k in writes:
            self.keys[k] = [tok, {}]

    def _mkwaits(self, eng, toks):
        best = {}
        for s, v in toks:
            if s == eng and eng == 'pe':
                continue
            if best.get(s, 0) < v:
                best[s] = v
        seen = self.seen[eng]
        out = []
        for s, v in best.items():
            if seen.get(s, 0) >= v:
                continue
            seen[s] = v
            out.append((s, v))
        return out

    def op(self, eng, fn, reads=(), writes=(), inc=True):
        waits = self._mkwaits(eng, self._deps(reads, writes))
        if inc:
            self.cnt[eng] += 1
            tok = (eng, self.cnt[eng])
        else:
            tok = (eng, self.cnt[eng] + 1)
        self.q[eng].append((waits, fn, eng if inc else None, 1))
        self._commit(tok, reads, writes)
        self.nops += 1

    def dma(self, eng, fn, chan, reads=(), writes=()):
        waits = self._mkwaits(eng, self._deps(reads, writes))
        self.dtot[chan] = self.dtot.get(chan, 0) + 16
        tok = (chan, self.dtot[chan])
        self.q[eng].append((waits, fn, chan, 16))
        self._commit(tok, reads, writes)
        self.nops += 1

    def barrier(self):
        toks = [(e, self.cnt[e]) for e in ENG if self.cnt[e] > 0] + list(self.dtot.items())
        for e in ENG:
            waits = self._mkwaits(e, toks)
            if waits:
                self.q[e].append((waits, None, None, 0))

    def build(self):
        nc = self.nc
        names = sorted(set(ENG) | set(self.dtot.keys()))
        sems = {n: self.es.enter_context(nc.semaphore("s_" + n)) for n in names}
        fin = list(self.dtot.items()) + [(e, self.cnt[e]) for e in ENG if e != 'sp' and self.cnt[e] > 0]
        q = self.q
        block = self.es.enter_context(nc.Block())
        emap = {'pe': block.tensor, 'act': block.scalar, 'dve': block.vector, 'pool': block.gpsimd, 'sp': block.sync}

        def mk(engname):
            def body(e):
                for waits, fn, s, n in q[engname]:
                    for (ws, wv) in waits:
                        e.wait_ge(sems[ws], wv)
                    if fn is None:
                        continue
                    ins = fn(e)
                    if s is not None:
                        ins.then_inc(sems[s], n)
                if engname == 'sp':
                    for (ws, wv) in fin:
                        e.wait_ge(sems[ws], wv)
            return body
        for engname in ENG:
            emap[engname](mk(engname))
        self.es.close()


class Arena:
    def __init__(self, P, words):
        self.t = P.sb("arena", [128, words], F32)
        self.words = words
        self.off = 0
        self.gen = 0

    def reset(self):
        self.off = 0
        self.gen += 1

    def alloc(self, shape, dt):
        n = 1
        for s in shape[1:]:
            n *= s
        w = n if dt == F32 else (n + 1) // 2
        w = (w + 15) // 16 * 16
        assert self.off + w <= self.words, (self.off, w, self.words)
        v = self.t[:, self.off:self.off + w]
        self.off += w
        if dt != F32:
            v = v.bitcast(dt)
        v = v[:, :n]
        if len(shape) == 3:
            v = v.rearrange("p (a b) -> p a b", a=shape[1])
        elif len(shape) == 4:
            v = v.rearrange("p (a b c) -> p a b c", a=shape[1], b=shape[2])
        elif len(shape) == 5:
            v = v.rearrange("p (a b c d) -> p a b c d", a=shape[1], b=shape[2], c=shape[3])
        return v


def build_program(L, debug=False):
    nc = bass.Bass("TRN2", target_bir_lowering=False)
    P = Prog(nc)
    A = Arena(P, 46000)
    P.mkbanks()

    def din(name, shape):
        return nc.dram_tensor(name, list(shape), F32, kind="ExternalInput").ap()

    def dout(name, shape):
        return nc.dram_tensor(name, list(shape), F32, kind="ExternalOutput").ap()

    def dscr(name, shape, dt):
        return nc.dram_tensor(name, list(shape), dt, kind=("ExternalOutput" if debug else "Internal")).ap()

    xT = din("xT", [128, 16, NT])
    w_in = din("w_in", [L, NBLK_IN, 128, 16, 128])
    w_glu = din("w_glu", [L, 8, 128, 8, 128])
    w_up = din("w_up", [L, 3, 16, 128, 8, 128])
    w_out = din("w_out", [L, 16, 128, 16, 128])
    w_ff1 = din("w_ff1", [L, 64, 128, 16, 128])
    w_ff2 = din("w_ff2", [L, 16, 4, 128, 16, 128])
    gnorm = din("gnorm", [L, 128, 4, 16])
    s5rep = din("s5rep", [L, 128, 5, 512])
    s5cm = din("s5cm", [L, 128, 3, 32])
    s5c = din("s5c", [L, 128, 2, 32, 16])
    s5d = din("s5d", [L, 128, 2, 8])
    s5x0 = din("s5x0", [L, 128, 2, 32, NS])
    gla_w2 = din("gla_w2", [L, 16, 512])
    gla_p = din("gla_p", [L, 128, 6])
    gla_s0 = din("gla_s0", [L, NS, 4, 128, 256])
    swa_sink = din("swa_sink", [L, 64, 16])
    swa_kT = din("swa_kT", [L, 128, NS, 2, 128])
    swa_kn = din("swa_kn", [L, NS, 128, 256])
    swa_vn = din("swa_vn", [L, NS, 128, 256])
    relb = din("relb", [32, 16])
    c_oh = din("c_oh", [33, 2 * 256 + 128 + 4])
    c_misc = din("c_misc", [128, 128 * 3 + 8 + 4])
    c_tpos = din("c_tpos", [128, NT])
    c_rmask = din("c_rmask", [128, NT])
    c_pad = din("c_pad", [128, 2])
    yT = dout("yT", [128, 16, NT])
    o_ps5 = dout("o_ps5", [L, 128, 2, 32])
    o_pgla = dout("o_pgla", [L, 4, 128, 256])
    o_pk = dout("o_pk", [L, 128, 256])
    o_pv = dout("o_pv", [L, 128, 256])
    o_ss5 = dout("o_ss5", [L, 128, 2, 32, NS])
    o_sgla = dout("o_sgla", [L, NS, 4, 128, 256])
    o_sk = dout("o_sk", [L, NS, 128, 256])
    o_sv = dout("o_sv", [L, NS, 128, 256])
    x_scr = dscr("x_scr", [128, 16, NT], F32)
    h_scr = dscr("h_scr", [128, 16, NT], BF16)
    z_scr = dscr("z_scr", [NBLK_IN, 128, NT], BF16)
    y_scr = dscr("y_scr", [24, 128, NT], BF16)
    hid_scr = dscr("hid_scr", [64, 128, NT], BF16)
    e_scr = dscr("e_scr", [2, 16, 256], F32)

    NWB = 4
    wbufs = [P.sb(f"wb{i}", [128, 16, 128], BF16) for i in range(NWB)]
    cm = P.sb("cmisc", [128, 128 * 3 + 12], F32)
    cb = P.sb("cbf", [128, 128 * 3 + 128], BF16)
    gn = P.sb("gn", [128, L, 4, 16], F32)
    ident_f = cm[:, 0:128]
    maskBD = cm[:, 384:392]
    I4 = cm[:, 392:396]
    ident_b = cb[:, 0:128]
    J_b = cb[:, 128:256]
    tri_b = cb[:, 256:384]
    ones_b = cb[:, 384:512]

    P.dma('sp', lambda e: e.dma_start(out=cm[:], in_=c_misc), 'cm', writes=['cm'])
    P.op('dve', lambda e: e.tensor_copy(out=cb[:, 0:384], in_=cm[:, 0:384]), reads=['cm'], writes=['cb'])
    P.op('dve', lambda e: e.memset(cb[:, 384:512], 1.0), writes=['cb1'])
    P.dma('sp', lambda e: e.dma_start(out=gn[:], in_=gnorm.rearrange("l p a c -> p l a c")), 'gn', writes=['gn'])

    wcnt = [0]

    def wload(src, kc):
        i = wcnt[0] % NWB
        wcnt[0] += 1
        P.dma('pool', lambda e: e.dma_start(out=wbufs[i][:, :kc, :], in_=src), f'wb{i}', writes=[f'wb{i}'])
        return wbufs[i], f'wb{i}'

    def gemm(blocks, rhs_fn, rhs_keys, tiles, epi):
        flat = [(ap, kc) for blk in blocks for (ap, kc) in blk]
        LA = NWB - 1
        loaded = {}
        for i in range(min(LA, len(flat))):
            loaded[i] = wload(*flat[i])
        idx = 0
        for j, blk in enumerate(blocks):
            bks = [P.bank() for _ in tiles]
            nk = sum(kc for _, kc in blk)
            kd = 0
            for (ap, kc) in blk:
                if idx + LA < len(flat):
                    loaded[idx + LA] = wload(*flat[idx + LA])
                wb, wk = loaded.pop(idx)
                idx += 1
                for ti, (t0, tn) in enumerate(tiles):
                    bk, bkey = bks[ti]
                    for k in range(kc):
                        kg = kd + k
                        P.op('pe', lambda e, bk=bk, wb=wb, k=k, kg=kg, t0=t0, tn=tn, nk=nk:
                             e.matmul(bk[:, :tn], lhsT=wb[:, k, :], rhs=rhs_fn(kg, t0, tn), start=(kg == 0), stop=(kg == nk - 1)),
                             reads=[wk] + rhs_keys, writes=[bkey], inc=(k == kc - 1))
                kd += kc
            for ti, (t0, tn) in enumerate(tiles):
                epi(j, ti, t0, tn, bks[ti][0], bks[ti][1])

    def act(out, in_, func, r, w, **kw):
        P.op('act', lambda e: e.activation(out=out, in_=in_, func=func, **kw), reads=r, writes=w)

    def tt(eng, out, a, b, op, r, w):
        P.op(eng, lambda e: e.tensor_tensor(out=out, in0=a, in1=b, op=op), reads=r, writes=w)

    def ts(eng, out, a, s1, s2, op0, op1, r, w):
        if op1 is None:
            P.op(eng, lambda e: e.tensor_scalar(out=out, in0=a, scalar1=s1, scalar2=None, op0=op0), reads=r, writes=w)
        else:
            P.op(eng, lambda e: e.tensor_scalar(out=out, in0=a, scalar1=s1, scalar2=s2, op0=op0, op1=op1), reads=r, writes=w)

    def stt(out, a, s, b, op0, op1, r, w):
        P.op('dve', lambda e: e.scalar_tensor_tensor(out=out, in0=a, scalar=s, in1=b, op0=op0, op1=op1), reads=r, writes=w)

    def cp(eng, out, in_, r, w):
        if eng == 'act':
            act(out, in_, AF.Copy, r, w)
        else:
            P.op(eng, lambda e: e.tensor_copy(out=out, in_=in_), reads=r, writes=w)

    def mm(out, lhsT, rhs, start, stop, r, w, inc=True):
        P.op('pe', lambda e: e.matmul(out, lhsT=lhsT, rhs=rhs, start=start, stop=stop), reads=r, writes=w, inc=inc)

    def tr(out, in_, ident, r, w):
        P.op('pe', lambda e: e.transpose(out, in_, ident), reads=r, writes=w)

    def dm(eng, out, in_, chan, r, w):
        P.dma(eng, lambda e: e.dma_start(out=out, in_=in_), chan, reads=r, writes=w)

    def memset(eng, ap, val, w):
        P.op(eng, lambda e: e.memset(ap, val), writes=w)

    def rstd_of(src, skey, n, sq, rstd, tag, ckey=None):
        tl = [(0, min(512, n))] + ([(512, n - 512)] if n > 512 else [])
        for (t0, tn) in tl:
            bk, bkey = P.bank()
            for c in range(16):
                sk = tag + 'sq' + str(c % 2)
                rk = [skey] + ([ckey(c)] if ckey else [])
                act(sq[:, c % 2, t0:t0 + tn], src[:, c, t0:t0 + tn], AF.Square, rk, [sk])
                mm(bk[:, :tn], ones_b, sq[:, c % 2, t0:t0 + tn], c == 0, c == 15, [sk, 'cb1'], [bkey])
            ts('dve', rstd[:, t0:t0 + tn], bk[:, :tn], 1.0 / D, EPS, ALU.mult, ALU.add, [bkey], [tag + 'rs'])
        act(rstd[:, :n], rstd[:, :n], AF.Ln, [tag + 'rs'], [tag + 'rs'])
        act(rstd[:, :n], rstd[:, :n], AF.Exp, [tag + 'rs'], [tag + 'rs'], scale=-0.5)

    def norm_to(out, okey, src, skey, n, gcols, rstd, tag, ckey=None):
        for c in range(16):
            rk = [skey, tag + 'rs', 'gn'] + ([ckey(c)] if ckey else [])
            stt(out[:, c, :n], src[:, c, :n], gcols[:, c:c + 1], rstd[:, :n], ALU.mult, ALU.mult, rk, [okey])

    def stage_prenorm0():
        A.reset()
        xq = A.alloc([128, 16, QW], F32)
        hq = A.alloc([128, 16, QW], BF16)
        sq = A.alloc([128, 2, QW], BF16)
        rs = A.alloc([128, QW], F32)
        for q in range(NQ):
            c0 = q * QW
            dm('sp', xq, xT[:, :, c0:c0 + QW], 'xq', [], ['xq'])
            dm('sp', x_scr[:, :, c0:c0 + QW], xq, 'xst', ['xq'], [('x_scr', q)])
            rstd_of(xq, 'xq', QW, sq, rs, 'n0')
            norm_to(hq, 'hq', xq, 'xq', QW, gn[:, 0, 0, :], rs, 'n0')
            dm('sp', h_scr[:, :, c0:c0 + QW], hq, 'hst', ['hq'], [('h_scr', q)])

    def stage_win(l):
        A.reset()
        hT = A.alloc([128, 16, NT], BF16)
        stg = [A.alloc([128, NT], BF16) for _ in range(2)]
        dm('sp', hT, h_scr, 'hT', [('h_scr', q) for q in range(NQ)], ['hT'])
        blocks = [[(w_in[l, j], 16)] for j in range(NBLK_IN)]

        def epi(j, ti, t0, tn, bk, bkey):
            s = stg[j % 2]
            sk = ('stg', j % 2, ti)
            if j >= 45:
                act(s[:, t0:t0 + tn], bk[:, :tn], AF.Sigmoid, [bkey], [sk])
            elif 25 <= j < 33:
                act(s[:, t0:t0 + tn], bk[:, :tn], AF.Silu, [bkey], [sk])
            else:
                cp('dve' if ti % 2 == 0 else 'act', s[:, t0:t0 + tn], bk[:, :tn], [bkey], [sk])
            if ti == len(TILES_ALL) - 1:
                dm('sp', z_scr[j], s, f'zst{j % 2}', [('stg', j % 2, t) for t in range(len(TILES_ALL))], [('z', j)])
        gemm(blocks, lambda kg, t0, tn: hT[:, kg, t0:t0 + tn], ['hT'], TILES_ALL, epi)

    def stage_s5(l):
        A.reset()
        ub = A.alloc([128, 8, NT], BF16)
        BD = A.alloc([128, 8, 4, 2, 128], BF16)
        CD = A.alloc([128, 8, 4, 2, 128], BF16)
        uni = A.alloc([128, 6656], F32)
        rep = uni[:, 0:2560].rearrange("p (a b) -> p a b", a=5)
        wk = uni[:, 2560:6656].rearrange("p (a b) -> p a b", a=8)
        cmp_ = A.alloc([128, 3, 32], F32)
        cw = A.alloc([128, 12, 32], F32)
        ct = A.alloc([128, 2, 32, 16], F32)
        dd = A.alloc([128, 2, 8], F32)
        x0 = A.alloc([128, 2, 32, NS], F32)
        x1 = A.alloc([128, 2, 32, NS], F32)
        pst = A.alloc([128, 2, 32], F32)
        tpos = A.alloc([128, 512], F32)
        tcs = [A.alloc([128, 512], F32) for _ in range(2)]
        tss = [A.alloc([128, 512], F32) for _ in range(2)]
        targ = A.alloc([128, 512], F32)
        tki = A.alloc([128, 512], F32).bitcast(mybir.dt.int32)
        vini = A.alloc([128, 4], F32)
        magt = A.alloc([128, 512], F32)
        t4 = [A.alloc([128, 512], F32) for _ in range(4)]
        et = [A.alloc([128, 512], F32) for _ in range(2)]
        vv = [[A.alloc([128, 512], F32) for _ in range(2)] for _ in range(2)]
        xx = [[A.alloc([128, 512], BF16) for _ in range(2)] for _ in range(2)]
        pt = [A.alloc([128, 512], F32) for _ in range(2)]
        yacc = A.alloc([128, NT], F32)
        g1 = A.alloc([128, 512], F32)
        g2 = A.alloc([128, 512], F32)
        ystg = [A.alloc([128, NT], BF16) for _ in range(2)]
        dtmp = A.alloc([128, 8], F32)

        dm('sp', ub, z_scr[0:8].rearrange("b p n -> p b n"), 'ub', [('z', j) for j in range(8)], ['ub'])
        dm('sp', rep, s5rep[l], 'rep', [], ['rep'])
        dm('sp', cmp_, s5cm[l], 'cmp', [], ['cmp'])
        dm('sp', ct, s5c[l], 'ct', [], ['ct'])
        dm('sp', dd, s5d[l], 'dd', [], ['dd'])
        dm('sp', x0, s5x0[l], 'x0', [], ['x0'])
        dm('sp', tpos, c_tpos[:, COL0:COL0 + 512], 'tpos', [], ['tpos'])
        memset('pool', BD.rearrange("p a b c d -> p (a b c d)"), 0.0, ['BD'])
        memset('pool', CD.rearrange("p a b c d -> p (a b c d)"), 0.0, ['CD'])
        for par in range(2):
            for ri in range(2):
                memset('dve', xx[par][ri], 0.0, [('xx', par)])

        def sincos(ang, sn, cs, tmp, tmpi, key):
            K = [key]
            ts('dve', tmpi, ang, 1.0 / (2 * PI), None, ALU.mult, None, K, K)
            stt(tmp, tmpi, -2 * PI, ang, ALU.mult, ALU.add, K, K)
            ts('dve', tmp, tmp, PI, -PI, ALU.min, ALU.max, K, K)
            act(sn, tmp, AF.Sin, K, K)
            ts('dve', tmpi, ang, 0.5 * PI, 1.0 / (2 * PI), ALU.add, ALU.mult, K, K)
            stt(tmp, tmpi, -2 * PI, ang, ALU.mult, ALU.add, K, K)
            ts('dve', tmp, tmp, 0.5 * PI, PI, ALU.add, ALU.min, K, K)
            ts('dve', tmp, tmp, -PI, None, ALU.max, None, K, K)
            act(cs, tmp, AF.Sin, K, K)

        lr, li, ls, br, bi = (rep[:, i, :] for i in range(5))
        w = [wk[:, i, :] for i in range(8)]
        R = ['rep', 'wk']
        act(w[0], ls, AF.Exp, ['rep'], ['wk'])
        tt('dve', w[1], lr, w[0], ALU.mult, R, ['wk'])
        act(w[1], w[1], AF.Exp, ['wk'], ['wk'])
        tt('dve', w[2], li, w[0], ALU.mult, R, ['wk'])
        sincos(w[2], w[3], w[4], w[5], w[6].bitcast(mybir.dt.int32), 'wk')
        tt('dve', w[3], w[3], w[1], ALU.mult, R, ['wk'])
        tt('dve', w[4], w[4], w[1], ALU.mult, R, ['wk'])
        ts('dve', w[4], w[4], -1.0, None, ALU.add, None, R, ['wk'])
        tt('dve', w[5], lr, lr, ALU.mult, R, ['wk'])
        tt('dve', w[6], li, li, ALU.mult, R, ['wk'])
        tt('dve', w[5], w[5], w[6], ALU.add, R, ['wk'])
        P.op('dve', lambda e: e.reciprocal(out=w[5], in_=w[5]), reads=R, writes=['wk'])
        tt('dve', w[6], w[4], lr, ALU.mult, R, ['wk'])
        tt('dve', w[7], w[3], li, ALU.mult, R, ['wk'])
        tt('dve', w[6], w[6], w[7], ALU.add, R, ['wk'])
        tt('dve', w[6], w[6], w[5], ALU.mult, R, ['wk'])
        tt('dve', w[7], w[3], lr, ALU.mult, R, ['wk'])
        tt('dve', w[0], w[4], li, ALU.mult, R, ['wk'])
        tt('dve', w[7], w[7], w[0], ALU.subtract, R, ['wk'])
        tt('dve', w[7], w[7], w[5], ALU.mult, R, ['wk'])
        tt('dve', w[0], w[6], br, ALU.mult, R, ['wk'])
        tt('dve', w[1], w[7], bi, ALU.mult, R, ['wk'])
        tt('dve', w[0], w[0], w[1], ALU.subtract, R, ['wk'])
        tt('dve', w[1], w[6], bi, ALU.mult, R, ['wk'])
        tt('dve', w[2], w[7], br, ALU.mult, R, ['wk'])
        tt('dve', w[1], w[1], w[2], ALU.add, R, ['wk'])
        for ri in range(2):
            src = w[ri].rearrange("p (a b) -> p a b", a=8)
            for ccl in range(4):
                for g2_ in range(2):
                    ts('dve', BD[:, :, ccl, ri, g2_ * 64:(g2_ + 1) * 64], src, maskBD[:, ccl * 2 + g2_:ccl * 2 + g2_ + 1], None,
                       ALU.mult, None, ['wk', 'cm', 'BD'], ['BD'])
        c = [cw[:, i, :] for i in range(12)]
        Rc = ['cmp', 'cw']
        act(c[0], cmp_[:, 2, :], AF.Exp, ['cmp'], ['cw'])
        tt('dve', c[1], cmp_[:, 0, :], c[0], ALU.mult, Rc, ['cw'])
        act(c[1], c[1], AF.Exp, ['cw'], ['cw'])
        tt('dve', c[2], cmp_[:, 1, :], c[0], ALU.mult, Rc, ['cw'])
        sincos(c[2], c[3], c[4], c[5], c[6].bitcast(mybir.dt.int32), 'cw')
        tt('dve', c[3], c[3], c[1], ALU.mult, Rc, ['cw'])
        tt('dve', c[4], c[4], c[1], ALU.mult, Rc, ['cw'])
        mag_c, th_c, ai_c, ar_c = c[1], c[2], c[3], c[4]
        c = [cw[:, i, :] for i in range(12)]
        ts('dve', c[7], th_c, 512.0, None, ALU.mult, None, Rc, ['cw'])
        sincos(c[7], c[8], c[9], c[5], c[6].bitcast(mybir.dt.int32), 'cw')
        s512_c, c512_c = c[8], c[9]
        for ri in range(2):
            c5 = ct[:, ri, :, :].rearrange("p (a b) h -> p a b h", a=8)
            for ccl in range(4):
                for g2_ in range(2):
                    ps_ = slice(g2_ * 64, (g2_ + 1) * 64)
                    ts('dve', CD[ps_, :, ccl, ri, 32 * ccl + 16 * g2_:32 * ccl + 16 * g2_ + 16], c5[ps_, :, ccl, :],
                       (1.0 if ri == 0 else -1.0), None, ALU.mult, None, ['ct', 'CD'], ['CD'])

        P.barrier()
        pending = []
        for fc in range(8):
            for ccl in range(4):
                cc = fc * 4 + ccl
                thc = th_c[:, cc:cc + 1]
                nv = NT - COL0
                tp_ = cc % 2
                tc_, tsn = tcs[tp_], tss[tp_]
                tck, tsk = ('tc', tp_), ('tsn', tp_)
                ts('dve', targ, tpos, thc, None, ALU.mult, None, ['tpos', 'cw', 'targ'], ['targ'])
                ts('dve', tki, targ, 1.0 / (2 * PI), None, ALU.mult, None, ['targ', 'tki'], ['tki'])
                stt(tsn, tki, -2 * PI, targ, ALU.mult, ALU.add, ['tki', 'targ', tsk], [tsk])
                ts('dve', tsn, tsn, PI, -PI, ALU.min, ALU.max, [tsk], [tsk])
                act(tsn, tsn, AF.Sin, [tsk], [tsk])
                ts('dve', tki, targ, 0.5 * PI, 1.0 / (2 * PI), ALU.add, ALU.mult, ['targ', 'tki'], ['tki'])
                stt(tc_, tki, -2 * PI, targ, ALU.mult, ALU.add, ['tki', 'targ', tck], [tck])
                ts('dve', tc_, tc_, 0.5 * PI, PI, ALU.add, ALU.min, [tck], [tck])
                ts('dve', tc_, tc_, -PI, None, ALU.max, None, [tck], [tck])
                act(tc_, tc_, AF.Sin, [tck], [tck])
                ts('dve', magt, tpos, 0.0, mag_c[:, cc:cc + 1], ALU.mult, ALU.add, ['tpos', 'cw', 'magt'], ['magt'])
                for ti, (t0, tn) in enumerate(TILES_ALL):
                    par = ti % 2
                    be, bek = P.bank()
                    bi_, bik = P.bank()
                    mm(be[:, :tn], BD[:, fc, ccl, 0, :], ub[:, fc, t0:t0 + tn], True, True, ['BD', 'ub'], [bek])
                    mm(bi_[:, :tn], BD[:, fc, ccl, 1, :], ub[:, fc, t0:t0 + tn], True, True, ['BD', 'ub'], [bik])
                    cs_ = tc_[:, 0:tn]
                    sn_ = tsn[:, 0:tn]
                    er, ei = be[:, :tn], bi_[:, :tn]
                    tt('dve', t4[0][:, :tn], er, cs_, ALU.mult, [bek, tck], ['t40'])
                    tt('dve', t4[1][:, :tn], ei, sn_, ALU.mult, [bik, tsk], ['t41'])
                    tt('dve', et[0][:, :tn], t4[0][:, :tn], t4[1][:, :tn], ALU.add, ['t40', 't41'], ['et0'])
                    tt('dve', t4[2][:, :tn], ei, cs_, ALU.mult, [bik, tck], ['t42'])
                    tt('dve', t4[3][:, :tn], er, sn_, ALU.mult, [bek, tsk], ['t43'])
                    tt('dve', et[1][:, :tn], t4[2][:, :tn], t4[3][:, :tn], ALU.subtract, ['t42', 't43'], ['et1'])
                    if ti == 0:
                        memset('dve', et[0][:, 0:COL0], 0.0, ['et0'])
                        memset('dve', et[1][:, 0:COL0], 0.0, ['et1'])
                    else:
                        vrL, viL = vv[1 - par][0][:, 511:512], vv[1 - par][1][:, 511:512]
                        s5c_, c5c_ = s512_c[:, cc:cc + 1], c512_c[:, cc:cc + 1]
                        RK = [('vv', 1 - par, 0), ('vv', 1 - par, 1), 'cw', 'vini']
                        ts('dve', vini[:, 2:3], viL, s5c_, None, ALU.mult, None, RK, ['vini'])
                        stt(vini[:, 0:1], vrL, c5c_, vini[:, 2:3], ALU.mult, ALU.subtract, RK, ['vini'])
                        ts('dve', vini[:, 3:4], vrL, s5c_, None, ALU.mult, None, RK, ['vini'])
                        stt(vini[:, 1:2], viL, c5c_, vini[:, 3:4], ALU.mult, ALU.add, RK, ['vini'])
                    for ri in range(2):
                        vcur = vv[par][ri]
                        if ti == 0:
                            P.op('dve', lambda e, vcur=vcur, ri=ri, tn=tn: e.tensor_tensor_scan(out=vcur[:, :tn], data0=magt[:, :tn], data1=et[ri][:, :tn], initial=0.0, op0=ALU.mult, op1=ALU.add),
                                 reads=['magt', f'et{ri}'], writes=[('vv', par, ri)])
                        else:
                            vprev = vv[1 - par][ri]
                            P.op('dve', lambda e, vcur=vcur, vprev=vprev, ri=ri, tn=tn: e.tensor_tensor_scan(out=vcur[:, :tn], data0=magt[:, :tn], data1=et[ri][:, :tn], initial=vini[:, ri:ri + 1], op0=ALU.mult, op1=ALU.add),
                                 reads=['magt', f'et{ri}', 'vini'], writes=[('vv', par, ri)])
                    while pending:
                        pending.pop(0)()
                    vr, vi = vv[par][0][:, :tn], vv[par][1][:, :tn]
                    xr, xi = xx[par][0], xx[par][1]
                    lo = 0
                    if ti == 0:
                        lo = COL0
                    xk = ('xx', par)
                    tt('pool', pt[0][:, lo:tn], vr[:, lo:tn], cs_[:, lo:tn], ALU.mult, [('vv', par, 0), tck], ['pt0'])
                    tt('pool', pt[1][:, lo:tn], vi[:, lo:tn], sn_[:, lo:tn], ALU.mult, [('vv', par, 1), tsk], ['pt1'])
                    tt('pool', xr[:, lo:tn], pt[0][:, lo:tn], pt[1][:, lo:tn], ALU.subtract, ['pt0', 'pt1'], [xk])
                    tt('pool', pt[0][:, lo:tn], vr[:, lo:tn], sn_[:, lo:tn], ALU.mult, [('vv', par, 0), tsk, 'pt0'], ['pt0'])
                    tt('pool', pt[1][:, lo:tn], vi[:, lo:tn], cs_[:, lo:tn], ALU.mult, [('vv', par, 1), tck, 'pt1'], ['pt1'])
                    tt('pool', xi[:, lo:tn], pt[0][:, lo:tn], pt[1][:, lo:tn], ALU.add, ['pt0', 'pt1'], [xk])
                    if ti == 0:
                        arc, aic = ar_c[:, cc:cc + 1], ai_c[:, cc:cc + 1]
                        x0r, x0i = x0[:, 0, cc, :], x0[:, 1, cc, :]
                        ts('dve', dtmp[:, 0:4], x0i, aic, None, ALU.mult, None, ['x0', 'cw'], ['dtmp'])
                        stt(dtmp[:, 4:8], x0r, arc, dtmp[:, 0:4], ALU.mult, ALU.subtract, ['x0', 'cw', 'dtmp'], ['dtmp'])
                        tt('dve', x1[:, 0, cc, :], dtmp[:, 4:8], er[:, 0:NS], ALU.add, ['dtmp', bek], ['x1'])
                        ts('dve', dtmp[:, 0:4], x0r, aic, None, ALU.mult, None, ['x0', 'cw', 'dtmp'], ['dtmp'])
                        stt(dtmp[:, 4:8], x0i, arc, dtmp[:, 0:4], ALU.mult, ALU.add, ['x0', 'cw', 'dtmp'], ['dtmp'])
                        tt('dve', x1[:, 1, cc, :], dtmp[:, 4:8], ei[:, 0:NS], ALU.add, ['dtmp', bik], ['x1'])
                        cp('dve', xr[:, 0:NS], x1[:, 0, cc, :], ['x1'], [xk])
                        cp('dve', xi[:, 0:NS], x1[:, 1, cc, :], ['x1'], [xk])
                    if ti == len(TILES_ALL) - 1:
                        la = tn - 1
                        tt('dve', dtmp[:, 0:1], vr[:, la:la + 1], cs_[:, la:la + 1], ALU.mult, [('vv', par, 0), tck, 'dtmp'], ['dtmp'])
                        tt('dve', dtmp[:, 1:2], vi[:, la:la + 1], sn_[:, la:la + 1], ALU.mult, [('vv', par, 1), tsk, 'dtmp'], ['dtmp'])
                        tt('dve', pst[:, 0, cc:cc + 1], dtmp[:, 0:1], dtmp[:, 1:2], ALU.subtract, ['dtmp'], ['pst'])
                        tt('dve', dtmp[:, 2:3], vr[:, la:la + 1], sn_[:, la:la + 1], ALU.mult, [('vv', par, 0), tsk, 'dtmp'], ['dtmp'])
                        tt('dve', dtmp[:, 3:4], vi[:, la:la + 1], cs_[:, la:la + 1], ALU.mult, [('vv', par, 1), tck, 'dtmp'], ['dtmp'])
                        tt('dve', pst[:, 1, cc:cc + 1], dtmp[:, 2:3], dtmp[:, 3:4], ALU.add, ['dtmp'], ['pst'])
                    def cproj(fc=fc, ccl=ccl, ti=ti, t0=t0, tn=tn, xr=xr, xi=xi, xk=xk):
                        by, byk = P.bank()
                        mm(by[:, :tn], CD[:, fc, ccl, 0, :], xr[:, :tn], True, False, ['CD', xk], [byk], inc=False)
                        mm(by[:, :tn], CD[:, fc, ccl, 1, :], xi[:, :tn], False, True, ['CD', xk], [byk])
                        if ccl == 0:
                            cp('act', yacc[:, t0:t0 + tn], by[:, :tn], [byk], [('yacc', ti)])
                        else:
                            tt('dve', yacc[:, t0:t0 + tn], yacc[:, t0:t0 + tn], by[:, :tn], ALU.add, [byk, ('yacc', ti)], [('yacc', ti)])
                    pending.append(cproj)
            while pending:
                pending.pop(0)()
            for ti, (t0, tn) in enumerate(TILES_ALL):
                yk = ('yacc', ti)
                ysl = yacc[:, t0:t0 + tn]
                stt(ysl, ub[:, fc, t0:t0 + tn], dd[:, 0, fc:fc + 1], ysl, ALU.mult, ALU.add, ['ub', 'dd', yk], [yk])
                act(g1[:, :tn], ysl, AF.Square, [yk], ['g1'])
                ts('dve', g1[:, :tn], g1[:, :tn], 0.044715, 1.0, ALU.mult, ALU.add, ['g1'], ['g1'])
                tt('dve', g1[:, :tn], g1[:, :tn], ysl, ALU.mult, ['g1', yk], ['g1'])
                act(g2[:, :tn], g1[:, :tn], AF.Sigmoid, ['g1'], ['g2'], scale=2.0 * math.sqrt(2.0 / PI))
                tt('dve', ub[:, fc, t0:t0 + tn], ysl, g2[:, :tn], ALU.mult, [yk, 'g2'], ['ub'])
        dm('sp', o_ps5[l], pst, 'ops5', ['pst'], [])
        dm('sp', o_ss5[l], x1, 'oss5', ['x1'], [])
        blocks = [[(w_glu[l, j], 8)] for j in range(8)]

        def epi(j, ti, t0, tn, bk, bkey):
            s = ystg[j % 2]
            sk = ('ystg', j % 2, ti)
            act(g1[:, :tn], bk[:, :tn], AF.Sigmoid, [bkey, 'dd'], ['g1'], bias=dd[:, 1, j:j + 1])
            tt('dve', s[:, t0:t0 + tn], ub[:, j, t0:t0 + tn], g1[:, :tn], ALU.mult, ['ub', 'g1'], [sk])
            if ti == len(TILES_ALL) - 1:
                dm('sp', y_scr[j], s, f'yst{j % 2}', [('ystg', j % 2, t) for t in range(len(TILES_ALL))], [('y', j)])
        gemm(blocks, lambda kg, t0, tn: ub[:, kg, t0:t0 + tn], ['ub'], TILES_ALL, epi)

    def stage_gla(l):
        A.reset()
        aT = A.alloc([128, NT], BF16)
        w2 = A.alloc([128, 512], BF16)
        gp = A.alloc([128, 6], F32)
        nbg = A.alloc([128, 4], F32)
        rmask = A.alloc([128, NT], F32)
        qT = A.alloc([128, NT], BF16)
        kT = A.alloc([128, NT], BF16)
        vT = A.alloc([128, 2, NT], BF16)
        rT = A.alloc([128, 2, NT], BF16)
        G = A.alloc([128, NT], F32)
        B = A.alloc([128, NT], F32)
        E1 = A.alloc([128, NT], F32)
        E2 = A.alloc([128, NT], F32)
        qs = A.alloc([128, NT], BF16)
        ks = A.alloc([128, NT], BF16)
        ku = A.alloc([128, NT], BF16)
        cd = A.alloc([128, 17], F32)
        OT = A.alloc([128, 2, NT], F32)
        vtm = [A.alloc([128, 256], BF16) for _ in range(2)]
        kutm = [A.alloc([128, 128], BF16) for _ in range(2)]
        ATb = [A.alloc([128, 128], BF16) for _ in range(2)]
        Sf = A.alloc([128, 256], F32)
        Sb = A.alloc([128, 256], BF16)
        S0 = [A.alloc([128, 256], F32) for _ in range(2)]
        S1b = A.alloc([128, 256], BF16)
        krtm = A.alloc([128, 128], BF16)
        vrtm = A.alloc([128, 256], BF16)
        kmask = A.alloc([128, 128], BF16)
        gdec = A.alloc([128, NS], F32)
        sq = A.alloc([128, 512], BF16)
        rs = A.alloc([128, 512], F32)
        ystg = [A.alloc([128, NT], BF16) for _ in range(2)]
        scale = 128.0 ** -0.5

        dm('sp', aT[0:16, :], z_scr[24, 0:16, :], 'aT', [('z', 24)], ['aT'])
        dm('pool', w2[0:16, :], gla_w2[l], 'w2', [], ['w2'])
        dm('sp', gp, gla_p[l], 'gp', [], ['gp'])
        dm('sp', rmask, c_rmask, 'rmask', [], ['rmask'])
        ts('dve', nbg, gp[:, 0:4], -1.0, None, ALU.mult, None, ['gp'], ['nbg'])
        for h in range(4):
            dm('sp', qT, z_scr[8 + h], 'qT', [('z', 8 + h)], ['qT'])
            dm('sp', kT, z_scr[12 + h], 'kT', [('z', 12 + h)], ['kT'])
            dm('sp', vT, z_scr[16 + 2 * h:18 + 2 * h].rearrange("b p n -> p b n"), 'vT', [('z', 16 + 2 * h), ('z', 17 + 2 * h)], ['vT'])
            dm('sp', rT, z_scr[25 + 2 * h:27 + 2 * h].rearrange("b p n -> p b n"), 'rT', [('z', 25 + 2 * h), ('z', 26 + 2 * h)], ['rT'])
            for (t0, tn) in TILES_ALL:
                bk, bkey = P.bank()
                mm(bk[:, :tn], w2[0:16, h * 128:(h + 1) * 128], aT[0:16, t0:t0 + tn], True, True, ['w2', 'aT'], [bkey])
                act(G[:, t0:t0 + tn], bk[:, :tn], AF.Exp, [bkey, 'nbg'], ['G'], scale=-1.0, bias=nbg[:, h:h + 1])
            act(G, G, AF.Ln, ['G'], ['G'], bias=1.0)
            ts('dve', G, G, 1.0 / 16.0, None, ALU.mult, None, ['G'], ['G'])
            P.op('dve', lambda e: e.tensor_tensor_scan(out=B, data0=rmask, data1=G, initial=0.0, op0=ALU.mult, op1=ALU.add), reads=['rmask', 'G'], writes=['B'])
            act(E1, B, AF.Exp, ['B'], ['E1'], scale=-1.0)
            act(E2, B, AF.Exp, ['B'], ['E2'])
            stt(qs, qT, scale, E1, ALU.mult, ALU.mult, ['qT', 'E1'], ['qs'])
            tt('dve', ks, kT, E2, ALU.mult, ['kT', 'E2'], ['ks'])
            act(gdec, G[:, 0:NS], AF.Exp, ['G'], ['gdec'], scale=-1.0)
            B3 = B.rearrange("p (c n) -> p c n", n=128)
            E13 = E1.rearrange("p (c n) -> p c n", n=128)
            cp('dve', cd, E13[:, :, 127], ['E1'], ['cd'])
            tt('dve', E2.rearrange("p (c n) -> p c n", n=128), B3, B3[:, :, 127:128].broadcast_to([128, 17, 128]), ALU.subtract, ['B', 'ks'], ['E2'])
            act(E2, E2, AF.Exp, ['E2'], ['E2'])
            tt('dve', ku, kT, E2, ALU.mult, ['kT', 'E2'], ['ku'])
            memset('dve', ks[:, 0:COL0], 0.0, ['ks'])
            memset('dve', ku[:, 0:COL0], 0.0, ['ku'])
            memset('dve', Sf, 0.0, ['Sf'])
            memset('dve', Sb, 0.0, ['Sb'])
            def gla_A(c):
                cs_ = slice(c * 128, (c + 1) * 128)
                par = c % 2
                for vc in range(2):
                    bk, bkey = P.bank()
                    bb = bk.bitcast(BF16)
                    tr(bb[:, 0:128], vT[:, vc, cs_], ident_b, ['vT', 'cb'], [bkey])
                    cp('act' if vc == 0 else 'dve', vtm[par][:, vc * 128:(vc + 1) * 128], bb[:, 0:128], [bkey], [('vtm', par)])
                bk, bkey = P.bank()
                bb = bk.bitcast(BF16)
                tr(bb[:, 0:128], ku[:, cs_], ident_b, ['ku', 'cb'], [bkey])
                cp('act', kutm[par], bb[:, 0:128], [bkey], [('kutm', par)])
                ba, bak = P.bank()
                mm(ba[:, 0:128], ks[:, cs_], qs[:, cs_], True, True, ['ks', 'qs'], [bak])
                tt('dve', ATb[par], ba[:, 0:128], tri_b, ALU.mult, [bak, 'cb'], [('ATb', par)])

            def gla_B(c):
                cs_ = slice(c * 128, (c + 1) * 128)
                par = c % 2
                for vc in range(2):
                    bo, bok = P.bank()
                    mm(bo[:, 0:128], vtm[par][:, vc * 128:(vc + 1) * 128], ATb[par], True, False, [('vtm', par), ('ATb', par)], [bok], inc=False)
                    mm(bo[:, 0:128], Sb[:, vc * 128:(vc + 1) * 128], qs[:, cs_], False, True, ['Sb', 'qs'], [bok])
                    cp('act', OT[:, vc, cs_], bo[:, 0:128], [bok], ['OT'])
                bs, bsk = P.bank()
                mm(bs[:, 0:256], kutm[par], vtm[par], True, True, [('kutm', par), ('vtm', par)], [bsk])
                stt(Sf, Sf, cd[:, c:c + 1], bs[:, 0:256], ALU.mult, ALU.add, ['Sf', 'cd', bsk], ['Sf'])
                cp('dve', Sb, Sf, ['Sf'], ['Sb'])

            gla_A(0)
            for c in range(17):
                if c + 1 < 17:
                    gla_A(c + 1)
                gla_B(c)
            dm('sp', o_pgla[l, h], Sf, 'opgla', ['Sf'], [])
            bk, bkey = P.bank()
            bb = bk.bitcast(BF16)
            tr(bb[:, 0:128], kT[:, 0:128], ident_b, ['kT', 'cb'], [bkey])
            cp('act', krtm, bb[:, 0:128], [bkey], ['krtm'])
            for vc in range(2):
                bk, bkey = P.bank()
                bb = bk.bitcast(BF16)
                tr(bb[:, 0:128], vT[:, vc, 0:128], ident_b, ['vT', 'cb'], [bkey])
                cp('act', vrtm[:, vc * 128:(vc + 1) * 128], bb[:, 0:128], [bkey], ['vrtm'])
            for n in range(NS):
                s0 = S0[n % 2]
                s0k = ('S0', n % 2)
                dm('sp', s0, gla_s0[l, n, h], f's0{n % 2}', [], [s0k])
                ts('dve', kmask[0:4, :], krtm[0:4, :], I4[0:4, n:n + 1], None, ALU.mult, None, ['krtm', 'cm'], ['kmask'])
                bs, bsk = P.bank()
                mm(bs[:, 0:256], kmask[0:4, :], vrtm[0:4, :], True, True, ['kmask', 'vrtm'], [bsk])
                stt(s0, s0, gdec[:, n:n + 1], bs[:, 0:256], ALU.mult, ALU.add, [s0k, 'gdec', bsk], [s0k])
                cp('dve', S1b, s0, [s0k], ['S1b'])
                dm('sp', o_sgla[l, n, h], s0, f's1{n % 2}', [s0k], [])
                for vc in range(2):
                    bo, bok = P.bank()
                    mm(bo[:, 0:1], S1b[:, vc * 128:(vc + 1) * 128], qT[:, n:n + 1], True, True, ['S1b', 'qT'], [bok])
                    act(OT[:, vc, n:n + 1], bo[:, 0:1], AF.Copy, [bok], ['OT'], scale=scale)
            for ti, (t0, tn) in enumerate(TILES_ALL):
                bk, bkey = P.bank()
                for vc in range(2):
                    act(sq[:, :tn], OT[:, vc, t0:t0 + tn], AF.Square, ['OT'], ['gsq'])
                    mm(bk[:, :tn], ones_b, sq[:, :tn], vc == 0, vc == 1, ['gsq', 'cb1'], [bkey])
                ts('dve', rs[:, :tn], bk[:, :tn], 1.0 / 256.0, EPS, ALU.mult, ALU.add, [bkey], ['grs'])
                act(rs[:, :tn], rs[:, :tn], AF.Ln, ['grs'], ['grs'])
                act(rs[:, :tn], rs[:, :tn], AF.Exp, ['grs'], ['grs'], scale=-0.5)
                for vc in range(2):
                    j = 2 * h + vc
                    s = ystg[j % 2]
                    sk = ('ystg', j % 2, ti)
                    stt(OT[:, vc, t0:t0 + tn], OT[:, vc, t0:t0 + tn], gp[:, 4 + vc:5 + vc], rs[:, :tn], ALU.mult, ALU.mult, ['OT', 'gp', 'grs'], ['OT'])
                    tt('dve', s[:, t0:t0 + tn], OT[:, vc, t0:t0 + tn], rT[:, vc, t0:t0 + tn], ALU.mult, ['OT', 'rT'], [sk])
            for vc in range(2):
                j = 2 * h + vc
                dm('sp', y_scr[8 + j], ystg[j % 2], f'yst{j % 2}', [('ystg', j % 2, t) for t in range(len(TILES_ALL))], [('y', 8 + j)])

    def stage_swa(l, first):
        A.reset()
        qT = A.alloc([128, 8, NT], BF16)
        kT = A.alloc([128, 2, NT], BF16)
        vT = A.alloc([128, 2, NT], BF16)
        vtm = A.alloc([128, 17, 256], BF16)
        yc = A.alloc([128, 8, NT], BF16)
        biasT = A.alloc([128, 2, 16, 128], BF16)
        bdec = A.alloc([128, 16], BF16)
        bnew = A.alloc([128, 16], F32)
        rb33 = A.alloc([128, 16], F32)
        oh = A.alloc([128, 2 * 256 + 128 + 4], F32)
        Gt = A.alloc([128, 2, 16, 128], F32)
        Gb = A.alloc([128, 2, 16, 128], BF16)
        estg = A.alloc([128, 2, 256], F32)
        padc = A.alloc([128, 2], F32)
        sk_ = A.alloc([128, 16], F32)
        esk = A.alloc([128, 16, 128], F32)
        PT = [A.alloc([128, 512], BF16) for _ in range(6)]
        PTf = [A.alloc([128, 512], F32) for _ in range(3)]
        EB = A.alloc([128, 2, 16, 128], BF16)
        dens = [A.alloc([128, 512], F32) for _ in range(3)]
        kc_ = A.alloc([128, NS, 2, 128], BF16)
        vc_ = A.alloc([128, NS, 256], BF16)
        pd = A.alloc([128, 4], BF16)
        pn = A.alloc([128, 4], BF16)
        pn2 = A.alloc([128, 4], F32)
        d4 = A.alloc([128, 4], F32)
        ktm = A.alloc([128, 256], F32)
        vlast = A.alloc([128, 256], F32)
        k0tm = A.alloc([128, 256], F32)
        v0tm = A.alloc([128, 256], F32)

        dm('sp', qT, z_scr[33:41].rearrange("b p n -> p b n"), 'qT', [('z', j) for j in range(33, 41)], ['qT'])
        dm('sp', kT, z_scr[41:43].rearrange("b p n -> p b n"), 'kT', [('z', 41), ('z', 42)], ['kT'])
        dm('sp', vT, z_scr[43:45].rearrange("b p n -> p b n"), 'vT', [('z', 43), ('z', 44)], ['vT'])
        dm('sp', padc, c_pad, 'padc', [], ['padc'])
        memset('pool', yc[:, :, 0:COL0], 0.0, ['yc'])
        dm('sp', sk_[0:64, :], swa_sink[l], 'sk', [], ['sk'])
        dm('pool', kc_, swa_kT[l], 'kc', [], ['kc'])
        dm('pool', vc_, swa_vn[l].rearrange("n s f -> s n f"), 'vc', [], ['vc'])
        dm('sp', rb33[0:32, :], relb, 'rb', [], ['rb33'])
        memset('dve', rb33[32:33, :], NEG, ['rb33m'])
        dm('sp', oh[0:33, :], c_oh, 'oh', [], ['oh'])
        for rel in range(2):
            bk, bkey = P.bank()
            mm(bk[0:16, 0:256], rb33[0:33, :], oh[0:33, rel * 256:(rel + 1) * 256], True, True, ['rb33', 'rb33m', 'oh'], [bkey])
            act(estg[0:16, rel, :], bk[0:16, 0:256], AF.Copy, [bkey], ['estg'], scale=8.0)
        dm('sp', e_scr.rearrange("r h j -> h r j"), estg[0:16, :, :], 'est', ['estg'], ['e_scr'])
        for rel in range(2):
            src = bass.AP(tensor=e_scr.tensor, offset=rel * 16 * 256, ap=[[1, 128], [256, 16], [1, 128]])
            dm('sp', Gt[:, rel, :, :], src, f'Gt{rel}', ['e_scr'], [('Gt', rel)])
        cp('dve', Gb.rearrange("p a b c -> p (a b c)"), Gt.rearrange("p a b c -> p (a b c)"), [('Gt', 0), ('Gt', 1)], ['Gb'])
        Gb2 = Gb.rearrange("p a b c -> p (a b c)")
        bT2 = biasT.rearrange("p a b c -> p (a b c)")
        for i in range(8):
            bk, bkey = P.bank()
            mm(bk[:, :], J_b, Gb2[:, i * 512:(i + 1) * 512], True, True, ['Gb', 'cb'], [bkey])
            cp('act', bT2[:, i * 512:(i + 1) * 512], bk[:, :], [bkey], ['biasT'])
        act(EB.rearrange("p a b c -> p (a b c)"), bT2, AF.Exp, ['biasT'], ['EB'], scale=0.125)
        bk, bkey = P.bank()
        mm(bk[:, 0:16], oh[0:33, 512:640], rb33[0:33, :], True, True, ['rb33', 'rb33m', 'oh'], [bkey])
        act(bdec, bk[:, 0:16], AF.Copy, [bkey], ['bdec'], scale=8.0)
        bk, bkey = P.bank()
        mm(bk[0:4, 0:16], oh[0:33, 640:644], rb33[0:33, :], True, True, ['rb33', 'rb33m', 'oh'], [bkey])
        act(bnew[0:4, :], bk[0:4, 0:16], AF.Copy, [bkey], ['bnew'], scale=8.0)
        act(sk_[0:64, :], sk_[0:64, :], AF.Exp, ['sk'], ['sk'])
        cp('dve', esk[0:64, :, :], sk_[0:64, :].unsqueeze(2).broadcast_to([64, 16, 128]), ['sk'], ['esk'])
        for b in range(17):
            for c in range(2):
                bk, bkey = P.bank()
                bb = bk.bitcast(BF16)
                tr(bb[:, 0:128], vT[:, c, b * 128:(b + 1) * 128], ident_b, ['vT', 'cb'], [bkey])
                cp('act' if c == 0 else 'dve', vtm[:, b, c * 128:(c + 1) * 128], bb[:, 0:128], [bkey], ['vtm'])
        pti = [0]

        def swa_A(b, kvh):
            qs_ = slice(b * 128, (b + 1) * 128)
            kvp, po = kvh // 2, 64 * (kvh % 2)
            prt = slice(po, po + 64)
            kbs = [b] if b == 0 else [b - 1, b]
            plist = []
            for ki, kb in enumerate(kbs):
                rel = 1 if kb == b else 0
                bs, bsk = P.bank()
                mm(bs[:, :], kT[prt, kvp, kb * 128:(kb + 1) * 128], qT[prt, kvp * 4:kvp * 4 + 4, qs_], True, True, ['kT', 'qT'], [bsk])
                p_ = PT[pti[0] % 6]
                pk = ('PT', pti[0] % 6)
                pf = PTf[pti[0] % 3]
                pfk = ('PTf', pti[0] % 3)
                pti[0] += 1
                act(pf, bs[:, :], AF.Exp, [bsk, 'padc'], [pfk], scale=0.125, bias=padc[:, (1 if kb == 0 else 0):(2 if kb == 0 else 1)])
                tt('dve', p_.rearrange("p (g q) -> p g q", g=4), pf.rearrange("p (g q) -> p g q", g=4), EB[:, rel, kvh * 4:kvh * 4 + 4, :], ALU.mult, [pfk, 'EB'], [pk])
                plist.append((p_, pk, kb))
            return plist

        def swa_B(b, kvh, plist):
            kvp, po = kvh // 2, 64 * (kvh % 2)
            prt = slice(po, po + 64)
            bo, bok = P.bank()
            bd, bdk = P.bank()
            for ki, (p_, pk, kb) in enumerate(plist):
                mm(bo[0:64, :], vtm[:, kb, kvh * 64:(kvh + 1) * 64], p_, ki == 0, ki == len(plist) - 1, ['vtm', pk], [bok])
                mm(bd[0:64, :], ones_b[:, 0:64], p_, ki == 0, ki == len(plist) - 1, ['cb1', pk], [bdk])
            di = (b * 4 + kvh) % 3
            den, dnk = dens[di], ('den', di)
            tt('dve', den[0:64, :], bd[0:64, :], esk[0:64, kvh * 4:kvh * 4 + 4, :], ALU.add, [bdk, 'esk'], [dnk])
            act(den[0:64, :], den[0:64, :], AF.Ln, [dnk], [dnk])
            act(den[0:64, :], den[0:64, :], AF.Exp, [dnk], [dnk], scale=-1.0)
            lo = COL0 if b == 0 else 0
            tt('dve', yc[prt, kvp * 4:kvp * 4 + 4, b * 128 + lo:(b + 1) * 128], bo[0:64, :].rearrange("p (g q) -> p g q", g=4)[:, :, lo:],
               den[0:64, :].rearrange("p (g q) -> p g q", g=4)[:, :, lo:], ALU.mult, [bok, dnk], ['yc'])

        its = [(b, kvh) for b in range(17) for kvh in range(4)]
        nxt = swa_A(*its[0])
        for i, (b, kvh) in enumerate(its):
            cur = nxt
            if i + 1 < len(its):
                nxt = swa_A(*its[i + 1])
            swa_B(b, kvh, cur)
        for c in range(2):
            for (blk, dst, dk_) in ((16, ktm, 'ktm'), (0, k0tm, 'k0tm')):
                bk, bkey = P.bank()
                bb = bk.bitcast(BF16)
                tr(bb[:, 0:128], kT[:, c, blk * 128:(blk + 1) * 128], ident_b, ['kT', 'cb'], [bkey])
                cp('act', dst[:, c * 128:(c + 1) * 128], bb[:, 0:128], [bkey], [dk_])
        cp('dve', vlast, vtm[:, 16, :], ['vtm'], ['vlast'])
        cp('dve', v0tm, vtm[:, 0, :], ['vtm'], ['v0tm'])
        dm('sp', o_pk[l], ktm, 'opk', ['ktm'], [])
        dm('sp', o_pv[l], vlast, 'opv', ['vlast'], [])
        for n in range(NS):
            dm('sp', o_sk[l, n, 0:127, :], swa_kn[l, n, 1:128, :], 'osk', [], [])
            dm('sp', o_sv[l, n, 0:127, :], swa_vn[l, n, 1:128, :], 'osv', [], [])
        dm('sp', o_sk[l, :, 127, :], k0tm[0:NS, :], 'osk2', ['k0tm'], [])
        dm('sp', o_sv[l, :, 127, :], v0tm[0:NS, :], 'osv2', ['v0tm'], [])
        for n in range(NS):
            for kvh in range(4):
                kvp, po = kvh // 2, 64 * (kvh % 2)
                prt = slice(po, po + 64)
                qn = qT[prt, kvp * 4:kvp * 4 + 4, n]
                bs, bsk = P.bank()
                mm(bs[:, 0:4], kc_[prt, n, kvp, :], qn, True, False, ['kc', 'qT'], [bsk], inc=False)
                mm(bs[:, 0:4], ident_b, bdec[:, kvh * 4:kvh * 4 + 4], False, True, ['cb', 'bdec'], [bsk])
                act(pd, bs[:, 0:4], AF.Exp, [bsk], ['pd'], scale=0.125)
                bn, bnk = P.bank()
                mm(bn[0:4, 0:4], kT[prt, kvp, 0:NS], qn, True, True, ['kT', 'qT'], [bnk])
                tt('dve', pn2[0:4, :], bn[0:4, 0:4], bnew[0:4, kvh * 4:kvh * 4 + 4], ALU.add, [bnk, 'bnew'], ['pn2'])
                act(pn2[0:4, :], pn2[0:4, :], AF.Exp, ['pn2'], ['pn2'], scale=0.125)
                ts('dve', pn[0:4, :], pn2[0:4, :], I4[0:4, n:n + 1], None, ALU.mult, None, ['pn2', 'cm'], ['pn'])
                bo, bok = P.bank()
                mm(bo[0:64, 0:4], vc_[:, n, kvh * 64:(kvh + 1) * 64], pd, True, False, ['vc', 'pd'], [bok], inc=False)
                mm(bo[0:64, 0:4], vtm[0:4, 0, kvh * 64:(kvh + 1) * 64], pn[0:4, :], False, True, ['vtm', 'pn'], [bok])
                bd, bdk = P.bank()
                mm(bd[0:64, 0:4], ones_b[:, 0:64], pd, True, False, ['cb1', 'pd'], [bdk], inc=False)
                mm(bd[0:64, 0:4], ones_b[0:4, 0:64], pn[0:4, :], False, True, ['cb1', 'pn'], [bdk])
                tt('dve', d4[0:64, :], bd[0:64, 0:4], sk_[0:64, kvh * 4:kvh * 4 + 4], ALU.add, [bdk, 'sk'], ['d4'])
                P.op('dve', lambda e: e.reciprocal(out=d4[0:64, :], in_=d4[0:64, :]), reads=['d4'], writes=['d4'])
                tt('dve', yc[prt, kvp * 4:kvp * 4 + 4, n], bo[0:64, 0:4], d4[0:64, :], ALU.mult, [bok, 'd4'], ['yc'])
        for j in range(8):
            dm('sp', y_scr[16 + j], yc[:, j, :], 'ycst', ['yc'], [('y', 16 + j)])

    def stage_merge(l):
        A.reset()
        yq = A.alloc([128, 24, QW], BF16)
        gq = [A.alloc([128, 3, QW], BF16) for _ in range(2)]
        macc = A.alloc([128, QW], F32)
        mtmp = A.alloc([128, QW], F32)
        mq = A.alloc([128, 16, QW], BF16)
        mix = A.alloc([128, 16, QW], F32)
        xq = A.alloc([128, 16, QW], F32)
        hq = A.alloc([128, 16, QW], BF16)
        sq = A.alloc([128, 2, QW], BF16)
        rs = A.alloc([128, QW], F32)
        def load_yq(q):
            cq_ = slice(q * QW, q * QW + QW)
            dm('sp', yq, y_scr[:, :, cq_].rearrange("b p n -> p b n"), 'yq', [('y', j) for j in range(24)], ['yq'])
        load_yq(0)
        xck = lambda c: ('xqc', c)
        xall = ['xq'] + [('xqc', c) for c in range(16)]
        for q in range(NQ):
            c0 = q * QW
            cq = slice(c0, c0 + QW)
            dm('sp', xq, x_scr[:, :, cq], 'xq', [('x_scr', q)], ['xq'])
            blocks = []
            for j in range(16):
                for br in range(3):
                    blocks.append([(w_up[l, br, j], 8)])

            def epi(jj, ti, t0, tn, bk, bkey, q=q, cq=cq, c0=c0):
                j, br = jj // 3, jj % 3
                g = gq[j % 2]
                gk = ('gq', j % 2)
                if br == 0 and ti == 0:
                    src = bass.AP(tensor=z_scr.tensor, offset=(45 + j) * 128 * NT + c0, ap=[[NT, 128], [16 * 128 * NT, 3], [1, QW]])
                    dm('sp', g, src, f'gq{j % 2}', [('z', 45 + j), ('z', 61 + j), ('z', 77 + j)], [gk])
                sl = slice(t0, t0 + tn)
                if br == 0:
                    tt('dve', macc[:, sl], bk[:, :tn], g[:, 0, sl], ALU.mult, [bkey, gk], [('macc', ti)])
                else:
                    tt('dve', mtmp[:, sl], bk[:, :tn], g[:, br, sl], ALU.mult, [bkey, gk], [('mtmp', ti)])
                    if br == 1:
                        tt('dve', macc[:, sl], macc[:, sl], mtmp[:, sl], ALU.add, [('macc', ti), ('mtmp', ti)], [('macc', ti)])
                    else:
                        tt('dve', mq[:, j, sl], macc[:, sl], mtmp[:, sl], ALU.add, [('macc', ti), ('mtmp', ti)], ['mq'])
            gemm_blocks_rhs(blocks, [(lambda kg, t0, tn, br=jj % 3: yq[:, br * 8 + kg, t0:t0 + tn]) for jj in range(48)], ['yq'], TILES_Q, epi)
            if q + 1 < NQ:
                load_yq(q + 1)
            blocks = [[(w_out[l, j], 16)] for j in range(16)]

            def epi2(j, ti, t0, tn, bk, bkey):
                cp('act' if ti == 0 else 'dve', mix[:, j, t0:t0 + tn], bk[:, :tn], [bkey], ['mix'])
            gemm(blocks, lambda kg, t0, tn: mq[:, kg, t0:t0 + tn], ['mq'], TILES_Q, epi2)
            rstd_of(mix, 'mix', QW, sq, rs, 'n1')
            for c in range(16):
                stt(mix[:, c, :], mix[:, c, :], gn[:, l, 1, c:c + 1], rs, ALU.mult, ALU.mult, ['mix', 'n1rs', 'gn'], [('mixc', c)])
                tt('pool' if c % 2 == 0 else 'dve', xq[:, c, :], xq[:, c, :], mix[:, c, :], ALU.add, ['mix', ('mixc', c), 'xq'], [xck(c)])
            dm('sp', x_scr[:, :, cq], xq, 'xst', xall, [('x_scr', q)])
            rstd_of(xq, 'xq', QW, sq, rs, 'n2', ckey=xck)
            norm_to(hq, 'hq', xq, 'xq', QW, gn[:, l, 2, :], rs, 'n2', ckey=xck)
            dm('sp', h_scr[:, :, cq], hq, 'hst', ['hq'], [('h_scr', q)])

    def gemm_blocks_rhs(blocks, rhs_fns, rhs_keys, tiles, epi):
        flat = [(ap, kc) for blk in blocks for (ap, kc) in blk]
        LA = NWB - 1
        loaded = {}
        for i in range(min(LA, len(flat))):
            loaded[i] = wload(*flat[i])
        idx = 0
        for j, blk in enumerate(blocks):
            bks = [P.bank() for _ in tiles]
            (ap, kc) = blk[0]
            if idx + LA < len(flat):
                loaded[idx + LA] = wload(*flat[idx + LA])
            wb, wk = loaded.pop(idx)
            idx += 1
            rf = rhs_fns[j]
            for ti, (t0, tn) in enumerate(tiles):
                bk, bkey = bks[ti]
                for k in range(kc):
                    P.op('pe', lambda e, bk=bk, wb=wb, k=k, t0=t0, tn=tn, kc=kc, rf=rf:
                         e.matmul(bk[:, :tn], lhsT=wb[:, k, :], rhs=rf(k, t0, tn), start=(k == 0), stop=(k == kc - 1)),
                         reads=[wk] + rhs_keys, writes=[bkey], inc=(k == kc - 1))
            for ti, (t0, tn) in enumerate(tiles):
                epi(j, ti, t0, tn, bks[ti][0], bks[ti][1])

    def stage_ff1(l):
        A.reset()
        hT = A.alloc([128, 16, NT], BF16)
        stg = [A.alloc([128, NT], BF16) for _ in range(2)]
        rl = [A.alloc([128, 512], BF16) for _ in range(2)]
        dm('sp', hT, h_scr, 'hT', [('h_scr', q) for q in range(NQ)], ['hT'])
        blocks = [[(w_ff1[l, j], 16)] for j in range(64)]

        def epi(j, ti, t0, tn, bk, bkey):
            s = stg[j % 2]
            sk = ('stg', j % 2, ti)
            r = rl[ti % 2]
            act(r[:, :tn], bk[:, :tn], AF.Relu, [bkey], [('rl', ti % 2)])
            tt('pool', s[:, t0:t0 + tn], r[:, :tn], r[:, :tn], ALU.mult, [('rl', ti % 2)], [sk])
            if ti == len(TILES_ALL) - 1:
                dm('sp', hid_scr[j], s, f'hst{j % 2}', [('stg', j % 2, t) for t in range(len(TILES_ALL))], [('hid', j)])
        gemm(blocks, lambda kg, t0, tn: hT[:, kg, t0:t0 + tn], ['hT'], TILES_ALL, epi)

    def stage_ff2(l, last):
        A.reset()
        hid = A.alloc([128, 64, QW], BF16)
        ffo = A.alloc([128, 16, QW], F32)
        xq = A.alloc([128, 16, QW], F32)
        hq = A.alloc([128, 16, QW], BF16)
        sq = A.alloc([128, 2, QW], BF16)
        rs = A.alloc([128, QW], F32)
        def load_hid(q):
            cq_ = slice(q * QW, q * QW + QW)
            for part in range(4):
                dm('sp', hid[:, part * 16:(part + 1) * 16, :], hid_scr[part * 16:(part + 1) * 16, :, cq_].rearrange("b p n -> p b n"), f'hid{part}',
                   [('hid', j) for j in range(part * 16, part * 16 + 16)], [('hidq', part)])
        load_hid(0)
        xck = lambda c: ('xqc', c)
        xall = ['xq'] + [('xqc', c) for c in range(16)]
        for q in range(NQ):
            c0 = q * QW
            cq = slice(c0, c0 + QW)
            dm('sp', xq, x_scr[:, :, cq], 'xq', [('x_scr', q)], ['xq'])
            blocks = [[(w_ff2[l, j, kg], 16) for kg in range(4)] for j in range(16)]

            def epi(j, ti, t0, tn, bk, bkey):
                cp('act' if ti == 0 else 'dve', ffo[:, j, t0:t0 + tn], bk[:, :tn], [bkey], ['ffo'])
            gemm(blocks, lambda kg, t0, tn: hid[:, kg, t0:t0 + tn], [('hidq', p_) for p_ in range(4)], TILES_Q, epi)
            if q + 1 < NQ:
                load_hid(q + 1)
            rstd_of(ffo, 'ffo', QW, sq, rs, 'n3')
            for c in range(16):
                stt(ffo[:, c, :], ffo[:, c, :], gn[:, l, 3, c:c + 1], rs, ALU.mult, ALU.mult, ['ffo', 'n3rs', 'gn'], [('ffoc', c)])
                tt('pool' if c % 2 == 0 else 'dve', xq[:, c, :], xq[:, c, :], ffo[:, c, :], ALU.add, ['ffo', ('ffoc', c), 'xq'], [xck(c)])
            if last:
                dm('sp', yT[:, :, cq], xq, 'yout', xall, [])
            else:
                dm('sp', x_scr[:, :, cq], xq, 'xst', xall, [('x_scr', q)])
                rstd_of(xq, 'xq', QW, sq, rs, 'n4', ckey=xck)
                norm_to(hq, 'hq', xq, 'xq', QW, gn[:, l + 1, 0, :], rs, 'n4', ckey=xck)
                dm('sp', h_scr[:, :, cq], hq, 'hst', ['hq'], [('h_scr', q)])

    stage_prenorm0()
    for l in range(L):
        P.barrier(); stage_win(l)
        P.barrier(); stage_s5(l)
        P.barrier(); stage_gla(l)
        P.barrier(); stage_swa(l, l == 0)
        P.barrier(); stage_merge(l)
        P.barrier(); stage_ff1(l)
        P.barrier(); stage_ff2(l, l == L - 1)
    P.build()
    return nc, P


def _tile_w(W, kc):
    K, N = W.shape
    return np.ascontiguousarray(W.reshape(K // 128, 128, N // 128, 128).transpose(2, 1, 0, 3))


def _t5_bucket(d):
    if d < 16:
        return d
    v = 16 + int(np.float32(np.log(np.float32(d) / np.float32(16)) / np.float32(math.log(128 / 16)) * np.float32(16)))
    return min(v, 31)


def _consts():
    oh = np.zeros((33, 2 * 256 + 128 + 4), np.float32)
    for j in range(256):
        dist = j + 1
        if j <= 254 and dist < 128:
            oh[_t5_bucket(dist), j] = 1
        else:
            oh[32, j] = 1
        dist = j - 127
        if j <= 254 and dist >= 0:
            oh[_t5_bucket(dist), 256 + j] = 1
        else:
            oh[32, 256 + j] = 1
    for j in range(128):
        dist = 128 - j
        if dist < 128:
            oh[_t5_bucket(dist), 512 + j] = 1
        else:
            oh[32, 512 + j] = 1
    oh[0, 640:644] = 1
    misc = np.zeros((128, 128 * 3 + 12), np.float32)
    misc[:, 0:128] = np.eye(128)
    misc[:, 128:256] = np.eye(128)[::-1]
    misc[:, 256:384] = np.triu(np.ones((128, 128)))
    for gl in range(8):
        misc[gl * 16:(gl + 1) * 16, 384 + gl] = 1
    misc[0:4, 392:396] = np.eye(4)
    tpos = np.tile((np.arange(NT) - COL0).astype(np.float32)[None, :], (128, 1))
    rmask = np.ones((128, NT), np.float32)
    rmask[:, 0::128] = 0
    pad = np.zeros((128, 2), np.float32)
    pad[0:COL0, 1] = NEG
    return oh, misc, tpos, rmask, pad


def _prep_shared(inp, L):
    sh = {}
    w_in = inp['w_in'][:L]
    offs = np.cumsum([0, 1024, 512, 512, 1024, 16, 1024, 1024, 256, 256, 6144])
    cols = []
    cols.append(np.arange(offs[0], offs[1]))
    cols.append(np.arange(offs[1], offs[2]))
    cols.append(np.arange(offs[2], offs[3]))
    cols.append(np.arange(offs[3], offs[4]))
    a_cols = np.concatenate([np.arange(offs[4], offs[5]), -np.ones(112, np.int64)])
    cols.append(a_cols)
    cols.append(np.arange(offs[5], offs[6]))
    qc = []
    for kvp in range(2):
        for g in range(4):
            for half in range(2):
                qh = (2 * kvp + half) * 4 + g
                qc.append(offs[6] + qh * 64 + np.arange(64))
    qperm = np.concatenate(qc)
    cols.append(qperm)
    cols.append(np.arange(offs[7], offs[8]))
    cols.append(np.arange(offs[8], offs[9]))
    cols.append(np.arange(offs[9], offs[10]))
    allc = np.concatenate(cols)
    assert allc.shape[0] == NBLK_IN * 128
    wt = np.zeros((L, NBLK_IN, 128, 16, 128), np.float32)
    valid = allc >= 0
    for l in range(L):
        Wp = np.zeros((D, NBLK_IN * 128), np.float32)
        Wp[:, valid] = w_in[l][:, allc[valid]]
        wt[l] = _tile_w(Wp, 16)
    sh['w_in'] = wt
    sh['w_glu'] = np.stack([_tile_w(inp['s5_w_glu'][l], 8) for l in range(L)])
    qrow = qperm - offs[6]
    sh['w_up'] = np.stack([np.stack([_tile_w(inp['w_up_s5'][l], 8), _tile_w(inp['w_up_gla'][l], 8), _tile_w(inp['w_up_swa'][l][qrow, :], 8)]) for l in range(L)])
    sh['w_out'] = np.stack([_tile_w(inp['w_out'][l], 16) for l in range(L)])
    sh['w_ff1'] = np.stack([_tile_w(inp['w_ff1'][l], 16) for l in range(L)])
    ff2 = np.stack([_tile_w(inp['w_ff2'][l], 64) for l in range(L)])
    sh['w_ff2'] = np.ascontiguousarray(ff2.reshape(L, 16, 128, 4, 16, 128).transpose(0, 1, 3, 2, 4, 5))
    g = np.stack([inp[n][:L] for n in ('norm_pre_mix', 'norm_post_mix', 'norm_pre_ffn', 'norm_post_ffn')], axis=1)
    sh['gnorm'] = np.ascontiguousarray(g.reshape(L, 4, 16, 128).transpose(0, 3, 1, 2))
    def rep_gp(a):
        x = a.reshape(L, 8, 8, 64)
        x = np.broadcast_to(x[:, :, :, None, :], (L, 8, 8, 16, 64))
        return x.transpose(0, 2, 3, 1, 4).reshape(L, 128, 512)
    ls = np.broadcast_to(inp['s5_log_step'][:L, :, None], (L, 64, 64))
    def b_t(a):
        x = a.reshape(L, 8, 8, 64, 16)
        return x.transpose(0, 2, 4, 1, 3).reshape(L, 128, 512)
    sh['s5rep'] = np.ascontiguousarray(np.stack([rep_gp(inp['s5_lam_re'][:L]), rep_gp(inp['s5_lam_im'][:L]), rep_gp(ls), b_t(inp['s5_b_re'][:L]), b_t(inp['s5_b_im'][:L])], axis=2)).astype(np.float32)
    def cm_gp(a):
        return a.reshape(L, 32, 2, 64).transpose(0, 2, 3, 1).reshape(L, 128, 32)
    sh['s5cm'] = np.ascontiguousarray(np.stack([cm_gp(inp['s5_lam_re'][:L]), cm_gp(inp['s5_lam_im'][:L]), cm_gp(ls)], axis=2)).astype(np.float32)
    def c_t(a):
        return a.reshape(L, 32, 2, 16, 64).transpose(0, 2, 4, 1, 3).reshape(L, 128, 32, 16)
    sh['s5c'] = np.ascontiguousarray(np.stack([c_t(inp['s5_c_re'][:L]), c_t(inp['s5_c_im'][:L])], axis=2)).astype(np.float32)
    fp = lambda a: a.reshape(L, 8, 128).transpose(0, 2, 1)
    sh['s5d'] = np.ascontiguousarray(np.stack([fp(inp['s5_d'][:L]), fp(inp['s5_b_glu'][:L])], axis=2)).astype(np.float32)
    sh['gla_w2'] = np.ascontiguousarray(inp['gla_w_gate2'][:L])
    sh['gla_p'] = np.ascontiguousarray(np.concatenate([inp['gla_b_gate2'][:L].reshape(L, 4, 128).transpose(0, 2, 1), inp['gla_g_out'][:L].reshape(L, 2, 128).transpose(0, 2, 1)], axis=2)).astype(np.float32)
    sh['swa_sink'] = np.ascontiguousarray(np.broadcast_to(inp['swa_sinks'][:L, None, :], (L, 64, 16))).astype(np.float32)
    sh['relb'] = np.ascontiguousarray(inp['rel_bias'])
    oh, misc, tpos, rmask, pad = _consts()
    sh['c_oh'] = oh; sh['c_misc'] = misc; sh['c_tpos'] = tpos; sh['c_rmask'] = rmask; sh['c_pad'] = pad
    return sh


def _prep_core(inp, L, c):
    b = c % 4
    m = {}
    xt = np.zeros((NT, D), np.float32)
    xt[0:NS] = inp['x_sample'][NS * c:NS * c + NS, 0, :]
    xt[COL0:COL0 + NMETA] = inp['meta_tokens']
    xt[128:] = inp['x_prompt'][b]
    m['xT'] = np.ascontiguousarray(xt.T.reshape(16, 128, NT).transpose(1, 0, 2))
    ns = slice(NS * c, NS * c + NS)
    def x0cm(a):
        return a.reshape(L, NS, 32, 2, 64).transpose(0, 3, 4, 2, 1).reshape(L, 128, 32, NS)
    m['s5x0'] = np.ascontiguousarray(np.stack([x0cm(inp['state_s5_re'][:L, ns]), x0cm(inp['state_s5_im'][:L, ns])], axis=2)).astype(np.float32)
    m['gla_s0'] = np.ascontiguousarray(inp['state_gla'][:L, ns])
    ck = inp['cache_swa_k'][:L, ns]
    m['swa_kT'] = np.ascontiguousarray(ck.reshape(L, NS, 128, 2, 2, 64).transpose(0, 4, 5, 1, 3, 2).reshape(L, 128, NS, 2, 128))
    m['swa_kn'] = np.ascontiguousarray(ck.reshape(L, NS, 128, 256))
    m['swa_vn'] = np.ascontiguousarray(inp['cache_swa_v'][:L, ns].reshape(L, NS, 128, 256))
    return m


_CACHE = {}


def run(inputs, L=4, debug=False):
    inp = {k: np.asarray(v) for k, v in inputs.items()}
    key = (L, debug)
    if key not in _CACHE:
        _CACHE[key] = build_program(L, debug)
    nc, P = _CACHE[key]
    sh = _prep_shared(inp, L)
    in_maps = []
    for c in range(8):
        m = dict(sh)
        m.update(_prep_core(inp, L, c))
        in_maps.append(m)
    res = run_bass_kernel_spmd(nc, in_maps, core_ids=list(range(8)))
    return res.results


def assemble(R, L):
    B, NSAMP = 4, 32
    y_prompt = np.zeros((B, SEQ, D), np.float32)
    y_sample = np.zeros((NSAMP, 1, D), np.float32)
    p_s5r = np.zeros((L, B, 64, 64), np.float32); p_s5i = np.zeros_like(p_s5r)
    p_gla = np.zeros((L, B, 4, 128, 256), np.float32)
    p_k = np.zeros((L, B, 128, 4, 64), np.float32); p_v = np.zeros_like(p_k)
    s_s5r = np.zeros((L, NSAMP, 64, 64), np.float32); s_s5i = np.zeros_like(s_s5r)
    s_gla = np.zeros((L, NSAMP, 4, 128, 256), np.float32)
    s_k = np.zeros((L, NSAMP, 128, 4, 64), np.float32); s_v = np.zeros_like(s_k)
    for c in range(8):
        r = R[c]
        yt = r['yT'].transpose(1, 0, 2).reshape(D, NT).T
        ns = slice(NS * c, NS * c + NS)
        y_sample[ns, 0, :] = yt[0:NS]
        ss5 = r['o_ss5']
        v = ss5.reshape(L, 2, 64, 2, 32, NS).transpose(3, 0, 5, 4, 1, 2).reshape(2, L, NS, 64, 64)
        s_s5r[:, ns] = v[0]; s_s5i[:, ns] = v[1]
        s_gla[:, ns] = r['o_sgla']
        s_k[:, ns] = r['o_sk'].reshape(L, NS, 128, 4, 64)
        s_v[:, ns] = r['o_sv'].reshape(L, NS, 128, 4, 64)
        if c < 4:
            b = c
            y_prompt[b] = yt[128:]
            ps = r['o_ps5']
            v = ps.reshape(L, 2, 64, 2, 32).transpose(3, 0, 4, 1, 2).reshape(2, L, 64, 64)
            p_s5r[:, b] = v[0]; p_s5i[:, b] = v[1]
            p_gla[:, b] = r['o_pgla']
            p_k[:, b] = r['o_pk'].reshape(L, 128, 4, 64)
            p_v[:, b] = r['o_pv'].reshape(L, 128, 4, 64)
    return (y_prompt, y_sample, p_s5r, p_s5i, p_gla, p_k, p_v, s_s5r, s_s5i, s_gla, s_k, s_v)


def kernel(**inputs):
    R = run(inputs, L=4)
    return assemble(R, 4)
```
